# Optimizing a Trainium2 kernel written in Bass

```python
import jax, jax.numpy as jnp
from jax import lax
import numpy as np

D_MODEL = 1024
BATCH = 2
SEQ = 8192
DEPTH = 1

HEAD_DIM = 64
RWKV_HEADS = 16
RWKV_DIM = RWKV_HEADS * HEAD_DIM
DECAY_LORA = 64
AAA_LORA = 64
GATE_LORA = 160
GN_EPS = HEAD_DIM * 1e-5
RWKV_IN = 3 * RWKV_DIM + DECAY_LORA + AAA_LORA + GATE_LORA
DIL_GROUPS = ((128, 1), (512, 4), (2048, 16))
HEADS_PER_GROUP = 4
ATTN_HEADS = 3 * HEADS_PER_GROUP
ATTN_DIM = ATTN_HEADS * HEAD_DIM
ATTN_OUT_DIM = HEADS_PER_GROUP * HEAD_DIM
IN_SPLITS = (RWKV_DIM, RWKV_DIM, RWKV_DIM, DECAY_LORA, AAA_LORA, GATE_LORA,
             ATTN_DIM, ATTN_DIM, ATTN_DIM, D_MODEL, D_MODEL)
N_IN = RWKV_IN + 3 * ATTN_DIM + 2 * D_MODEL
MEM_LEN = 256
XATTN_HEADS = 4
XATTN_HEAD_DIM = D_MODEL // XATTN_HEADS
FFN_DIM = 4 * D_MODEL
NORM_EPS = 1e-6

kernel_name = 'hybrid_rwkv7_dilated_alibi_gated'


def rmsnorm(x, g):
    xf = x.astype(jnp.float32)
    y = xf * lax.rsqrt(jnp.mean(xf * xf, axis=-1, keepdims=True) + NORM_EPS) * g.astype(jnp.float32)
    return y.astype(x.dtype)


def alibi_slopes(n_heads):
    return jnp.exp2(-8.0 * (jnp.arange(n_heads, dtype=jnp.float32) + 1.0) / n_heads)


def rwkv7_scan(r, w, k, v, a_, b_):
    B, S, H, N = r.shape

    def step(state, inp):
        r_t, w_t, k_t, v_t, a_t, b_t = inp
        sa = jnp.einsum('bhij,bhj->bhi', state, a_t)
        state = state * w_t[:, :, None, :] + sa[..., None] * b_t[:, :, None, :] + v_t[..., None] * k_t[:, :, None, :]
        return state, jnp.einsum('bhij,bhj->bhi', state, r_t)

    xs = tuple(jnp.moveaxis(t, 1, 0) for t in (r, w, k, v, a_, b_))
    _, ys = lax.scan(step, jnp.zeros((B, H, N, N), jnp.float32), xs)
    return jnp.moveaxis(ys, 0, 1)


def rwkv7_mix(p_r, p_k, p_v, p_wd, p_ad, p_gd, w0, w2, a0, a2, g2, k_k, k_a, r_k, gn_w, gn_b):
    B, S, _ = p_r.shape
    heads = lambda t: t.reshape(B, S, RWKV_HEADS, HEAD_DIM)
    w_log = -jax.nn.softplus(-(w0 + jnp.tanh(p_wd) @ w2)) - 0.5
    decay = jnp.exp(-jnp.exp(w_log))
    a = jax.nn.sigmoid(a0 + p_ad @ a2)
    g = jax.nn.sigmoid(p_gd) @ g2
    kk = heads(p_k * k_k)
    kk = kk / jnp.maximum(jnp.sqrt(jnp.sum(kk * kk, axis=-1, keepdims=True)), 1e-12)
    k = p_k * (1.0 + (a - 1.0) * k_a)
    r, k, v, decay, a = heads(p_r), heads(k), heads(p_v), heads(decay), heads(a)
    y = rwkv7_scan(r, decay, k, v, -kk, kk * a)
    mean = jnp.mean(y, axis=-1, keepdims=True)
    var = jnp.mean(jnp.square(y - mean), axis=-1, keepdims=True)
    yn = ((y - mean) * lax.rsqrt(var + GN_EPS)).reshape(B, S, RWKV_DIM) * gn_w + gn_b
    bonus = (jnp.sum(r * k * r_k, axis=-1, keepdims=True) * v).reshape(B, S, RWKV_DIM)
    return (yn + bonus) * g


def dilated_attention(q, k, v, window, dilation, slopes):
    B, S, H, Dh = q.shape
    blk = window // dilation
    span = blk * dilation
    Sp = -(-S // span) * span
    nb = Sp // span
    padw = ((0, 0), (0, Sp - S), (0, 0), (0, 0))

    def blocks(t):
        return jnp.pad(t, padw).astype(jnp.float32).reshape(B, nb, blk, dilation, H, Dh)

    def with_prev(t):
        prev = jnp.pad(t, ((0, 0), (1, 0), (0, 0), (0, 0), (0, 0), (0, 0)))[:, :-1]
        return jnp.concatenate([prev, t], axis=2)

    qb = blocks(q) * (Dh ** -0.5)
    kc, vc = with_prev(blocks(k)), with_prev(blocks(v))
    s = jnp.einsum('bnqrhc,bnkrhc->bnrhqk', qb, kc)
    qi = jnp.arange(blk)[:, None]
    ki = jnp.arange(2 * blk)[None, :]
    steps = qi + blk - ki
    in_band = (steps >= 0) & (steps <= blk)
    exists = (jnp.arange(nb)[:, None, None] > 0) | (ki >= blk)[None]
    valid = in_band[None] & exists
    bias = -slopes[:, None, None] * (steps * dilation).astype(jnp.float32)[None]
    s = jnp.where(valid[None, :, None, None], s + bias[None, None, None], -jnp.inf)
    lse = jax.nn.logsumexp(s, axis=-1)
    p = jnp.exp(s - lse[..., None])
    o = jnp.einsum('bnrhqk,bnkrhc->bnqrhc', p, vc).reshape(B, Sp, H, Dh)[:, :S]
    lse = jnp.transpose(lse, (0, 1, 4, 2, 3)).reshape(B, Sp, H)[:, :S]
    return o, lse


def dilated_mixture(q, k, v):
    B, S = q.shape[:2]
    slopes = alibi_slopes(ATTN_HEADS)
    outs, lses = [], []
    for gi, (window, dilation) in enumerate(DIL_GROUPS):
        hs = slice(gi * HEADS_PER_GROUP, (gi + 1) * HEADS_PER_GROUP)
        o, l = dilated_attention(q[:, :, hs], k[:, :, hs], v[:, :, hs], window, dilation, slopes[hs])
        outs.append(o)
        lses.append(l)
    wts = jax.nn.softmax(jnp.stack(lses, axis=0), axis=0)
    y = jnp.sum(wts[..., None] * jnp.stack(outs, axis=0), axis=0)
    return y.reshape(B, S, ATTN_OUT_DIM)


def memory_cross_attention(xn, memn, wq, w_kv, wo):
    B, S, _ = xn.shape
    q = (xn @ wq).astype(jnp.float32).reshape(B, S, XATTN_HEADS, XATTN_HEAD_DIM)
    kv = (memn @ w_kv).astype(jnp.float32).reshape(B, MEM_LEN, 2, XATTN_HEADS, XATTN_HEAD_DIM)
    s = jnp.einsum('bshc,bmhc->bhsm', q, kv[:, :, 0]) * (XATTN_HEAD_DIM ** -0.5)
    p = jax.nn.softmax(s, axis=-1)
    o = jnp.einsum('bhsm,bmhc->bshc', p, kv[:, :, 1]).reshape(B, S, D_MODEL)
    return (o @ wo.astype(jnp.float32)).astype(xn.dtype)


def setup_inputs(seed: int = 0) -> dict:
    key = jax.random.key(seed)
    ks = jax.random.split(key, 32)
    f32 = jnp.float32
    nrm = lambda k, shape, scale: jax.random.normal(k, shape, f32) * scale
    L = DEPTH
    return {
        'x': jax.random.normal(ks[0], (BATCH, SEQ, D_MODEL), f32),
        'mem': jax.random.normal(ks[1], (BATCH, MEM_LEN, D_MODEL), f32),
        'norm_mix_g': 1.0 + nrm(ks[2], (L, D_MODEL), 0.02),
        'w_in': nrm(ks[3], (L, D_MODEL, N_IN), D_MODEL ** -0.5),
        'shift_mu': jax.random.uniform(ks[4], (L, RWKV_IN), f32),
        'w0': jax.random.uniform(ks[5], (L, RWKV_DIM), f32, -5.0, 1.0),
        'w2': nrm(ks[6], (L, DECAY_LORA, RWKV_DIM), 0.1),
        'a0': nrm(ks[7], (L, RWKV_DIM), 0.1),
        'a2': nrm(ks[8], (L, AAA_LORA, RWKV_DIM), 0.1),
        'g2': nrm(ks[9], (L, GATE_LORA, RWKV_DIM), GATE_LORA ** -0.5),
        'k_k': 0.85 + nrm(ks[10], (L, RWKV_DIM), 0.05),
        'k_a': 1.0 + nrm(ks[11], (L, RWKV_DIM), 0.05),
        'r_k': nrm(ks[12], (L, RWKV_HEADS, HEAD_DIM), 0.1),
        'gn_w': 1.0 + nrm(ks[13], (L, RWKV_DIM), 0.02),
        'gn_b': nrm(ks[14], (L, RWKV_DIM), 0.02),
        'p_rwkv': nrm(ks[15], (L, RWKV_DIM, D_MODEL), RWKV_DIM ** -0.5),
        'p_attn': nrm(ks[16], (L, ATTN_OUT_DIM, D_MODEL), ATTN_OUT_DIM ** -0.5),
        'w_out': nrm(ks[17], (L, D_MODEL, D_MODEL), D_MODEL ** -0.5),
        'norm_x_g': 1.0 + nrm(ks[18], (L, D_MODEL), 0.02),
        'norm_mem_g': 1.0 + nrm(ks[19], (L, D_MODEL), 0.02),
        'xa_wq': nrm(ks[20], (L, D_MODEL, D_MODEL), D_MODEL ** -0.5),
        'xa_wkv': nrm(ks[21], (L, D_MODEL, 2 * D_MODEL), D_MODEL ** -0.5),
        'xa_wo': nrm(ks[22], (L, D_MODEL, D_MODEL), D_MODEL ** -0.5),
        'norm_ffn_g': 1.0 + nrm(ks[23], (L, D_MODEL), 0.02),
        'ffn_w1': nrm(ks[24], (L, D_MODEL, FFN_DIM), D_MODEL ** -0.5),
        'ffn_w2': nrm(ks[25], (L, FFN_DIM, D_MODEL), FFN_DIM ** -0.5),
        'norm_final_g': 1.0 + nrm(ks[26], (D_MODEL,), 0.02),
    }


def reference(x, mem, norm_mix_g, w_in, shift_mu, w0, w2, a0, a2, g2, k_k, k_a, r_k, gn_w, gn_b,
              p_rwkv, p_attn, w_out, norm_x_g, norm_mem_g, xa_wq, xa_wkv, xa_wo,
              norm_ffn_g, ffn_w1, ffn_w2, norm_final_g):
    f32 = jnp.float32
    B, S, _ = x.shape
    bounds = list(np.cumsum(IN_SPLITS)[:-1])
    h = x
    for l in range(DEPTH):
        n = rmsnorm(h, norm_mix_g[l])
        proj = (n @ w_in[l]).astype(f32)
        rw = proj[..., :RWKV_IN]
        rw_prev = jnp.pad(rw, ((0, 0), (1, 0), (0, 0)))[:, :-1]
        proj = jnp.concatenate([rw + shift_mu[l].astype(f32) * (rw_prev - rw), proj[..., RWKV_IN:]], axis=-1)
        (p_r, p_k, p_v, p_wd, p_ad, p_gd, q, k, v, gate_r, gate_a) = jnp.split(proj, bounds, axis=-1)
        y_rwkv = rwkv7_mix(p_r, p_k, p_v, p_wd, p_ad, p_gd,
                           w0[l].astype(f32), w2[l].astype(f32), a0[l].astype(f32), a2[l].astype(f32),
                           g2[l].astype(f32), k_k[l].astype(f32), k_a[l].astype(f32), r_k[l].astype(f32),
                           gn_w[l].astype(f32), gn_b[l].astype(f32))
        hd = lambda t: t.reshape(B, S, ATTN_HEADS, HEAD_DIM)
        y_attn = dilated_mixture(hd(q), hd(k), hd(v))
        merged = (jax.nn.sigmoid(gate_r) * (y_rwkv @ p_rwkv[l].astype(f32))
                  + jax.nn.sigmoid(gate_a) * (y_attn @ p_attn[l].astype(f32)))
        h = h + (merged @ w_out[l].astype(f32)).astype(h.dtype)
        h = h + memory_cross_attention(rmsnorm(h, norm_x_g[l]), rmsnorm(mem, norm_mem_g[l]),
                                       xa_wq[l], xa_wkv[l], xa_wo[l])
        u = rmsnorm(h, norm_ffn_g[l]) @ ffn_w1[l]
        h = h + (jnp.square(jax.nn.relu(u)) @ ffn_w2[l]).astype(h.dtype)
    return rmsnorm(h, norm_final_g)
```

```python
import numpy as np
from contextlib import ExitStack
import concourse.bass as bass
import concourse.mybir as mybir
from concourse.bass_utils import run_bass_kernel_spmd

F32 = mybir.dt.float32
BF16 = mybir.dt.bfloat16
AF = mybir.ActivationFunctionType
ALU = mybir.AluOpType

D = 1024
SEQ = 8192
TT = 256
CH = 64
NCH = TT // CH
CDEC = 0.6065306597126334
GN_EPS = 64 * 1e-5
NORM_EPS = 1e-6


import os
STOP = 99
DBG = False


class _Stop(Exception):
    pass


def stop(n):
    if STOP == n:
        raise _Stop()


class Res:
    __slots__ = ("w", "r")

    def __init__(self):
        self.w = None
        self.r = {}


class Tl:
    def __init__(self, h, n=1):
        self.h = h
        self.rs = [Res() for _ in range(n)]

    def __getitem__(self, k):
        return self.h[k]


def _res(xs):
    out = []
    for x in xs:
        if isinstance(x, Tl):
            out.extend(x.rs)
        elif isinstance(x, tuple):
            out.append(x[0].rs[x[1]])
        else:
            out.append(x)
    return out


class Sched:
    def __init__(self, nc, es):
        self.nc = nc
        self.es = es
        self.engs = {"pe": nc.tensor, "dve": nc.vector, "act": nc.scalar, "pool": nc.gpsimd, "sp": nc.sync}
        self.sem = {}
        self.cnt = {}
        self.seen = {k: {} for k in self.engs}
        self.pre = ""
        self.rec = None
        self.front = 0.75
        for k in self.engs:
            self.sem[k] = es.enter_context(nc.semaphore("s_" + k))
            self.cnt[k] = 0

    def need(self, eng, src, val):
        if src == eng and src == "pe":
            return
        if self.seen[eng].get(src, 0) >= val:
            return
        self.engs[eng].wait_ge(self.sem[src], val)
        self.seen[eng][src] = val

    def _deps(self, eng, R, W):
        deps = {}
        for res in R:
            if res.w is not None:
                s, v = res.w
                deps[s] = max(deps.get(s, 0), v)
        for res in W:
            if res.w is not None:
                s, v = res.w
                deps[s] = max(deps.get(s, 0), v)
            for s, v in res.r.items():
                deps[s] = max(deps.get(s, 0), v)
        for s, v in deps.items():
            self.need(eng, s, v)

    def op(self, eng, fns, R=(), W=()):
        R = _res(R)
        W = _res(W)
        if not isinstance(fns, (list, tuple)):
            fns = [fns]
        if self.rec is not None:
            self.rec.append(("op", eng, fns, R, W))
            return
        self._deps(eng, R, W)
        e = self.engs[eng]
        ins = None
        for f in fns:
            ins = f(e)
        self.cnt[eng] += 1
        idx = self.cnt[eng]
        ins.then_inc(self.sem[eng], 1)
        for res in R:
            res.r[eng] = idx
        for res in W:
            res.w = (eng, idx)
            res.r = {}

    def dma(self, q, out, in_, R=(), W=(), slot=None):
        R = _res(R)
        W = _res(W)
        if self.rec is not None:
            self.rec.append(("dma", q, out, in_, R, W, slot))
            return
        slot = self.pre + slot
        if slot not in self.sem:
            self.sem[slot] = self.es.enter_context(self.nc.semaphore("d_" + slot))
            self.cnt[slot] = 0
        self._deps(q, R, W)
        self.engs[q].dma_start(out=out, in_=in_).then_inc(self.sem[slot], 16)
        self.cnt[slot] += 16
        v = self.cnt[slot]
        for res in R:
            res.r[slot] = v
        for res in W:
            res.w = (slot, v)
            res.r = {}

    def record(self, fn, *args):
        assert self.rec is None
        self.rec = []
        fn(*args)
        r, self.rec = self.rec, None
        return r

    def _emit(self, it):
        if it[0] == "op":
            self.op(it[1], it[2], it[3], it[4])
        else:
            self.dma(it[1], it[2], it[3], it[4], it[5], it[6])

    def replay(self, streams):
        streams = [st for st in streams if st]
        if not streams:
            return
        main, others = streams[0], streams[1:]
        quanta, cur, seen_nonpe = [], [], False
        for it in main:
            is_pe = it[0] == "op" and it[1] == "pe"
            if is_pe and seen_nonpe:
                quanta.append(cur)
                cur, seen_nonpe = [], False
            cur.append(it)
            if not is_pe:
                seen_nonpe = True
        if cur:
            quanta.append(cur)
        nq = len(quanta)
        pos = [0] * len(others)
        for qi, q in enumerate(quanta):
            for it in q:
                self._emit(it)
            for oi, st in enumerate(others):
                tgt = int(len(st) * (qi + 1) / (nq * self.front) + 0.999)
                while pos[oi] < min(tgt, len(st)):
                    self._emit(st[pos[oi]])
                    pos[oi] += 1
        for oi, st in enumerate(others):
            while pos[oi] < len(st):
                self._emit(st[pos[oi]])
                pos[oi] += 1

    def barrier(self):
        for eng in self.engs:
            for s, v in self.cnt.items():
                if s != eng and v > 0:
                    self.need(eng, s, v)


def _mk_dr(nc, ctx, pre):
    shared = ctx["shared"] if ctx else None

    def dr(n, s, kind="ExternalInput"):
        if shared is not None and kind == "ExternalInput" and n in ("xT", "gmix", "c_ident"):
            if n not in shared:
                shared[n] = nc.dram_tensor(n, list(s), F32, kind=kind).ap()
            return shared[n]
        return nc.dram_tensor(pre + n, list(s), F32, kind=kind).ap()
    return dr


class K:
    def __init__(self, nc, es):
        self.nc = nc
        self.es = es
        self.S = Sched(nc, es)
        self.n = 0
        self.epsb = {}

    def sb(self, shape, dt, n=1, name=None):
        self.n += 1
        return Tl(self.es.enter_context(self.nc.sbuf_tensor(name or f"t{self.n}", list(shape), dt)), n)

    def ps(self, shape, dt, n=1, name=None):
        self.n += 1
        return Tl(self.es.enter_context(self.nc.psum_tensor(name or f"p{self.n}", list(shape), dt)), n)

    def tt(self, eng, out, a, b, op, R, W):
        self.S.op(eng, lambda e: e.tensor_tensor(out=out, in0=a, in1=b, op=op), R, W)

    def ts(self, eng, out, a, s1, s2, op0, op1, R, W):
        if op1 is None:
            self.S.op(eng, lambda e: e.tensor_scalar(out=out, in0=a, scalar1=s1, scalar2=None, op0=op0), R, W)
        else:
            self.S.op(eng, lambda e: e.tensor_scalar(out=out, in0=a, scalar1=s1, scalar2=s2, op0=op0, op1=op1), R, W)

    def stt(self, out, a, s, b, op0, op1, R, W):
        self.S.op("dve", lambda e: e.scalar_tensor_tensor(out=out, in0=a, scalar=s, in1=b, op0=op0, op1=op1), R, W)

    def act(self, out, in_, func, R, W, bias=0.0, scale=1.0):
        self.S.op("act", lambda e: e.activation(out=out, in_=in_, func=func, bias=bias, scale=scale), R, W)

    def cp(self, eng, out, in_, R, W):
        if eng == "act":
            self.S.op("act", lambda e: e.activation(out=out, in_=in_, func=AF.Copy), R, W)
        else:
            self.S.op(eng, lambda e: e.tensor_copy(out=out, in_=in_), R, W)

    def mm(self, items, R, W):
        fns = []
        for (o, l, r, st, sp) in items:
            fns.append(lambda e, o=o, l=l, r=r, st=st, sp=sp: e.matmul(o, lhsT=l, rhs=r, start=st, stop=sp))
        self.S.op("pe", fns, R, W)

    def rsqrt(self, out, in_, eps, R, W):
        if eps not in self.epsb:
            t = self.sb((128, 1), F32)
            self.memset("pool", t[:], float(eps), [t])
            self.epsb[eps] = t
        eb = self.epsb[eps]
        np_ = out.shape[0]
        self.S.op("act", lambda e: e.activation(out=out, in_=in_, func=AF.Ln, bias=eb[0:np_, :], scale=1.0), list(R) + [eb], W)
        self.S.op("act", lambda e: e.activation(out=out, in_=out, func=AF.Exp, scale=-0.5), W, W)

    def memset(self, eng, ap, val, W):
        self.S.op(eng, lambda e: e.memset(ap, val), (), W)


def consts_p1():
    c = {}
    c["ident"] = np.eye(128, dtype=np.float32)
    s = np.arange(64)[:, None]
    t = np.arange(64)[None, :]
    U = (t > s).astype(np.float32)
    Ui = (t >= s).astype(np.float32)
    L = (s > t).astype(np.float32)
    c["mask6"] = np.concatenate([L, L, U, Ui, U, Ui], axis=1).astype(np.float32)
    bd = np.zeros((128, 128), np.float32)
    bd[:64, :64] = 1
    bd[64:, 64:] = 1
    c["bd"] = bd
    sh = np.zeros((128, 64), np.float32)
    sh[64 + np.arange(64), np.arange(64)] = 1
    c["shdn"] = sh
    ilow = np.zeros((64, 128), np.float32)
    ilow[np.arange(64), np.arange(64)] = 1
    iup = np.zeros((64, 128), np.float32)
    iup[np.arange(64), 64 + np.arange(64)] = 1
    c["ilu"] = np.concatenate([ilow, iup], axis=1)
    id2 = np.zeros((128, 64), np.float32)
    id2[np.arange(128), np.arange(128) % 64] = 1
    c["id2"] = id2
    rm = np.ones((128, TT), np.float32)
    rm[:, ::CH] = 0
    c["rmask"] = rm
    return c


CONST_SHAPES_P1 = {"ident": (128, 128), "mask6": (64, 384), "bd": (128, 128), "shdn": (128, 64),
                   "ilu": (64, 256), "id2": (128, 64), "rmask": (128, TT)}

PV_MU = 0
PV_W0 = 10
PV_A0 = 12
PV_KK = 14
PV_KA = 16
PV_RK = 18
PV_GW = 20
PV_GB = 22
NPV = 24

RW_CH = [(0, 128), (128, 128), (256, 128), (384, 128), (512, 128), (640, 128),
         (768, 64), (832, 64), (896, 128), (1024, 32)]
NRW = 1056


def build_p1a(seq, ctx=None):
    nt = seq // TT
    fused = ctx is not None
    nc = ctx["nc"] if fused else bass.Bass("TRN2", target_bir_lowering=False)
    dr = _mk_dr(nc, ctx, "a_" if fused else "")
    xT = dr("xT", (D, seq))
    w1 = dr("w1", (D, NRW))
    gmix = dr("gmix", (128, 8))
    pvec = dr("pvec", (128, NPV))
    w2s = dr("w2s", (64, 256))
    a2s = dr("a2s", (64, 256))
    g2s = dr("g2s", (160, 256))
    cd = {k: dr("c_" + k, v) for k, v in CONST_SHAPES_P1.items()}
    yr = None if fused else dr("yr", (256, seq), kind="ExternalOutput")
    prw = dr("prw", (256, D)) if fused else None
    dbgf = dr("dbgf", (128, 40 * TT), kind="ExternalOutput") if DBG else None
    dbgb = nc.dram_tensor("dbgb", [128, 40 * TT], BF16, kind="ExternalOutput").ap() if DBG else None
    dbc = {"f": 0, "b": 0}

    def dump(S, tile, ap, kind, name, rows=128, cols=TT):
        i = dbc[kind]
        dbc[kind] += (cols + TT - 1) // TT
        dst = (dbgf if kind == "f" else dbgb)[0:rows, i * TT:i * TT + cols]
        S.dma("sp", dst, ap, R=[tile], slot=f"dbg{kind}{i}")

    with ExitStack() as es:
        if fused:
            k = ctx["k"]
            k.es = es
            k.epsb = {}
        else:
            k = K(nc, es)
        S = k.S
        if fused:
            PRWb = k.sb((128, 2, D), BF16)
            YOb = [k.sb((128, TT), BF16) for _ in range(2)]
            PRD = k.sb((128, 8, TT), BF16)
        def load_const(name, shape, to_bf16=True, q="sp"):
            f = k.sb(shape, F32)
            S.dma(q, f[:], cd[name], W=[f], slot="c_" + name)
            if not to_bf16:
                return f
            b = k.sb(shape, BF16)
            k.cp("pool", b[:], f[:], [f], [b])
            return b
        ident = load_const("ident", (128, 128))
        mask6 = load_const("mask6", (64, 384), to_bf16=False)
        bd = load_const("bd", (128, 128))
        shdn = load_const("shdn", (128, 64))
        ilu = load_const("ilu", (64, 256))
        id2f = load_const("id2", (128, 64), to_bf16=False)
        id2 = k.sb((128, 64), BF16)
        k.cp("pool", id2[:], id2f[:], [id2f], [id2])
        rmask = load_const("rmask", (128, TT), to_bf16=False)
        bd64 = k.sb((128, 128), BF16)
        k.ts("pool", bd64[:], bd[:], 1.0 / 64, None, ALU.mult, None, [bd], [bd64])
        onesm = k.sb((128, 128), BF16)
        k.memset("pool", onesm[:], 1.0 / 1024, [onesm])
        pv = k.sb((128, NPV), F32)
        S.dma("sp", pv[:], pvec, W=[pv], slot="pv")
        gm = k.sb((128, 8), F32)
        S.dma("sp", gm[:], gmix, W=[gm], slot="gm")
        def load_bf(ap, shape, q="sp", name="l"):
            f = k.sb(shape, F32)
            S.dma(q, f[:], ap, W=[f], slot=name)
            b = k.sb(shape, BF16)
            k.cp("pool", b[:], f[:], [f], [b])
            return b
        w2b = load_bf(w2s, (64, 256), name="w2")
        a2b = load_bf(a2s, (64, 256), name="a2")
        g2b0 = load_bf(g2s[0:128, :], (128, 256), name="g20")
        g2b1 = load_bf(g2s[128:160, :], (32, 256), name="g21")
        WB = k.sb((128, 8, NRW), BF16, n=8)
        wst = [k.sb((128, NRW), F32)]
        for kc in range(8):
            st = wst[0]
            S.dma("sp" if kc % 2 == 0 else "act", st[:], w1[kc * 128:(kc + 1) * 128, :], W=[st], slot="w0")
            k.ts("dve", WB[:, kc, :], st[:], gm[:, kc:kc + 1], None, ALU.mult, None, [st, gm], [(WB, kc)])
        if fused:
            for h_ in range(2):
                S.dma("sp", wst[0][:, 0:D], prw[h_ * 128:(h_ + 1) * 128, :], W=[wst[0]], slot="w0")
                k.cp("pool", PRWb[:, h_, :], wst[0][:, 0:D], [wst[0]], [PRWb])

        XT32 = [k.sb((128, 8, TT), F32) for _ in range(2)]
        XB = [k.sb((128, 8, TT), BF16) for _ in range(2)]
        XSQs = [k.sb((128, 8, TT), BF16) for _ in range(2)]
        RSTD = k.sb((128, TT), F32)
        PROJ = [k.sb((128, 10, TT + 1), F32, n=11) for _ in range(2)]
        for p in PROJ:
            k.memset("pool", p[:], 0.0, [p])
        PP = k.sb((128, 10, TT), F32, n=2)
        X6 = k.sb((64, 2, TT), BF16)
        SG = k.sb((128, TT), BF16)
        SG8 = k.sb((32, TT), BF16)
        f32t = lambda: k.sb((128, TT), F32)
        bft = lambda: k.sb((128, TT), BF16)
        base = dict()
        for nm in ["LD", "AS", "KK", "RN", "KN", "T1", "KM", "B", "CS", "WI", "WV", "E1", "WE", "E2", "WH"]:
            base[nm] = f32t()
        for nm in ["KK2", "RKB", "YB", "YSQ"]:
            base[nm] = bft()
        for nm in ["MS", "NEG", "VAR", "RS", "YC", "YN", "YG", "YO"]:
            base[nm] = f32t()
        tmp = []
        for i in range(4):
            d = dict(base)
            if i < 2:
                d["FM"] = k.sb((128, 5, TT), BF16, n=5)
                d["TM"] = k.sb((128, 3, TT), BF16, n=3)
                d["FMo"] = k.sb((64, 5, TT), BF16)
            else:
                d["FM"], d["TM"], d["FMo"] = tmp[i - 2]["FM"], tmp[i - 2]["TM"], tmp[i - 2]["FMo"]
            d["GT"] = f32t()
            d["BON"] = f32t()
            tmp.append(d)
        hd = [dict() for _ in range(4)]
        for h in range(4):
            d = hd[h]
            d["S1"] = k.sb((64, NCH, 576), BF16, n=3)
            d["VT"] = k.sb((64, NCH, 64), BF16)
            d["XT"] = [k.sb((64, NCH, 192), BF16) for _ in range(2)]
            d["Q3"] = k.sb((64, NCH, 128), BF16)
            d["RM"] = k.sb((64, NCH, 128), BF16)
            d["AG"] = k.sb((64, NCH, 128), BF16)
            d["ST"] = k.sb((64, 64), BF16)
            k.memset("pool", d["ST"][:], 0.0, [d["ST"]])
            d["YS"] = k.sb((64, TT), BF16)
        PB0 = k.ps((128, 512), F32)
        PB1 = k.ps((128, 512), F32)
        PB3 = k.ps((128, 512), F32)
        PT = k.ps((64, 2, 4, 128), BF16)
        SC = k.ps((64, 2048), F32, n=4)
        PJ = [(PB0, 0)]
        PM = [(PB1, 0), (PB3, 0)]
        pmc = [0]

        def pm():
            return PM[pmc[0]]

        def pap(slot, rows=128, cols=TT):
            t, i = slot
            return t[0:rows, 0:cols]

        pjc = [0]

        def emitT(ti):
            t0 = ti * TT
            par = ti % 2
            xt = XT32[par]
            xb = XB[par]
            pj = PROJ[par]
            pjn = PROJ[1 - par]
            def x_dma(tj):
                pj_ = tj % 2
                S.dma("sp", XT32[pj_][:], xT.rearrange("(kc p) t -> p kc t", p=128)[:, :, tj * TT:(tj + 1) * TT], W=[XT32[pj_]], slot=f"x{pj_}")

            def x_prep(tj):
                pj_ = tj % 2
                k.cp("dve", XB[pj_][:], XT32[pj_][:], [XT32[pj_]], [XB[pj_]])
                k.act(XSQs[pj_][:], XT32[pj_][:], AF.Square, [XT32[pj_]], [XSQs[pj_]])
            if ti == 0:
                x_dma(0)
                x_prep(0)
            if ti + 1 < nt:
                x_dma(ti + 1)
            XSQ = XSQs[par]
            slot = pm()
            k.mm([(pap(slot), onesm[:], XSQ[:, kc, :], kc == 0, kc == 7) for kc in range(8)], [onesm, XSQ], [slot])
            k.rsqrt(RSTD[:], pap(slot), NORM_EPS, [slot], [RSTD])
            def shift(c0, c1, pres):
                grp = [(pj, i) for i in range(c0, c1)] + [(pj, 10)]
                k.tt("dve", PP[:, c0:c1, :], pj[:, c0:c1, 0:TT], pj[:, c0:c1, 1:TT + 1], ALU.subtract, grp, [pres])
                k.tt("dve", PP[:, c0:c1, :], PP[:, c0:c1, :],
                     pv[:, PV_MU + c0:PV_MU + c1].unsqueeze(2).broadcast_to([128, c1 - c0, TT]), ALU.mult, [pres, pv], [pres])
                k.tt("dve", PP[:, c0:c1, :], PP[:, c0:c1, :], pj[:, c0:c1, 1:TT + 1], ALU.add, [pres] + grp, [pres])
            for n_, ci in enumerate([6, 7, 8, 9, 0, 1, 2, 3, 4, 5]):
                co, M = RW_CH[ci]
                slot = PJ[0]
                k.mm([(pap(slot, M), WB[:, kc, co:co + M], xb[:, kc, :], kc == 0, kc == 7) for kc in range(8)],
                     [WB, xb], [slot])
                k.tt("dve", pj[0:M, ci, 1:TT + 1], pap(slot, M), RSTD[0:M, :], ALU.mult, [slot, RSTD], [(pj, ci)])
                if n_ == 3:
                    shift(6, 10, (PP, 1))
                    k.act(X6[:, 0, :], PP[0:64, 6, :], AF.Tanh, [(PP, 1)], [X6])
                    k.cp("pool", X6[:, 1, :], PP[0:64, 7, :], [(PP, 1)], [X6])
                    k.act(SG[:], PP[:, 8, :], AF.Sigmoid, [(PP, 1)], [SG])
                    k.act(SG8[:], PP[0:32, 9, :], AF.Sigmoid, [(PP, 1)], [SG8])
            shift(0, 6, (PP, 0))
            allpj = [(pj, i) for i in range(11)]
            k.cp("pool", pjn[:, :, 0:1], pj[:, :, TT:TT + 1], allpj, [(pjn, 10)])
            if ti + 1 < nt:
                x_prep(ti + 1)
        def emitA(ti, hp, u):
            d = tmp[u % 4]
            hs = slice(hp * 128, (hp + 1) * 128)
            pc = lambda c: pv[:, c + hp:c + hp + 1]
            PR, PK, PVv = PP[:, 0 + hp, :], PP[:, 2 + hp, :], PP[:, 4 + hp, :]
            FM, TM = d["FM"], d["TM"]
            heads = [hp * 2, hp * 2 + 1]
            t0 = ti * TT
            hs = slice(hp * 128, (hp + 1) * 128)
            pc = lambda c: pv[:, c + hp:c + hp + 1]
            PR, PK, PVv = PP[:, 0 + hp, :], PP[:, 2 + hp, :], PP[:, 4 + hp, :]
            FM, TM = d["FM"], d["TM"]
            s1 = pm()
            k.mm([(pap(s1), w2b[:, hs], X6[:, 0, :], True, True)], [w2b, X6], [s1])
            k.act(d["LD"][:], pap(s1), AF.Sigmoid, [s1, pv], [d["LD"]], bias=pc(PV_W0))
            s2 = pm()
            k.mm([(pap(s2), a2b[:, hs], X6[:, 1, :], True, True)], [a2b, X6], [s2])
            k.act(d["AS"][:], pap(s2), AF.Sigmoid, [s2, pv], [d["AS"]], bias=pc(PV_A0))
            s3 = pm()
            k.mm([(pap(s3), g2b0[:, hs], SG[:], True, False), (pap(s3), g2b1[:, hs], SG8[:], False, True)],
                 [g2b0, g2b1, SG, SG8], [s3])
            k.cp("act", d["GT"][:], pap(s3), [s3], [d["GT"]])
            S.op("dve", lambda e, d=d: e.tensor_tensor_scan(out=d["CS"][:], data0=rmask[:], data1=d["LD"][:],
                                                             initial=0.0, op0=ALU.mult, op1=ALU.add),
                 _res([rmask, d["LD"]]), _res([d["CS"]]))
            k.act(d["WI"][:], d["CS"][:], AF.Exp, [d["CS"]], [d["WI"]], scale=-CDEC)
            k.act(d["WV"][:], d["CS"][:], AF.Exp, [d["CS"]], [d["WV"]], scale=CDEC)
            k.tt("dve", d["E1"][:], d["CS"][:], d["LD"][:], ALU.subtract, [d["CS"], d["LD"]], [d["E1"]])
            k.act(d["WE"][:], d["E1"][:], AF.Exp, [d["E1"]], [d["WE"]], scale=-CDEC)
            cs3 = d["CS"][:].rearrange("p (c t) -> p c t", t=CH)
            k.tt("pool", d["E2"][:].rearrange("p (c t) -> p c t", t=CH),
                 cs3[:, :, CH - 1:CH].broadcast_to([128, NCH, CH]), cs3, ALU.subtract, [d["CS"]], [d["E2"]])
            k.act(d["WH"][:], d["E2"][:], AF.Exp, [d["E2"]], [d["WH"]], scale=-CDEC)
            wi3 = d["WI"][:].rearrange("p (c t) -> p c t", t=CH)
            k.tt("dve", FM[:, 4, :].rearrange("p (c t) -> p c t", t=CH),
                 id2f[:].unsqueeze(1).broadcast_to([128, NCH, CH]),
                 wi3[:, :, CH - 1:CH].broadcast_to([128, NCH, CH]), ALU.mult, [id2f, d["WI"]], [(FM, 4)])
            k.ts("dve", d["KK"][:], PK, pc(PV_KK), None, ALU.mult, None, [(PP, 0), pv], [d["KK"]])
            k.tt("dve", d["KK2"][:], d["KK"][:], d["KK"][:], ALU.mult, [d["KK"]], [d["KK2"]])
            k.ts("dve", d["T1"][:], d["AS"][:], -1.0, pc(PV_KA), ALU.add, ALU.mult, [d["AS"], pv], [d["T1"]])
            k.stt(d["KM"][:], d["T1"][:], 1.0, PK, ALU.add, ALU.mult, [d["T1"], (PP, 0)], [d["KM"]])
            s4 = pm()
            k.mm([(pap(s4), bd[:], d["KK2"][:], True, True)], [bd, d["KK2"]], [s4])
            k.ts("dve", d["RN"][:], pap(s4), 1e-24, None, ALU.max, None, [s4], [d["RN"]])
            k.stt(d["RKB"][:], PR, pc(PV_RK), d["KM"][:], ALU.mult, ALU.mult, [(PP, 0), pv, d["KM"]], [d["RKB"]])
            k.tt("pool", FM[:, 3, :], PR, d["WI"][:], ALU.mult, [(PP, 0), d["WI"]], [(FM, 3)])
            k.cp("pool", TM[:, 2, :], PVv, [(PP, 0)], [(TM, 2)])
            k.tt("dve", FM[:, 0, :], d["KM"][:], d["WV"][:], ALU.mult, [d["KM"], d["WV"]], [(FM, 0)])
            k.tt("dve", TM[:, 1, :], d["KM"][:], d["WH"][:], ALU.mult, [d["KM"], d["WH"]], [(TM, 1)])
            s5 = pm()
            k.mm([(pap(s5), bd[:], d["RKB"][:], True, True)], [bd, d["RKB"]], [s5])
            k.tt("dve", d["BON"][:], pap(s5), PVv, ALU.mult, [s5, (PP, 0)], [d["BON"]])
            k.act(d["RN"][:], d["RN"][:], AF.Ln, [d["RN"]], [d["RN"]])
            k.act(d["RN"][:], d["RN"][:], AF.Exp, [d["RN"]], [d["RN"]], scale=-0.5)
            k.tt("dve", d["KN"][:], d["KK"][:], d["RN"][:], ALU.mult, [d["KK"], d["RN"]], [d["KN"]])
            k.tt("pool", d["B"][:], d["KN"][:], d["AS"][:], ALU.mult, [d["KN"], d["AS"]], [d["B"]])
            k.stt(FM[:, 2, :], d["KN"][:], -1.0, d["WE"][:], ALU.mult, ALU.mult, [d["KN"], d["WE"]], [(FM, 2)])
            k.tt("pool", FM[:, 1, :], d["B"][:], d["WV"][:], ALU.mult, [d["B"], d["WV"]], [(FM, 1)])
            k.tt("dve", TM[:, 0, :], d["B"][:], d["WH"][:], ALU.mult, [d["B"], d["WH"]], [(TM, 0)])
            if DBG and ti == 0 and hp == 0:
                for ci in range(10):
                    dump(S, PP, PP[:, ci, :], "f", f"PP{ci}")
                for nm in ["LD", "AS", "GT", "KN", "KM", "B", "CS", "WI", "WE", "WH", "BON"]:
                    dump(S, d[nm], d[nm][:], "f", nm)
                for q in range(5):
                    dump(S, FM, FM[:, q, :], "b", f"FM{q}")
                for q in range(3):
                    dump(S, TM, TM[:, q, :], "b", f"TM{q}")
            S.dma("act", d["FMo"][:], FM[64:128, :, :], R=[FM], W=[d["FMo"]], slot=f"fmo{u % 2}")
            srcs = [FM[:, 2, :], TM[:, 0, :], TM[:, 1, :], TM[:, 2, :]]
            for half in range(2):
                fns = []
                for cc in range(2):
                    c = half * 2 + cc
                    for q in range(4):
                        fns.append(lambda e, cc=cc, q=q, c=c: e.transpose(
                            out=PT[:, cc, q, :], in_=srcs[q][:, c * CH:(c + 1) * CH], identity=ident[:]))
                S.op("pe", fns, _res([FM, TM, ident]), _res([PT]))
                for e_ in range(2):
                    h = hp * 2 + e_
                    S1, VT = hd[h]["S1"], hd[h]["VT"]
                    cs_ = slice(half * 2, half * 2 + 2)
                    es_ = slice(e_ * 64, e_ * 64 + 64)
                    eng = "dve"
                    k.cp(eng, S1[:, cs_, 0:64], PT[:, :, 0, es_], [PT], [(S1, 1)])
                    k.cp(eng, S1[:, cs_, 320:384], PT[:, :, 1, es_], [PT], [(S1, 1)])
                    k.cp(eng, S1[:, cs_, 512:576], PT[:, :, 2, es_], [PT], [(S1, 1)])
                    k.cp(eng, VT[:, cs_, :], PT[:, :, 3, es_], [PT], [VT])
        def emitTA(ti, hp, u):
            pmc[0] = 0
            if hp == 0:
                emitT(ti)
            emitA(ti, hp, u)
        def emitB(ti, hp, u):
            d = tmp[u % 4]
            hs = slice(hp * 128, (hp + 1) * 128)
            pc = lambda c: pv[:, c + hp:c + hp + 1]
            PR, PK, PVv = PP[:, 0 + hp, :], PP[:, 2 + hp, :], PP[:, 4 + hp, :]
            FM, TM = d["FM"], d["TM"]
            heads = [hp * 2, hp * 2 + 1]
            t0 = ti * TT
            heads = [hp * 2, hp * 2 + 1]
            fmh = [FM[0:64], d["FMo"]]
            fmr = [[(FM, q) for q in range(5)], [d["FMo"]]]
            reg = [0, 1024]
            regres = [[(SC, 0), (SC, 1)], [(SC, 2), (SC, 3)]]
            for half in range(2):
                for e_ in range(2):
                    h = heads[e_]
                    f = fmh[e_]
                    items = []
                    for cc in range(2):
                        c = half * 2 + cc
                        tsl = slice(c * CH, (c + 1) * CH)
                        base = reg[e_] + cc * 384
                        items.append((SC[:, base:base + 128], f[:, 2, tsl], f[:, 0:2, tsl], True, True))
                        items.append((SC[:, base + 128:base + 256], f[:, 1, tsl], f[:, 2:4, tsl], True, True))
                        items.append((SC[:, base + 256:base + 384], f[:, 0, tsl], f[:, 2:4, tsl], True, True))
                    k.mm(items, fmr[e_], regres[e_])
                for e_ in range(2):
                    h = heads[e_]
                    S1 = hd[h]["S1"]
                    cs_ = slice(half * 2, half * 2 + 2)
                    src = SC[:, reg[e_]:reg[e_] + 768].rearrange("p (c x) -> p c x", x=384)
                    m6 = mask6[:].unsqueeze(1).broadcast_to([64, 2, 384])
                    k.tt("dve", S1[:, cs_, 64:320], src[:, :, 0:256], m6[:, :, 0:256], ALU.mult,
                         regres[e_] + [mask6], [(S1, 0)])
                    k.tt("dve", S1[:, cs_, 384:512], src[:, :, 256:384], m6[:, :, 256:384], ALU.mult,
                         regres[e_] + [mask6], [(S1, 0)])
            for lvl in range(6):
                for e_ in range(2):
                    h = heads[e_]
                    S1 = hd[h]["S1"]
                    cur = hd[h]["XT"][lvl % 2]
                    i64 = ident[0:64, 0:64]
                    items = []
                    for c in range(NCH):
                        base = reg[e_] + c * 192
                        if lvl == 0:
                            NTm, Nm = S1[:, c, 128:192], S1[:, c, 192:256]
                            items.append((SC[:, base:base + 64], NTm, Nm, True, True))
                            items.append((SC[:, base + 128:base + 192], Nm, NTm, True, True))
                        elif lvl < 5:
                            items.append((SC[:, base:base + 128], cur[:, c, 128:192], cur[:, c, 0:128], True, False))
                            items.append((SC[:, base + 64:base + 128], i64, cur[:, c, 64:128], False, True))
                            items.append((SC[:, base + 128:base + 192], cur[:, c, 0:64], cur[:, c, 128:192], True, True))
                        else:
                            items.append((SC[:, base + 64:base + 128], cur[:, c, 128:192], cur[:, c, 64:128], True, False))
                            items.append((SC[:, base + 64:base + 128], i64, cur[:, c, 64:128], False, True))
                    k.mm(items, [(S1, 0)] if lvl == 0 else [cur, ident], regres[e_])
                for e_ in range(2):
                    h = heads[e_]
                    S1 = hd[h]["S1"]
                    nxt = hd[h]["XT"][(lvl + 1) % 2]
                    src = SC[:, reg[e_]:reg[e_] + 768].rearrange("p (c x) -> p c x", x=192)
                    eng = "act" if e_ == 0 else "dve"
                    if lvl == 0:
                        k.cp(eng, nxt[:, :, 0:64], src[:, :, 0:64], regres[e_], [nxt])
                        k.cp(eng, nxt[:, :, 128:192], src[:, :, 128:192], regres[e_], [nxt])
                        k.tt("pool", nxt[:, :, 64:128], S1[:, :, 192:256],
                             ident[0:64, 0:64].unsqueeze(1).broadcast_to([64, NCH, 64]), ALU.add,
                             [(S1, 0), ident], [nxt])
                    elif lvl < 5:
                        k.cp(eng, nxt[:, :, :], src[:, :, :], regres[e_], [nxt])
                    else:
                        k.cp(eng, nxt[:, :, 64:128], src[:, :, 64:128], regres[e_], [nxt])
            for e_ in range(2):
                h = heads[e_]
                S1 = hd[h]["S1"]
                Tm = hd[h]["XT"][0]
                items = [(SC[:, reg[e_] + c * 128:reg[e_] + (c + 1) * 128], Tm[:, c, 64:128], S1[:, c, 0:128], True, True)
                         for c in range(NCH)]
                k.mm(items, [Tm, S1], regres[e_])
            for e_ in range(2):
                h = heads[e_]
                k.cp("act" if e_ == 0 else "dve", hd[h]["Q3"][:],
                     SC[:, reg[e_]:reg[e_] + 512].rearrange("p (c x) -> p c x", x=128), regres[e_], [hd[h]["Q3"]])
            for e_ in range(2):
                h = heads[e_]
                S1, Q3, f = hd[h]["S1"], hd[h]["Q3"], fmh[e_]
                items = []
                for c in range(NCH):
                    o = SC[:, reg[e_] + c * 128:reg[e_] + (c + 1) * 128]
                    tsl = slice(c * CH, (c + 1) * CH)
                    items.append((o, Q3[:, c, 0:64], S1[:, c, 256:384], True, False))
                    items.append((o, ident[0:64, 0:64], f[:, 3:5, tsl], False, True))
                k.mm(items, [Q3, S1, ident] + fmr[e_], regres[e_])
            for e_ in range(2):
                h = heads[e_]
                k.cp("act" if e_ == 0 else "dve", hd[h]["RM"][:],
                     SC[:, reg[e_]:reg[e_] + 512].rearrange("p (c x) -> p c x", x=128), regres[e_], [hd[h]["RM"]])
            for e_ in range(2):
                h = heads[e_]
                S1, Q3 = hd[h]["S1"], hd[h]["Q3"]
                items = []
                for c in range(NCH):
                    o = SC[:, reg[e_] + c * 128:reg[e_] + (c + 1) * 128]
                    items.append((o, Q3[:, c, 64:128], S1[:, c, 256:384], True, False))
                    items.append((o, ident[0:64, 0:64], S1[:, c, 448:576], False, True))
                k.mm(items, [Q3, S1, ident], regres[e_])
            for e_ in range(2):
                h = heads[e_]
                k.cp("act" if e_ == 0 else "dve", hd[h]["AG"][:],
                     SC[:, reg[e_]:reg[e_] + 512].rearrange("p (c x) -> p c x", x=128), regres[e_], [hd[h]["AG"]])
            for c in range(NCH):
                for e_ in range(2):
                    h = heads[e_]
                    RM, AG, VT, ST = hd[h]["RM"], hd[h]["AG"], hd[h]["VT"], hd[h]["ST"]
                    yo = SC[:, reg[e_] + 768 + c * CH:reg[e_] + 768 + (c + 1) * CH]
                    so = SC[:, reg[e_] + 512:reg[e_] + 576]
                    yres = (SC, 1) if e_ == 0 else (SC, 3)
                    k.mm([(yo, ST[:], RM[:, c, 0:64], True, False), (yo, VT[:, c, :], AG[:, c, 0:64], False, True)],
                         [ST, RM, VT, AG], [yres])
                    k.mm([(so, RM[:, c, 64:128], ST[:], True, False), (so, AG[:, c, 64:128], VT[:, c, :], False, True)],
                         [ST, RM, VT, AG], [yres])
                    k.cp("act" if e_ == 0 else "dve", ST[:], so, [yres], [ST])
            for e_ in range(2):
                h = heads[e_]
                yres = (SC, 1) if e_ == 0 else (SC, 3)
                k.cp("act" if e_ == 0 else "dve", hd[h]["YS"][:], SC[:, reg[e_] + 768:reg[e_] + 1024], [yres], [hd[h]["YS"]])
            if DBG and ti == 0 and hp == 0:
                h0 = hd[0]
                dump(S, h0["S1"], h0["S1"][:, 0, :], "b", "S1c0", rows=64, cols=576)
                dump(S, h0["XT"][0], h0["XT"][0][:, 0, :], "b", "XTc0", rows=64, cols=192)
                dump(S, h0["Q3"], h0["Q3"][:, 0, :], "b", "Q3c0", rows=64, cols=128)
                dump(S, h0["RM"], h0["RM"][:, 0, :], "b", "RMc0", rows=64, cols=128)
                dump(S, h0["AG"], h0["AG"][:, 0, :], "b", "AGc0", rows=64, cols=128)
                dump(S, h0["VT"], h0["VT"][:, 0, :], "b", "VTc0", rows=64, cols=64)
                dump(S, h0["YS"], h0["YS"][:], "b", "YS0", rows=64)
                dump(S, hd[1]["YS"], hd[1]["YS"][:], "b", "YS1", rows=64)
                dump(S, d["FMo"], d["FMo"][:, 0, :], "b", "FMo0", rows=64)
        def emitC(ti, hp, u):
            pmc[0] = 1
            d = tmp[u % 4]
            hs = slice(hp * 128, (hp + 1) * 128)
            pc = lambda c: pv[:, c + hp:c + hp + 1]
            PR, PK, PVv = PP[:, 0 + hp, :], PP[:, 2 + hp, :], PP[:, 4 + hp, :]
            FM, TM = d["FM"], d["TM"]
            heads = [hp * 2, hp * 2 + 1]
            t0 = ti * TT
            sy = pm()
            k.mm([(pap(sy), ilu[:, 0:128], hd[heads[0]]["YS"][:], True, False),
                  (pap(sy), ilu[:, 128:256], hd[heads[1]]["YS"][:], False, True)],
                 [ilu, hd[heads[0]]["YS"], hd[heads[1]]["YS"]], [sy])
            k.cp("act", d["YC"][:], pap(sy), [sy], [d["YC"]])
            k.cp("dve", d["YB"][:], d["YC"][:], [d["YC"]], [d["YB"]])
            k.tt("dve", d["YSQ"][:], d["YC"][:], d["YC"][:], ALU.mult, [d["YC"]], [d["YSQ"]])
            sm = pm()
            k.mm([(pap(sm), bd64[:], d["YB"][:], True, True)], [bd64, d["YB"]], [sm])
            k.cp("act", d["MS"][:], pap(sm), [sm], [d["MS"]])
            sq = pm()
            k.mm([(pap(sq), bd64[:], d["YSQ"][:], True, True)], [bd64, d["YSQ"]], [sq])
            k.tt("pool", d["NEG"][:], d["MS"][:], d["MS"][:], ALU.mult, [d["MS"]], [d["NEG"]])
            k.tt("dve", d["VAR"][:], pap(sq), d["NEG"][:], ALU.subtract, [sq, d["NEG"]], [d["VAR"]])
            k.rsqrt(d["RS"][:], d["VAR"][:], GN_EPS, [d["VAR"]], [d["RS"]])
            k.tt("dve", d["YC"][:], d["YC"][:], d["MS"][:], ALU.subtract, [d["YC"], d["MS"]], [d["YC"]])
            k.tt("pool", d["YN"][:], d["YC"][:], d["RS"][:], ALU.mult, [d["YC"], d["RS"]], [d["YN"]])
            k.ts("dve", d["YG"][:], d["YN"][:], pc(PV_GW), pc(PV_GB), ALU.mult, ALU.add, [d["YN"], pv], [d["YG"]])
            k.tt("pool", d["YG"][:], d["YG"][:], d["BON"][:], ALU.add, [d["YG"], d["BON"]], [d["YG"]])
            k.tt("pool", d["YO"][:], d["YG"][:], d["GT"][:], ALU.mult, [d["YG"], d["GT"]], [d["YO"]])
            if not fused:
                S.dma("sp", yr[hp * 128:(hp + 1) * 128, t0:t0 + TT], d["YO"][:], R=[d["YO"]], slot=f"yo{hp}")
            else:
                k.cp("pool", YOb[hp][:], d["YO"][:], [d["YO"]], [YOb[hp]])
            if hp == 1:
                emitP(ti)
        def emitP(ti):
            t0 = ti * TT
            if fused:
                ntok = ctx["ntok"]
                sh, col = t0 // ntok, t0 % ntok
                for oc in range(8):
                    so = pm()
                    k.mm([(pap(so), PRWb[:, hp_, oc * 128:(oc + 1) * 128], YOb[hp_][:], hp_ == 0, hp_ == 1) for hp_ in range(2)],
                         [PRWb, YOb[0], YOb[1]], [so])
                    k.cp("act", PRD[:, oc, :], pap(so), [so], [PRD])
                S.dma("sp", ctx["src_r"][sh * 1024:sh * 1024 + 1024, col:col + TT].rearrange("(oc p) t -> p oc t", p=128),
                      PRD[:], R=[PRD], slot="prd")
        units = [(ti, hp) for ti in range(nt) for hp in range(2)]
        NU = len(units)
        S.replay([S.record(emitTA, units[0][0], units[0][1], 0)])
        for n in range(NU + 1):
            streams = []
            if n < NU:
                streams.append(S.record(emitB, units[n][0], units[n][1], n))
            if n + 1 < NU:
                streams.append(S.record(emitTA, units[n + 1][0], units[n + 1][1], n + 1))
            if n >= 1:
                streams.append(S.record(emitC, units[n - 1][0], units[n - 1][1], n - 1))
            S.replay(streams)
        S.barrier()
    return nc


def p1a_inputs(x_b, j, norm_mix_g, w_in, shift_mu, w0, w2, a0, a2, g2, k_k, k_a, r_k, gn_w, gn_b):
    cs = slice(256 * j, 256 * j + 256)
    cols = np.concatenate([np.arange(256) + 256 * j, 1024 + np.arange(256) + 256 * j, 2048 + np.arange(256) + 256 * j,
                           np.arange(3072, 3360)])
    w1 = np.ascontiguousarray(w_in[:, cols])
    mu = shift_mu[cols]
    pvec = np.zeros((128, NPV), np.float32)
    for ci, (co, M) in enumerate(RW_CH):
        pvec[:M, PV_MU + ci] = mu[co:co + M]
    def two(v):
        return np.ascontiguousarray(v[cs].reshape(2, 128).T)
    pvec[:, PV_W0:PV_W0 + 2] = two(w0)
    pvec[:, PV_A0:PV_A0 + 2] = two(a0)
    pvec[:, PV_KK:PV_KK + 2] = two(k_k)
    pvec[:, PV_KA:PV_KA + 2] = two(k_a)
    pvec[:, PV_RK:PV_RK + 2] = two(r_k.reshape(-1))
    pvec[:, PV_GW:PV_GW + 2] = two(gn_w)
    pvec[:, PV_GB:PV_GB + 2] = two(gn_b)
    m = {"xT": np.ascontiguousarray(x_b.T), "w1": w1,
         "gmix": np.ascontiguousarray(norm_mix_g.reshape(8, 128).T), "pvec": pvec,
         "w2s": np.ascontiguousarray(w2[:, cs]), "a2s": np.ascontiguousarray(a2[:, cs]),
         "g2s": np.ascontiguousarray(g2[:, cs])}
    for kk, v in consts_p1().items():
        m["c_" + kk] = v
    return m


DILS = (1, 4, 16)


def attn_bias(j):
    out = np.zeros((128, 3, 2, 128), np.float32)
    kk = np.arange(128)[:, None].astype(np.float32)
    q = np.arange(128)[None, :].astype(np.float32)
    for g, d in enumerate(DILS):
        h = 4 * g + j
        slope = np.float32(2.0) ** np.float32(-8.0 * (h + 1.0) / 12.0)
        sp = q + 128 - kk
        out[:, g, 0, :] = np.where(sp <= 128, -slope * (sp * d), -30000.0)
        sc = q - kk
        out[:, g, 1, :] = np.where(sc >= 0, -slope * (sc * d), -30000.0)
    return out


def build_p1b(seq, ctx=None):
    nt = seq // TT
    fused = ctx is not None
    nc = ctx["nc"] if fused else bass.Bass("TRN2", target_bir_lowering=False)
    dr = _mk_dr(nc, ctx, "b_" if fused else "")
    xT = dr("xT", (D, seq))
    w1a = dr("w1a", (D, 576))
    gmix = dr("gmix", (128, 8))
    identd = dr("c_ident", (128, 128))
    biasd = dr("abias", (128, 3 * 2 * 128))
    seld = dr("sel", (65, 64))
    ya = None if fused else dr("ya", (64, seq), kind="ExternalOutput")
    pat = dr("pat", (64, D)) if fused else None
    with ExitStack() as es:
        if fused:
            k = ctx["k"]
            k.es = es
            k.epsb = {}
        else:
            k = K(nc, es)
        S = k.S
        if fused:
            PATb = k.sb((64, D), BF16)
            YAb = k.sb((64, 512), BF16)
            PRA = k.sb((128, 4, 512), BF16)
        idf = k.sb((128, 128), F32)
        S.dma("sp", idf[:], identd, W=[idf], slot="id")
        ident = k.sb((128, 128), BF16)
        k.cp("pool", ident[:], idf[:], [idf], [ident])
        bias = k.sb((128, 3, 2, 128), F32)
        S.dma("sp", bias[:].rearrange("p a b c -> p (a b c)"), biasd, W=[bias], slot="bias")
        sel = k.sb((65, 64), F32)
        S.dma("sp", sel[:], seld, W=[sel], slot="sel")
        gm = k.sb((128, 8), F32)
        S.dma("sp", gm[:], gmix, W=[gm], slot="gm")
        onesm = k.sb((128, 128), BF16)
        k.memset("pool", onesm[:], 1.0 / 1024, [onesm])
        WB = k.sb((128, 8, 576), BF16)
        wst = k.sb((128, 576), F32)
        for kc in range(8):
            S.dma("sp", wst[:], w1a[kc * 128:(kc + 1) * 128, :], W=[wst], slot="w")
            k.ts("dve", WB[:, kc, :], wst[:], gm[:, kc:kc + 1], None, ALU.mult, None, [wst, gm], [WB])
        if fused:
            for h_ in range(2):
                S.dma("sp", wst[0:64, 0:512], pat[:, h_ * 512:(h_ + 1) * 512], W=[wst], slot="w")
                k.cp("pool", PATb[:, h_ * 512:(h_ + 1) * 512], wst[0:64, 0:512], [wst], [PATb])
        XT32 = [k.sb((128, 8, TT), F32) for _ in range(2)]
        XBs = [k.sb((128, 8, TT), BF16) for _ in range(2)]
        XSQ1 = k.sb((128, 8, TT), BF16)
        XSQs = [XSQ1, XSQ1]
        RSTD = k.sb((128, TT), F32)
        QKVs = [k.sb((64, 3, seq), BF16) for _ in range(2)]
        OF = k.sb((65, seq), F32)
        VA = k.sb((128, seq // 128, 65), BF16)
        k.memset("pool", VA[:], 1.0, [VA])
        TMPs = [k.sb((128, 2, 128), F32) for _ in range(2)]
        PTbs = [k.sb((128, 2, 128), BF16) for _ in range(2)]
        RD = k.sb((64, 512), F32)
        YA = k.sb((64, 512), F32)
        PJ = k.ps((128, 512), F32)
        PM = k.ps((128, 512), F32)
        PSSs = [k.ps((128, 512), F32) for _ in range(2)]
        POs = [k.ps((128, 512), F32) for _ in range(2)]
        PSS = PSSs[0]
        PTr = k.ps((128, 4, 128), BF16)

        def emit_proj(g):
            QKV = QKVs[g % 2]

            def x_dma(tj):
                pj_ = tj % 2
                S.dma("sp", XT32[pj_][:], xT.rearrange("(kc p) t -> p kc t", p=128)[:, :, tj * TT:(tj + 1) * TT], W=[XT32[pj_]], slot=f"x{pj_}")

            def x_prep(tj):
                pj_ = tj % 2
                k.cp("dve", XBs[pj_][:], XT32[pj_][:], [XT32[pj_]], [XBs[pj_]])
                k.act(XSQs[pj_][:], XT32[pj_][:], AF.Square, [XT32[pj_]], [XSQs[pj_]])
            x_dma(0)
            x_prep(0)
            for ti in range(nt):
                t0 = ti * TT
                XB_, XSQ_ = XBs[ti % 2], XSQs[ti % 2]
                if ti + 1 < nt:
                    x_dma(ti + 1)
                k.mm([(PM[:, 0:TT], onesm[:], XSQ_[:, kc, :], kc == 0, kc == 7) for kc in range(8)], [onesm, XSQ_], [PM])
                k.rsqrt(RSTD[:], PM[:, 0:TT], NORM_EPS, [PM], [RSTD])
                for qi in range(3):
                    co = qi * 192 + g * 64
                    k.mm([(PJ[0:64, 0:TT], WB[:, kc, co:co + 64], XB_[:, kc, :], kc == 0, kc == 7) for kc in range(8)],
                         [WB, XB_], [PJ])
                    k.tt("dve", QKV[:, qi, t0:t0 + TT], PJ[0:64, 0:TT], RSTD[0:64, :], ALU.mult, [PJ, RSTD], [QKV])
                if ti + 1 < nt:
                    x_prep(ti + 1)

        def emit_blocks(g):
            d = DILS[g]
            QKV = QKVs[g % 2]
            span = 128 * d

            def blk_ap(qi, n, r):
                v = QKV[:, qi, n * span:(n + 1) * span]
                if d == 1:
                    return v
                return v.rearrange("p (q r) -> p r q", r=d)[:, r, :]

            def of_ap(n, r, rows):
                v = OF[0:rows, n * span:(n + 1) * span]
                if d == 1:
                    return v
                return v.rearrange("p (q r) -> p r q", r=d)[:, r, :]
            nb = seq // span
            for n in range(nb):
                for r0 in range(0, d, 4):
                    rr = list(range(r0, min(d, r0 + 4)))
                    fns = [lambda e, i=i, r=r, n=n: e.transpose(out=PTr[:, i, 0:64], in_=blk_ap(2, n, r), identity=ident[0:64, 0:64])
                           for i, r in enumerate(rr)]
                    S.op("pe", fns, _res([QKV, ident]), _res([PTr]))
                    b0 = n * d + r0
                    k.cp("dve", VA[:, b0:b0 + len(rr), 0:64], PTr[:, 0:len(rr), 0:64], [PTr], [VA])
            for n in range(nb):
                for r in range(d):
                    b = n * d + r
                    PSSb, PO, TMP, PTb = PSSs[b % 2], POs[b % 2], TMPs[b % 2], PTbs[b % 2]
                    items = []
                    if n > 0:
                        items.append((PSSb[:, 0:128], blk_ap(1, n - 1, r), blk_ap(0, n, r), True, True))
                    items.append((PSSb[:, 128:256], blk_ap(1, n, r), blk_ap(0, n, r), True, True))
                    k.mm(items, [QKV], [PSSb])
                    lo = 0 if n > 0 else 1
                    k.stt(TMP[:, lo:2, :], PSSb[:, lo * 128:256].rearrange("p (a b) -> p a b", b=128), 0.125,
                          bias[:, g, lo:2, :], ALU.mult, ALU.add, [PSSb, bias], [TMP])
                    k.act(PTb[:, lo:2, :], TMP[:, lo:2, :], AF.Exp, [TMP], [PTb])
                    items = []
                    if n > 0:
                        items.append((PO[0:65, 0:128], VA[:, b - d, :], PTb[:, 0, :], True, False))
                    items.append((PO[0:65, 0:128], VA[:, b, :], PTb[:, 1, :], n == 0, True))
                    k.mm(items, [VA, PTb], [PO])
                    if g == 0:
                        k.cp("dve", of_ap(n, r, 65), PO[0:65, 0:128], [PO], [OF])
                    else:
                        k.tt("dve", of_ap(n, r, 65), PO[0:65, 0:128], of_ap(n, r, 65), ALU.add, [PO, OF], [OF])

        S.replay([S.record(emit_proj, 0)])
        for g in range(3):
            streams = [S.record(emit_blocks, g)]
            if g + 1 < 3:
                streams.append(S.record(emit_proj, g + 1))
            S.replay(streams)
        RD2 = [RD, k.sb((64, 512), F32)]
        YA2 = [YA, k.sb((64, 512), F32)]
        if fused:
            YAb2 = [YAb, k.sb((64, 512), BF16)]
            PRA2 = [PRA, k.sb((128, 4, 512), BF16)]

        def emit_norm(par):
            RDp, YAp = RD2[par], YA2[par]
            pden = PM if par == 0 else PJ
            pps = [PSSs[par], POs[par]]
            for c0 in range(par * 512, seq, 1024):
                k.mm([(pden[0:64, 0:512], sel[:], OF[:, c0:c0 + 512], True, True)], [sel, OF], [pden])
                S.op("dve", lambda e: e.reciprocal(out=RDp[:], in_=pden[0:64, 0:512]), _res([pden]), _res([RDp]))
                k.tt("dve", YAp[:], OF[0:64, c0:c0 + 512], RDp[:], ALU.mult, [OF, RDp], [YAp])
                if not fused:
                    S.dma("sp", ya[:, c0:c0 + 512], YAp[:], R=[YAp], slot=f"ya{par}")
                else:
                    ntok = ctx["ntok"]
                    sh, col = c0 // ntok, c0 % ntok
                    k.cp("pool", YAb2[par][:], YAp[:], [YAp], [YAb2[par]])
                    for och in range(2):
                        for o4 in range(4):
                            oc = och * 4 + o4
                            pp_ = pps[oc % 2]
                            k.mm([(pp_[:, 0:512], PATb[:, oc * 128:(oc + 1) * 128], YAb2[par][:], True, True)], [PATb, YAb2[par]], [pp_])
                            k.cp("act" if par == 0 else "dve", PRA2[par][:, o4, :], pp_[:, 0:512], [pp_], [PRA2[par]])
                        r0 = sh * 1024 + och * 512
                        S.dma("sp", ctx["src_a"][r0:r0 + 512, col:col + 512].rearrange("(oc p) t -> p oc t", p=128),
                              PRA2[par][:], R=[PRA2[par]], slot=f"pra{par}")
        S.replay([S.record(emit_norm, 0), S.record(emit_norm, 1)])
        S.barrier()
    return nc


def p1b_inputs(x_b, j, norm_mix_g, w_in):
    cols = []
    for base in (3360, 3360 + 768, 3360 + 1536):
        for g in range(3):
            h = 4 * g + j
            cols.append(base + 64 * h + np.arange(64))
    cols = np.concatenate(cols)
    sel = np.zeros((65, 64), np.float32)
    sel[64, :] = 1
    return {"xT": np.ascontiguousarray(x_b.T), "w1a": np.ascontiguousarray(w_in[:, cols]),
            "gmix": np.ascontiguousarray(norm_mix_g.reshape(8, 128).T), "c_ident": np.eye(128, dtype=np.float32),
            "abias": attn_bias(j).reshape(128, -1), "sel": sel}


def build_p2(ntok, ctx):
    NPASS = min(ntok, 1024)
    NB = NPASS // 512
    nc = ctx["nc"]
    dr = lambda n, s, kind="ExternalInput": nc.dram_tensor("c_" + n, list(s), F32, kind=kind).ap()
    xT = dr("xT", (D, ntok))
    memT = dr("memT", (D, 256))
    wg = dr("wg", (D, 2048))
    w_out = dr("w_out", (D, D))
    wq = dr("wq", (D, D))
    wkv = dr("wkv", (D, 2048))
    wo = dr("wo", (D, D))
    w1 = dr("w1", (D, 4096))
    w2 = dr("w2", (4096, D))
    gains = dr("gains", (128, 40))
    outT = dr("outT", (D, ntok), kind="ExternalOutput")
    dst_r, dst_a, dstres_r, dstres_a = ctx["dst_r"], ctx["dst_a"], ctx["dstres_r"], ctx["dstres_a"]
    with ExitStack() as es:
        k = ctx["k"]
        k.es = es
        k.epsb = {}
        S = k.S
        gn = k.sb((128, 40), F32)
        S.dma("sp", gn[:], gains, W=[gn], slot="gn")
        onesm = k.sb((128, 128), BF16)
        k.memset("pool", onesm[:], 1.0 / 1024, [onesm])
        ones1 = k.sb((128, 128), BF16)
        k.memset("pool", ones1[:], 1.0, [ones1])
        NST = 4
        stF = [k.sb((128, 8, 256), F32) for _ in range(NST)]
        stB = [k.sb((128, 8, 256), BF16) for _ in range(NST)]
        PA = [k.ps((128, 512), F32) for _ in range(4)]
        PR = k.ps((128, 512), F32)
        PSc = [k.ps((128, 512), F32) for _ in range(2)]
        PD = k.ps((128, 512), F32)
        lc = [0]
        pac = [0]

        staged = {}

        def stage(W, kp, cb, key=None):
            key = key if key is not None else (id(W), kp, cb)
            if key in staged:
                return staged.pop(key)
            i = lc[0] % NST
            lc[0] += 1
            S.dma("sp" if i % 2 == 0 else "pool", stF[i][:],
                  W[kp * 1024:(kp + 1) * 1024, cb * 256:(cb + 1) * 256].rearrange("(kc p) o -> p kc o", p=128),
                  W=[stF[i]], slot=f"st{i}")
            k.cp("act", stB[i][:], stF[i][:], [stF[i]], [stB[i]])
            return stB[i]

        def prefetch(W, kp, cb):
            key = (id(W), kp, cb)
            if key not in staged:
                staged[key] = stage(W, kp, cb, key=("pf",) + key)

        pool8 = [False]

        def nextpa():
            lst = PA if not pool8[0] else PA + PSc + [PD, PR]
            p = lst[pac[0] % len(lst)]
            pac[0] += 1
            return p

        def linear(W, ocs, Xb, blocks, epi, xres=None):
            ocs = list(ocs)
            for i0 in range(0, len(ocs), 2):
                wb = stage(W, 0, ocs[i0] // 2)
                if i0 + 2 < len(ocs):
                    prefetch(W, 0, ocs[i0 + 2] // 2)
                for o2 in range(2):
                    oc = ocs[i0 + o2]
                    for blk in blocks:
                        pa = nextpa()
                        bs = slice(blk * 512, (blk + 1) * 512)
                        k.mm([(pa[:, :], wb[:, kc, o2 * 128:(o2 + 1) * 128], Xb[:, kc, bs], kc == 0, kc == 7) for kc in range(8)],
                             [wb] + (xres if xres is not None else [Xb]), [pa])
                        epi(oc, blk, pa[:, :], pa)

        XSQ = k.sb((128, 8, 512), BF16)
        RSTD = k.sb((128, NPASS), F32)

        def normed(X, gcol, out_bf, nblk, rstd=None, bw=512, off=0):
            rstd = RSTD if rstd is None else rstd
            for blk in range(nblk):
                bs = slice(blk * bw, (blk + 1) * bw)
                xs = slice(off + blk * bw, off + (blk + 1) * bw)
                k.act(XSQ[:, :, 0:bw], X[:, :, xs], AF.Square, [X], [XSQ])
                k.mm([(PR[:, 0:bw], onesm[:], XSQ[:, kc, 0:bw], kc == 0, kc == 7) for kc in range(8)], [onesm, XSQ], [PR])
                k.rsqrt(rstd[:, bs], PR[:, 0:bw], NORM_EPS, [PR], [rstd])
                if out_bf is not None:
                    for kc in range(8):
                        k.stt(out_bf[:, kc, xs], X[:, kc, xs], gn[:, gcol + kc:gcol + kc + 1], rstd[:, bs], ALU.mult, ALU.mult,
                              [X, gn, rstd], [out_bf])

        M32 = k.sb((128, 8, 256), F32)
        S.dma("act", M32[:], memT.rearrange("(kc p) t -> p kc t", p=128), W=[M32], slot="mem")
        MN = k.sb((128, 8, 256), BF16)
        RSM = k.sb((128, 256), F32)
        normed(M32, 16, MN, 1, rstd=RSM, bw=256)
        KT = k.sb((128, 8, 256), BF16)
        for cb in range(4):
            wb = stage(wkv, 0, cb)
            for o2 in range(2):
                oc = 2 * cb + o2
                pa = nextpa()
                k.mm([(pa[:, 0:256], wb[:, kc, o2 * 128:(o2 + 1) * 128], MN[:, kc, :], kc == 0, kc == 7) for kc in range(8)], [wb, MN], [pa])
                k.cp("act", KT[:, oc, :], pa[:, 0:256], [pa], [KT])
        VM = k.sb((128, 2, 1024), BF16)
        for cb in range(4):
            wb = stage(wkv, 0, 4 + cb)
            for o2 in range(2):
                oc = 2 * cb + o2
                pa = nextpa()
                for mc in range(2):
                    k.mm([(pa[:, mc * 128:(mc + 1) * 128], MN[:, kc, mc * 128:(mc + 1) * 128], wb[:, kc, o2 * 128:(o2 + 1) * 128], kc == 0, kc == 7)
                          for kc in range(8)], [wb, MN], [pa])
                k.cp("act", VM[:, :, oc * 128:(oc + 1) * 128], pa[:, 0:256].rearrange("p (a b) -> p a b", b=128), [pa], [VM])
        X = k.sb((128, 8, NPASS), F32)
        XN = k.sb((128, 8, NPASS), BF16)
        AB = k.sb((128, 32, NPASS), BF16)
        ABv = AB[:, 0:8, :]
        MRt = [k.sb((128, 512), BF16) for _ in range(2)]
        MAt = [k.sb((128, 512), BF16) for _ in range(2)]
        TG = [k.sb((128, 512), F32) for _ in range(2)]
        TR = TG
        trc = [0]
        PTm = k.sb((128, 2, 512), BF16)
        RDEN = k.sb((128, 512), F32)
        OAT = XN
        for ps_ in range(ntok // NPASS):
          toff = ps_ * NPASS
          S.dma("sp", X[:], xT.rearrange("(kc p) t -> p kc t", p=128)[:, :, toff:toff + NPASS], W=[X], slot="x")
          blocks = list(range(NB))
          normed(X, 0, XN, NB)
          for cb in range(4):
            wbr_ = stage(wg, 0, cb)
            wba_ = stage(wg, 0, 4 + cb)
            for o2 in range(2):
              oc = 2 * cb + o2
              wbr = wbr_[:, :, o2 * 128:(o2 + 1) * 128]
              wba = wba_[:, :, o2 * 128:(o2 + 1) * 128]
              for blk in blocks:
                  bs = slice(blk * 512, (blk + 1) * 512)
                  gs = slice(toff + blk * 512, toff + (blk + 1) * 512)
                  mr, ma = MRt[blk % 2], MAt[blk % 2]
                  S.dma("sp", mr[:], dst_r[oc * 128:(oc + 1) * 128, gs], R=[dstres_r], W=[mr], slot=f"mr{blk % 2}")
                  S.dma("pool", ma[:], dst_a[oc * 128:(oc + 1) * 128, gs], R=[dstres_a], W=[ma], slot=f"ma{blk % 2}")
                  pa = nextpa()
                  k.mm([(pa[:, :], wbr[:, kc, :], XN[:, kc, bs], kc == 0, kc == 7) for kc in range(8)], [wbr_, XN], [pa])
                  k.act(TG[0][:], pa[:, :], AF.Sigmoid, [pa], [TG[0]])
                  k.tt("dve", TG[0][:], TG[0][:], mr[:], ALU.mult, [mr, TG[0]], [TG[0]])
                  pa = nextpa()
                  k.mm([(pa[:, :], wba[:, kc, :], XN[:, kc, bs], kc == 0, kc == 7) for kc in range(8)], [wba_, XN], [pa])
                  k.act(TG[1][:], pa[:, :], AF.Sigmoid, [pa], [TG[1]])
                  k.tt("dve", TG[1][:], TG[1][:], ma[:], ALU.mult, [ma, TG[1]], [TG[1]])
                  k.tt("dve", ABv[:, oc, bs], TG[0][:], TG[1][:], ALU.add, [TG[0], TG[1]], [AB])
          def epi_res(oc, blk, ap, pt):
              bs = slice(blk * 512, (blk + 1) * 512)
              k.tt("dve", X[:, oc, bs], ap, X[:, oc, bs], ALU.add, [pt, X], [X])
          linear(w_out, range(8), ABv, blocks, epi_res, xres=[AB])
          normed(X, 8, XN, NB)
          linear(wq, range(8), XN, blocks,
                 lambda oc, blk, ap, pt: k.cp("act", ABv[:, oc, blk * 512:(blk + 1) * 512], ap, [pt], [AB]))
          for blk in blocks:
              bs = slice(blk * 512, (blk + 1) * 512)
              for h in range(4):
                  for mc in range(2):
                      ps = PSc[mc]
                      k.mm([(ps[:, :], KT[:, 2 * h + hf, mc * 128:(mc + 1) * 128], ABv[:, 2 * h + hf, bs], hf == 0, hf == 1)
                            for hf in range(2)], [KT, AB], [ps])
                      k.act(PTm[:, mc, :], ps[:, :], AF.Exp, [ps], [PTm], scale=0.0625)
                  k.mm([(PD[:, :], ones1[:], PTm[:, mc, :], mc == 0, mc == 1) for mc in range(2)], [ones1, PTm], [PD])
                  S.op("dve", lambda e: e.reciprocal(out=RDEN[:], in_=PD[:, :]), _res([PD]), _res([RDEN]))
                  for ch in range(2):
                      oc = 2 * h + ch
                      k.mm([(PR[:, :], VM[:, mc, oc * 128:(oc + 1) * 128], PTm[:, mc, :], mc == 0, mc == 1) for mc in range(2)],
                           [VM, PTm], [PR])
                      k.tt("dve", OAT[:, oc, bs], PR[:, :], RDEN[:], ALU.mult, [PR, RDEN], [OAT])
          linear(wo, range(8), OAT, blocks, epi_res)
          normed(X, 24, XN, NB)

          def epi_u(oc, blk, ap, pt):
              t = TR[trc[0] % 2]
              trc[0] += 1
              k.act(t[:], ap, AF.Relu, [pt], [t])
              k.tt("dve", AB[:, oc, blk * 512:(blk + 1) * 512], t[:], t[:], ALU.mult, [t], [AB])
          linear(w1, range(32), XN, blocks, epi_u)
          pool8[0] = True
          for cb in range(4):
              pas = {(o2, blk): nextpa() for o2 in range(2) for blk in blocks}
              for kp in range(4):
                  wb = stage(w2, kp, cb)
                  if kp < 3:
                      prefetch(w2, kp + 1, cb)
                  elif cb < 3:
                      prefetch(w2, 0, cb + 1)
                  for o2 in range(2):
                      for blk in blocks:
                          k.mm([(pas[(o2, blk)][:, :], wb[:, kc, o2 * 128:(o2 + 1) * 128], AB[:, kp * 8 + kc, blk * 512:(blk + 1) * 512],
                                 kp == 0 and kc == 0, kp == 3 and kc == 7) for kc in range(8)], [wb, AB], [pas[(o2, blk)]])
              for o2 in range(2):
                  oc = 2 * cb + o2
                  for blk in blocks:
                      bs = slice(blk * 512, (blk + 1) * 512)
                      k.tt("dve", X[:, oc, bs], pas[(o2, blk)][:, :], X[:, oc, bs], ALU.add, [pas[(o2, blk)], X], [X])
          pool8[0] = False
          for blk in blocks:
              bs = slice(blk * 512, (blk + 1) * 512)
              normed(X, 32, None, 1, off=blk * 512)
              for kc in range(8):
                  k.stt(X[:, kc, bs], X[:, kc, bs], gn[:, 32 + kc:33 + kc], RSTD[:, 0:512], ALU.mult, ALU.mult, [X, gn, RSTD], [X])
          S.dma("sp", outT.rearrange("(kc p) t -> p kc t", p=128)[:, :, toff:toff + NPASS], X[:], R=[X], slot="out")

        S.barrier()
    return nc


def g8(v):
    return np.ascontiguousarray(np.asarray(v, np.float32).reshape(8, 128).T)


def build_fused(seq):
    ntok = seq // 4
    nc = bass.Bass("TRN2", target_bir_lowering=False)
    RG = [[0, 1, 2, 3], [4, 5, 6, 7]]
    with ExitStack() as es:
        k = K(nc, es)
        S = k.S
        mk = lambda n, r: nc.dram_tensor(n, [r, ntok], BF16).ap()
        src_r, dst_r, src_a, dst_a = mk("rs_src_r", 4096), mk("rs_dst_r", 1024), mk("rs_src_a", 4096), mk("rs_dst_a", 1024)
        ctx = {"nc": nc, "k": k, "src_r": src_r, "dst_r": dst_r, "src_a": src_a, "dst_a": dst_a, "ntok": ntok, "shared": {}}

        def reduce_scatter(name, src, dst):
            cc = es.enter_context(nc.semaphore(name))
            nc.gpsimd.collective_compute("ReduceScatter", ALU.add, replica_groups=RG, ins=[src.opt()], outs=[dst.opt()]).then_inc(cc, 1)
            S.sem[name] = cc
            S.cnt[name] = 1
            r = Res()
            r.w = (name, 1)
            return r
        S.pre = "a_"
        build_p1a(seq, ctx)
        ctx["dstres_r"] = reduce_scatter("cc_r", src_r, dst_r)
        S.pre = "b_"
        build_p1b(seq, ctx)
        ctx["dstres_a"] = reduce_scatter("cc_a", src_a, dst_a)
        S.pre = "c_"
        build_p2(ntok, ctx)
    return nc


def kernel(x, mem, norm_mix_g, w_in, shift_mu, w0, w2, a0, a2, g2, k_k, k_a, r_k, gn_w, gn_b,
           p_rwkv, p_attn, w_out, norm_x_g, norm_mem_g, xa_wq, xa_wkv, xa_wo,
           norm_ffn_g, ffn_w1, ffn_w2, norm_final_g):
    A = lambda v: np.asarray(v, np.float32)
    x, mem = A(x), A(mem)
    L0 = lambda v: A(v)[0]
    B, Sq, _ = x.shape
    cores = list(range(8))
    ntok = Sq // 4
    nc = build_fused(Sq)
    gains = np.concatenate([g8(L0(norm_mix_g)), g8(L0(norm_x_g)), g8(L0(norm_mem_g)), g8(L0(norm_ffn_g)), g8(A(norm_final_g))], axis=1)
    wgc = np.ascontiguousarray(L0(w_in)[:, 3360 + 2304:])
    maps = []
    for c in cores:
        b, j = c // 4, c % 4
        m = {}
        ma = p1a_inputs(x[b], j, L0(norm_mix_g), L0(w_in), L0(shift_mu), L0(w0), L0(w2), L0(a0), L0(a2), L0(g2),
                        L0(k_k), L0(k_a), L0(r_k), L0(gn_w), L0(gn_b))
        mb = p1b_inputs(x[b], j, L0(norm_mix_g), L0(w_in))
        shared = ("xT", "gmix", "c_ident")
        for kk_, v in ma.items():
            m[kk_ if kk_ in shared else "a_" + kk_] = v
        for kk_, v in mb.items():
            if kk_ not in shared:
                m["b_" + kk_] = v
        m["a_prw"] = np.ascontiguousarray(L0(p_rwkv)[256 * j:256 * j + 256])
        m["b_pat"] = np.ascontiguousarray(L0(p_attn)[64 * j:64 * j + 64])
        ts_ = slice(j * ntok, (j + 1) * ntok)
        mc = {"xT": np.ascontiguousarray(x[b, ts_].T), "memT": np.ascontiguousarray(mem[b].T),
              "wg": wgc, "w_out": L0(w_out), "wq": L0(xa_wq),
              "wkv": L0(xa_wkv), "wo": L0(xa_wo), "w1": L0(ffn_w1), "w2": L0(ffn_w2), "gains": gains}
        for kk_, v in mc.items():
            m["c_" + kk_] = v
        maps.append(m)
    r3 = run_bass_kernel_spmd(nc, maps, core_ids=cores).results
    out = np.zeros((B, Sq, D), np.float32)
    for c in cores:
        b, j = c // 4, c % 4
        out[b, j * ntok:(j + 1) * ntok] = A(r3[c]["c_outT"]).T
    return out
```

```python
import numpy as np
from contextlib import ExitStack
import concourse.bass as bass
import concourse.mybir as mybir
from concourse.bass_utils import run_bass_kernel_spmd

F32 = mybir.dt.float32
BF16 = mybir.dt.bfloat16
AF = mybir.ActivationFunctionType
ALU = mybir.AluOpType

D = 1024
SEQ = 8192
TT = 256
CH = 64
NCH = TT // CH
CDEC = 0.6065306597126334
GN_EPS = 64 * 1e-5
NORM_EPS = 1e-6


import os
STOP = 99
DBG = False


class _Stop(Exception):
    pass


def stop(n):
    if STOP == n:
        raise _Stop()


class Res:
    __slots__ = ("w", "r")

    def __init__(self):
        self.w = None
        self.r = {}


class Tl:
    def __init__(self, h, n=1):
        self.h = h
        self.rs = [Res() for _ in range(n)]

    def __getitem__(self, k):
        return self.h[k]


def _res(xs):
    out = []
    for x in xs:
        if isinstance(x, Tl):
            out.extend(x.rs)
        elif isinstance(x, tuple):
            out.append(x[0].rs[x[1]])
        else:
            out.append(x)
    return out


class Sched:
    def __init__(self, nc, es):
        self.nc = nc
        self.es = es
        self.engs = {"pe": nc.tensor, "dve": nc.vector, "act": nc.scalar, "pool": nc.gpsimd, "sp": nc.sync}
        self.sem = {}
        self.cnt = {}
        self.seen = {k: {} for k in self.engs}
        self.pre = ""
        self.rec = None
        for k in self.engs:
            self.sem[k] = es.enter_context(nc.semaphore("s_" + k))
            self.cnt[k] = 0

    def need(self, eng, src, val):
        if src == eng and src == "pe":
            return
        if self.seen[eng].get(src, 0) >= val:
            return
        self.engs[eng].wait_ge(self.sem[src], val)
        self.seen[eng][src] = val

    def _deps(self, eng, R, W):
        deps = {}
        for res in R:
            if res.w is not None:
                s, v = res.w
                deps[s] = max(deps.get(s, 0), v)
        for res in W:
            if res.w is not None:
                s, v = res.w
                deps[s] = max(deps.get(s, 0), v)
            for s, v in res.r.items():
                deps[s] = max(deps.get(s, 0), v)
        for s, v in deps.items():
            self.need(eng, s, v)

    def op(self, eng, fns, R=(), W=()):
        R = _res(R)
        W = _res(W)
        if not isinstance(fns, (list, tuple)):
            fns = [fns]
        if self.rec is not None:
            self.rec.append(("op", eng, fns, R, W))
            return
        self._deps(eng, R, W)
        e = self.engs[eng]
        ins = None
        for f in fns:
            ins = f(e)
        self.cnt[eng] += 1
        idx = self.cnt[eng]
        ins.then_inc(self.sem[eng], 1)
        for res in R:
            res.r[eng] = idx
        for res in W:
            res.w = (eng, idx)
            res.r = {}

    def dma(self, q, out, in_, R=(), W=(), slot=None):
        R = _res(R)
        W = _res(W)
        if self.rec is not None:
            self.rec.append(("dma", q, out, in_, R, W, slot))
            return
        slot = self.pre + slot
        if slot not in self.sem:
            self.sem[slot] = self.es.enter_context(self.nc.semaphore("d_" + slot))
            self.cnt[slot] = 0
        self._deps(q, R, W)
        self.engs[q].dma_start(out=out, in_=in_).then_inc(self.sem[slot], 16)
        self.cnt[slot] += 16
        v = self.cnt[slot]
        for res in R:
            res.r[slot] = v
        for res in W:
            res.w = (slot, v)
            res.r = {}

    def record(self, fn, *args):
        assert self.rec is None
        self.rec = []
        fn(*args)
        r, self.rec = self.rec, None
        return r

    def _emit(self, it):
        if it[0] == "op":
            self.op(it[1], it[2], it[3], it[4])
        else:
            self.dma(it[1], it[2], it[3], it[4], it[5], it[6])

    def replay(self, streams):
        streams = [st for st in streams if st]
        if not streams:
            return
        main, others = streams[0], streams[1:]
        quanta, cur, seen_nonpe = [], [], False
        for it in main:
            is_pe = it[0] == "op" and it[1] == "pe"
            if is_pe and seen_nonpe:
                quanta.append(cur)
                cur, seen_nonpe = [], False
            cur.append(it)
            if not is_pe:
                seen_nonpe = True
        if cur:
            quanta.append(cur)
        nq = len(quanta)
        pos = [0] * len(others)
        for qi, q in enumerate(quanta):
            for it in q:
                self._emit(it)
            for oi, st in enumerate(others):
                tgt = (len(st) * (qi + 1) + nq - 1) // nq
                while pos[oi] < min(tgt, len(st)):
                    self._emit(st[pos[oi]])
                    pos[oi] += 1
        for oi, st in enumerate(others):
            while pos[oi] < len(st):
                self._emit(st[pos[oi]])
                pos[oi] += 1

    def barrier(self):
        for eng in self.engs:
            for s, v in self.cnt.items():
                if s != eng and v > 0:
                    self.need(eng, s, v)


def _mk_dr(nc, ctx, pre):
    shared = ctx["shared"] if ctx else None

    def dr(n, s, kind="ExternalInput"):
        if shared is not None and kind == "ExternalInput" and n in ("xT", "gmix", "c_ident"):
            if n not in shared:
                shared[n] = nc.dram_tensor(n, list(s), F32, kind=kind).ap()
            return shared[n]
        return nc.dram_tensor(pre + n, list(s), F32, kind=kind).ap()
    return dr


class K:
    def __init__(self, nc, es):
        self.nc = nc
        self.es = es
        self.S = Sched(nc, es)
        self.n = 0
        self.epsb = {}

    def sb(self, shape, dt, n=1, name=None):
        self.n += 1
        return Tl(self.es.enter_context(self.nc.sbuf_tensor(name or f"t{self.n}", list(shape), dt)), n)

    def ps(self, shape, dt, n=1, name=None):
        self.n += 1
        return Tl(self.es.enter_context(self.nc.psum_tensor(name or f"p{self.n}", list(shape), dt)), n)

    def tt(self, eng, out, a, b, op, R, W):
        self.S.op(eng, lambda e: e.tensor_tensor(out=out, in0=a, in1=b, op=op), R, W)

    def ts(self, eng, out, a, s1, s2, op0, op1, R, W):
        if op1 is None:
            self.S.op(eng, lambda e: e.tensor_scalar(out=out, in0=a, scalar1=s1, scalar2=None, op0=op0), R, W)
        else:
            self.S.op(eng, lambda e: e.tensor_scalar(out=out, in0=a, scalar1=s1, scalar2=s2, op0=op0, op1=op1), R, W)

    def stt(self, out, a, s, b, op0, op1, R, W):
        self.S.op("dve", lambda e: e.scalar_tensor_tensor(out=out, in0=a, scalar=s, in1=b, op0=op0, op1=op1), R, W)

    def act(self, out, in_, func, R, W, bias=0.0, scale=1.0):
        self.S.op("act", lambda e: e.activation(out=out, in_=in_, func=func, bias=bias, scale=scale), R, W)

    def cp(self, eng, out, in_, R, W):
        if eng == "act":
            self.S.op("act", lambda e: e.activation(out=out, in_=in_, func=AF.Copy), R, W)
        else:
            self.S.op(eng, lambda e: e.tensor_copy(out=out, in_=in_), R, W)

    def mm(self, items, R, W):
        fns = []
        for (o, l, r, st, sp) in items:
            fns.append(lambda e, o=o, l=l, r=r, st=st, sp=sp: e.matmul(o, lhsT=l, rhs=r, start=st, stop=sp))
        self.S.op("pe", fns, R, W)

    def rsqrt(self, out, in_, eps, R, W):
        if eps not in self.epsb:
            t = self.sb((128, 1), F32)
            self.memset("pool", t[:], float(eps), [t])
            self.epsb[eps] = t
        eb = self.epsb[eps]
        np_ = out.shape[0]
        self.S.op("act", lambda e: e.activation(out=out, in_=in_, func=AF.Ln, bias=eb[0:np_, :], scale=1.0), list(R) + [eb], W)
        self.S.op("act", lambda e: e.activation(out=out, in_=out, func=AF.Exp, scale=-0.5), W, W)

    def memset(self, eng, ap, val, W):
        self.S.op(eng, lambda e: e.memset(ap, val), (), W)


def consts_p1():
    c = {}
    c["ident"] = np.eye(128, dtype=np.float32)
    s = np.arange(64)[:, None]
    t = np.arange(64)[None, :]
    U = (t > s).astype(np.float32)
    Ui = (t >= s).astype(np.float32)
    L = (s > t).astype(np.float32)
    c["mask6"] = np.concatenate([L, L, U, Ui, U, Ui], axis=1).astype(np.float32)
    bd = np.zeros((128, 128), np.float32)
    bd[:64, :64] = 1
    bd[64:, 64:] = 1
    c["bd"] = bd
    sh = np.zeros((128, 64), np.float32)
    sh[64 + np.arange(64), np.arange(64)] = 1
    c["shdn"] = sh
    ilow = np.zeros((64, 128), np.float32)
    ilow[np.arange(64), np.arange(64)] = 1
    iup = np.zeros((64, 128), np.float32)
    iup[np.arange(64), 64 + np.arange(64)] = 1
    c["ilu"] = np.concatenate([ilow, iup], axis=1)
    id2 = np.zeros((128, 64), np.float32)
    id2[np.arange(128), np.arange(128) % 64] = 1
    c["id2"] = id2
    rm = np.ones((128, TT), np.float32)
    rm[:, ::CH] = 0
    c["rmask"] = rm
    return c


CONST_SHAPES_P1 = {"ident": (128, 128), "mask6": (64, 384), "bd": (128, 128), "shdn": (128, 64),
                   "ilu": (64, 256), "id2": (128, 64), "rmask": (128, TT)}

PV_MU = 0
PV_W0 = 10
PV_A0 = 12
PV_KK = 14
PV_KA = 16
PV_RK = 18
PV_GW = 20
PV_GB = 22
NPV = 24

RW_CH = [(0, 128), (128, 128), (256, 128), (384, 128), (512, 128), (640, 128),
         (768, 64), (832, 64), (896, 128), (1024, 32)]
NRW = 1056


def build_p1a(seq, ctx=None):
    nt = seq // TT
    fused = ctx is not None
    nc = ctx["nc"] if fused else bass.Bass("TRN2", target_bir_lowering=False)
    dr = _mk_dr(nc, ctx, "a_" if fused else "")
    xT = dr("xT", (D, seq))
    w1 = dr("w1", (D, NRW))
    gmix = dr("gmix", (128, 8))
    pvec = dr("pvec", (128, NPV))
    w2s = dr("w2s", (64, 256))
    a2s = dr("a2s", (64, 256))
    g2s = dr("g2s", (160, 256))
    cd = {k: dr("c_" + k, v) for k, v in CONST_SHAPES_P1.items()}
    yr = None if fused else dr("yr", (256, seq), kind="ExternalOutput")
    prw = dr("prw", (256, D)) if fused else None
    dbgf = dr("dbgf", (128, 40 * TT), kind="ExternalOutput") if DBG else None
    dbgb = nc.dram_tensor("dbgb", [128, 40 * TT], BF16, kind="ExternalOutput").ap() if DBG else None
    dbc = {"f": 0, "b": 0}

    def dump(S, tile, ap, kind, name, rows=128, cols=TT):
        i = dbc[kind]
        dbc[kind] += (cols + TT - 1) // TT
        dst = (dbgf if kind == "f" else dbgb)[0:rows, i * TT:i * TT + cols]
        S.dma("sp", dst, ap, R=[tile], slot=f"dbg{kind}{i}")

    with ExitStack() as es:
        if fused:
            k = ctx["k"]
            k.es = es
            k.epsb = {}
        else:
            k = K(nc, es)
        S = k.S
        if fused:
            PRWb = k.sb((128, 2, D), BF16)
            YOb = [k.sb((128, TT), BF16) for _ in range(2)]
            PRD = k.sb((128, 8, TT), BF16)
        def load_const(name, shape, to_bf16=True, q="sp"):
            f = k.sb(shape, F32)
            S.dma(q, f[:], cd[name], W=[f], slot="c_" + name)
            if not to_bf16:
                return f
            b = k.sb(shape, BF16)
            k.cp("pool", b[:], f[:], [f], [b])
            return b
        ident = load_const("ident", (128, 128))
        mask6 = load_const("mask6", (64, 384), to_bf16=False)
        bd = load_const("bd", (128, 128))
        shdn = load_const("shdn", (128, 64))
        ilu = load_const("ilu", (64, 256))
        id2f = load_const("id2", (128, 64), to_bf16=False)
        id2 = k.sb((128, 64), BF16)
        k.cp("pool", id2[:], id2f[:], [id2f], [id2])
        rmask = load_const("rmask", (128, TT), to_bf16=False)
        bd64 = k.sb((128, 128), BF16)
        k.ts("pool", bd64[:], bd[:], 1.0 / 64, None, ALU.mult, None, [bd], [bd64])
        onesm = k.sb((128, 128), BF16)
        k.memset("pool", onesm[:], 1.0 / 1024, [onesm])
        pv = k.sb((128, NPV), F32)
        S.dma("sp", pv[:], pvec, W=[pv], slot="pv")
        gm = k.sb((128, 8), F32)
        S.dma("sp", gm[:], gmix, W=[gm], slot="gm")
        def load_bf(ap, shape, q="sp", name="l"):
            f = k.sb(shape, F32)
            S.dma(q, f[:], ap, W=[f], slot=name)
            b = k.sb(shape, BF16)
            k.cp("pool", b[:], f[:], [f], [b])
            return b
        w2b = load_bf(w2s, (64, 256), name="w2")
        a2b = load_bf(a2s, (64, 256), name="a2")
        g2b0 = load_bf(g2s[0:128, :], (128, 256), name="g20")
        g2b1 = load_bf(g2s[128:160, :], (32, 256), name="g21")
        WB = k.sb((128, 8, NRW), BF16, n=8)
        wst = [k.sb((128, NRW), F32)]
        for kc in range(8):
            st = wst[0]
            S.dma("sp" if kc % 2 == 0 else "act", st[:], w1[kc * 128:(kc + 1) * 128, :], W=[st], slot="w0")
            k.ts("dve", WB[:, kc, :], st[:], gm[:, kc:kc + 1], None, ALU.mult, None, [st, gm], [(WB, kc)])
        if fused:
            for h_ in range(2):
                S.dma("sp", wst[0][:, 0:D], prw[h_ * 128:(h_ + 1) * 128, :], W=[wst[0]], slot="w0")
                k.cp("pool", PRWb[:, h_, :], wst[0][:, 0:D], [wst[0]], [PRWb])

        XT32 = [k.sb((128, 8, TT), F32) for _ in range(2)]
        XB = [k.sb((128, 8, TT), BF16) for _ in range(2)]
        XSQs = [k.sb((128, 8, TT), BF16) for _ in range(2)]
        RSTD = k.sb((128, TT), F32)
        PROJ = [k.sb((128, 10, TT + 1), F32, n=11) for _ in range(2)]
        for p in PROJ:
            k.memset("pool", p[:], 0.0, [p])
        PP = k.sb((128, 10, TT), F32, n=2)
        X6 = k.sb((64, 2, TT), BF16)
        SG = k.sb((128, TT), BF16)
        SG8 = k.sb((32, TT), BF16)
        f32t = lambda: k.sb((128, TT), F32)
        bft = lambda: k.sb((128, TT), BF16)
        base = dict()
        for nm in ["LD", "AS", "KK", "RN", "KN", "T1", "KM", "B", "CS", "WI", "WV", "E1", "WE", "E2", "WH"]:
            base[nm] = f32t()
        for nm in ["KK2", "RKB", "YB", "YSQ"]:
            base[nm] = bft()
        for nm in ["MS", "NEG", "VAR", "RS", "YC", "YN", "YG", "YO"]:
            base[nm] = f32t()
        tmp = []
        for i in range(4):
            d = dict(base)
            if i < 2:
                d["FM"] = k.sb((128, 5, TT), BF16, n=5)
                d["TM"] = k.sb((128, 3, TT), BF16, n=3)
                d["FMo"] = k.sb((64, 5, TT), BF16)
            else:
                d["FM"], d["TM"], d["FMo"] = tmp[i - 2]["FM"], tmp[i - 2]["TM"], tmp[i - 2]["FMo"]
            d["GT"] = f32t()
            d["BON"] = f32t()
            tmp.append(d)
        hd = [dict() for _ in range(4)]
        for h in range(4):
            d = hd[h]
            d["S1"] = k.sb((64, NCH, 576), BF16, n=3)
            d["VT"] = k.sb((64, NCH, 64), BF16)
            d["XT"] = [k.sb((64, NCH, 192), BF16) for _ in range(2)]
            d["Q3"] = k.sb((64, NCH, 128), BF16)
            d["RM"] = k.sb((64, NCH, 128), BF16)
            d["AG"] = k.sb((64, NCH, 128), BF16)
            d["ST"] = k.sb((64, 64), BF16)
            k.memset("pool", d["ST"][:], 0.0, [d["ST"]])
            d["YS"] = k.sb((64, TT), BF16)
        PB0 = k.ps((128, 512), F32)
        PB1 = k.ps((128, 512), F32)
        PB3 = k.ps((128, 512), F32)
        PT = k.ps((64, 2, 4, 128), BF16)
        SC = k.ps((64, 2048), F32, n=4)
        PJ = [(PB0, 0)]
        PM = [(PB1, 0), (PB3, 0)]
        pmc = [0]

        def pm():
            return PM[pmc[0]]

        def pap(slot, rows=128, cols=TT):
            t, i = slot
            return t[0:rows, 0:cols]

        pjc = [0]

        def emitT(ti):
            t0 = ti * TT
            par = ti % 2
            xt = XT32[par]
            xb = XB[par]
            pj = PROJ[par]
            pjn = PROJ[1 - par]
            def x_dma(tj):
                pj_ = tj % 2
                S.dma("sp", XT32[pj_][:], xT.rearrange("(kc p) t -> p kc t", p=128)[:, :, tj * TT:(tj + 1) * TT], W=[XT32[pj_]], slot=f"x{pj_}")

            def x_prep(tj):
                pj_ = tj % 2
                k.cp("dve", XB[pj_][:], XT32[pj_][:], [XT32[pj_]], [XB[pj_]])
                k.act(XSQs[pj_][:], XT32[pj_][:], AF.Square, [XT32[pj_]], [XSQs[pj_]])
            if ti == 0:
                x_dma(0)
                x_prep(0)
            if ti + 1 < nt:
                x_dma(ti + 1)
            XSQ = XSQs[par]
            slot = pm()
            k.mm([(pap(slot), onesm[:], XSQ[:, kc, :], kc == 0, kc == 7) for kc in range(8)], [onesm, XSQ], [slot])
            k.rsqrt(RSTD[:], pap(slot), NORM_EPS, [slot], [RSTD])
            def shift(c0, c1, pres):
                grp = [(pj, i) for i in range(c0, c1)] + [(pj, 10)]
                k.tt("dve", PP[:, c0:c1, :], pj[:, c0:c1, 0:TT], pj[:, c0:c1, 1:TT + 1], ALU.subtract, grp, [pres])
                k.tt("dve", PP[:, c0:c1, :], PP[:, c0:c1, :],
                     pv[:, PV_MU + c0:PV_MU + c1].unsqueeze(2).broadcast_to([128, c1 - c0, TT]), ALU.mult, [pres, pv], [pres])
                k.tt("dve", PP[:, c0:c1, :], PP[:, c0:c1, :], pj[:, c0:c1, 1:TT + 1], ALU.add, [pres] + grp, [pres])
            for n_, ci in enumerate([6, 7, 8, 9, 0, 1, 2, 3, 4, 5]):
                co, M = RW_CH[ci]
                slot = PJ[0]
                k.mm([(pap(slot, M), WB[:, kc, co:co + M], xb[:, kc, :], kc == 0, kc == 7) for kc in range(8)],
                     [WB, xb], [slot])
                k.tt("dve", pj[0:M, ci, 1:TT + 1], pap(slot, M), RSTD[0:M, :], ALU.mult, [slot, RSTD], [(pj, ci)])
                if n_ == 3:
                    shift(6, 10, (PP, 1))
                    k.act(X6[:, 0, :], PP[0:64, 6, :], AF.Tanh, [(PP, 1)], [X6])
                    k.cp("pool", X6[:, 1, :], PP[0:64, 7, :], [(PP, 1)], [X6])
                    k.act(SG[:], PP[:, 8, :], AF.Sigmoid, [(PP, 1)], [SG])
                    k.act(SG8[:], PP[0:32, 9, :], AF.Sigmoid, [(PP, 1)], [SG8])
            shift(0, 6, (PP, 0))
            allpj = [(pj, i) for i in range(11)]
            k.cp("pool", pjn[:, :, 0:1], pj[:, :, TT:TT + 1], allpj, [(pjn, 10)])
            if ti + 1 < nt:
                x_prep(ti + 1)
        def emitA(ti, hp, u):
            d = tmp[u % 4]
            hs = slice(hp * 128, (hp + 1) * 128)
            pc = lambda c: pv[:, c + hp:c + hp + 1]
            PR, PK, PVv = PP[:, 0 + hp, :], PP[:, 2 + hp, :], PP[:, 4 + hp, :]
            FM, TM = d["FM"], d["TM"]
            heads = [hp * 2, hp * 2 + 1]
            t0 = ti * TT
            hs = slice(hp * 128, (hp + 1) * 128)
            pc = lambda c: pv[:, c + hp:c + hp + 1]
            PR, PK, PVv = PP[:, 0 + hp, :], PP[:, 2 + hp, :], PP[:, 4 + hp, :]
            FM, TM = d["FM"], d["TM"]
            s1 = pm()
            k.mm([(pap(s1), w2b[:, hs], X6[:, 0, :], True, True)], [w2b, X6], [s1])
            k.act(d["LD"][:], pap(s1), AF.Sigmoid, [s1, pv], [d["LD"]], bias=pc(PV_W0))
            s2 = pm()
            k.mm([(pap(s2), a2b[:, hs], X6[:, 1, :], True, True)], [a2b, X6], [s2])
            k.act(d["AS"][:], pap(s2), AF.Sigmoid, [s2, pv], [d["AS"]], bias=pc(PV_A0))
            s3 = pm()
            k.mm([(pap(s3), g2b0[:, hs], SG[:], True, False), (pap(s3), g2b1[:, hs], SG8[:], False, True)],
                 [g2b0, g2b1, SG, SG8], [s3])
            k.cp("act", d["GT"][:], pap(s3), [s3], [d["GT"]])
            S.op("dve", lambda e, d=d: e.tensor_tensor_scan(out=d["CS"][:], data0=rmask[:], data1=d["LD"][:],
                                                             initial=0.0, op0=ALU.mult, op1=ALU.add),
                 _res([rmask, d["LD"]]), _res([d["CS"]]))
            k.act(d["WI"][:], d["CS"][:], AF.Exp, [d["CS"]], [d["WI"]], scale=-CDEC)
            k.act(d["WV"][:], d["CS"][:], AF.Exp, [d["CS"]], [d["WV"]], scale=CDEC)
            k.tt("dve", d["E1"][:], d["CS"][:], d["LD"][:], ALU.subtract, [d["CS"], d["LD"]], [d["E1"]])
            k.act(d["WE"][:], d["E1"][:], AF.Exp, [d["E1"]], [d["WE"]], scale=-CDEC)
            cs3 = d["CS"][:].rearrange("p (c t) -> p c t", t=CH)
            k.tt("pool", d["E2"][:].rearrange("p (c t) -> p c t", t=CH),
                 cs3[:, :, CH - 1:CH].broadcast_to([128, NCH, CH]), cs3, ALU.subtract, [d["CS"]], [d["E2"]])
            k.act(d["WH"][:], d["E2"][:], AF.Exp, [d["E2"]], [d["WH"]], scale=-CDEC)
            wi3 = d["WI"][:].rearrange("p (c t) -> p c t", t=CH)
            k.tt("dve", FM[:, 4, :].rearrange("p (c t) -> p c t", t=CH),
                 id2f[:].unsqueeze(1).broadcast_to([128, NCH, CH]),
                 wi3[:, :, CH - 1:CH].broadcast_to([128, NCH, CH]), ALU.mult, [id2f, d["WI"]], [(FM, 4)])
            k.ts("dve", d["KK"][:], PK, pc(PV_KK), None, ALU.mult, None, [(PP, 0), pv], [d["KK"]])
            k.tt("dve", d["KK2"][:], d["KK"][:], d["KK"][:], ALU.mult, [d["KK"]], [d["KK2"]])
            k.ts("dve", d["T1"][:], d["AS"][:], -1.0, pc(PV_KA), ALU.add, ALU.mult, [d["AS"], pv], [d["T1"]])
            k.stt(d["KM"][:], d["T1"][:], 1.0, PK, ALU.add, ALU.mult, [d["T1"], (PP, 0)], [d["KM"]])
            s4 = pm()
            k.mm([(pap(s4), bd[:], d["KK2"][:], True, True)], [bd, d["KK2"]], [s4])
            k.ts("dve", d["RN"][:], pap(s4), 1e-24, None, ALU.max, None, [s4], [d["RN"]])
            k.stt(d["RKB"][:], PR, pc(PV_RK), d["KM"][:], ALU.mult, ALU.mult, [(PP, 0), pv, d["KM"]], [d["RKB"]])
            k.tt("pool", FM[:, 3, :], PR, d["WI"][:], ALU.mult, [(PP, 0), d["WI"]], [(FM, 3)])
            k.cp("pool", TM[:, 2, :], PVv, [(PP, 0)], [(TM, 2)])
            k.tt("dve", FM[:, 0, :], d["KM"][:], d["WV"][:], ALU.mult, [d["KM"], d["WV"]], [(FM, 0)])
            k.tt("dve", TM[:, 1, :], d["KM"][:], d["WH"][:], ALU.mult, [d["KM"], d["WH"]], [(TM, 1)])
            s5 = pm()
            k.mm([(pap(s5), bd[:], d["RKB"][:], True, True)], [bd, d["RKB"]], [s5])
            k.tt("dve", d["BON"][:], pap(s5), PVv, ALU.mult, [s5, (PP, 0)], [d["BON"]])
            k.act(d["RN"][:], d["RN"][:], AF.Ln, [d["RN"]], [d["RN"]])
            k.act(d["RN"][:], d["RN"][:], AF.Exp, [d["RN"]], [d["RN"]], scale=-0.5)
            k.tt("dve", d["KN"][:], d["KK"][:], d["RN"][:], ALU.mult, [d["KK"], d["RN"]], [d["KN"]])
            k.tt("pool", d["B"][:], d["KN"][:], d["AS"][:], ALU.mult, [d["KN"], d["AS"]], [d["B"]])
            k.stt(FM[:, 2, :], d["KN"][:], -1.0, d["WE"][:], ALU.mult, ALU.mult, [d["KN"], d["WE"]], [(FM, 2)])
            k.tt("pool", FM[:, 1, :], d["B"][:], d["WV"][:], ALU.mult, [d["B"], d["WV"]], [(FM, 1)])
            k.tt("dve", TM[:, 0, :], d["B"][:], d["WH"][:], ALU.mult, [d["B"], d["WH"]], [(TM, 0)])
            if DBG and ti == 0 and hp == 0:
                for ci in range(10):
                    dump(S, PP, PP[:, ci, :], "f", f"PP{ci}")
                for nm in ["LD", "AS", "GT", "KN", "KM", "B", "CS", "WI", "WE", "WH", "BON"]:
                    dump(S, d[nm], d[nm][:], "f", nm)
                for q in range(5):
                    dump(S, FM, FM[:, q, :], "b", f"FM{q}")
                for q in range(3):
                    dump(S, TM, TM[:, q, :], "b", f"TM{q}")
            S.dma("act", d["FMo"][:], FM[64:128, :, :], R=[FM], W=[d["FMo"]], slot=f"fmo{u % 2}")
            srcs = [FM[:, 2, :], TM[:, 0, :], TM[:, 1, :], TM[:, 2, :]]
            for half in range(2):
                fns = []
                for cc in range(2):
                    c = half * 2 + cc
                    for q in range(4):
                        fns.append(lambda e, cc=cc, q=q, c=c: e.transpose(
                            out=PT[:, cc, q, :], in_=srcs[q][:, c * CH:(c + 1) * CH], identity=ident[:]))
                S.op("pe", fns, _res([FM, TM, ident]), _res([PT]))
                for e_ in range(2):
                    h = hp * 2 + e_
                    S1, VT = hd[h]["S1"], hd[h]["VT"]
                    cs_ = slice(half * 2, half * 2 + 2)
                    es_ = slice(e_ * 64, e_ * 64 + 64)
                    eng = "dve"
                    k.cp(eng, S1[:, cs_, 0:64], PT[:, :, 0, es_], [PT], [(S1, 1)])
                    k.cp(eng, S1[:, cs_, 320:384], PT[:, :, 1, es_], [PT], [(S1, 1)])
                    k.cp(eng, S1[:, cs_, 512:576], PT[:, :, 2, es_], [PT], [(S1, 1)])
                    k.cp(eng, VT[:, cs_, :], PT[:, :, 3, es_], [PT], [VT])
        def emitTA(ti, hp, u):
            pmc[0] = 0
            if hp == 0:
                emitT(ti)
            emitA(ti, hp, u)
        def emitB(ti, hp, u):
            d = tmp[u % 4]
            hs = slice(hp * 128, (hp + 1) * 128)
            pc = lambda c: pv[:, c + hp:c + hp + 1]
            PR, PK, PVv = PP[:, 0 + hp, :], PP[:, 2 + hp, :], PP[:, 4 + hp, :]
            FM, TM = d["FM"], d["TM"]
            heads = [hp * 2, hp * 2 + 1]
            t0 = ti * TT
            heads = [hp * 2, hp * 2 + 1]
            fmh = [FM[0:64], d["FMo"]]
            fmr = [[(FM, q) for q in range(5)], [d["FMo"]]]
            reg = [0, 1024]
            regres = [[(SC, 0), (SC, 1)], [(SC, 2), (SC, 3)]]
            for half in range(2):
                for e_ in range(2):
                    h = heads[e_]
                    f = fmh[e_]
                    items = []
                    for cc in range(2):
                        c = half * 2 + cc
                        tsl = slice(c * CH, (c + 1) * CH)
                        base = reg[e_] + cc * 384
                        items.append((SC[:, base:base + 128], f[:, 2, tsl], f[:, 0:2, tsl], True, True))
                        items.append((SC[:, base + 128:base + 256], f[:, 1, tsl], f[:, 2:4, tsl], True, True))
                        items.append((SC[:, base + 256:base + 384], f[:, 0, tsl], f[:, 2:4, tsl], True, True))
                    k.mm(items, fmr[e_], regres[e_])
                for e_ in range(2):
                    h = heads[e_]
                    S1 = hd[h]["S1"]
                    cs_ = slice(half * 2, half * 2 + 2)
                    src = SC[:, reg[e_]:reg[e_] + 768].rearrange("p (c x) -> p c x", x=384)
                    m6 = mask6[:].unsqueeze(1).broadcast_to([64, 2, 384])
                    k.tt("dve", S1[:, cs_, 64:320], src[:, :, 0:256], m6[:, :, 0:256], ALU.mult,
                         regres[e_] + [mask6], [(S1, 0)])
                    k.tt("dve", S1[:, cs_, 384:512], src[:, :, 256:384], m6[:, :, 256:384], ALU.mult,
                         regres[e_] + [mask6], [(S1, 0)])
            for lvl in range(6):
                for e_ in range(2):
                    h = heads[e_]
                    S1 = hd[h]["S1"]
                    cur = hd[h]["XT"][lvl % 2]
                    i64 = ident[0:64, 0:64]
                    items = []
                    for c in range(NCH):
                        base = reg[e_] + c * 192
                        if lvl == 0:
                            NTm, Nm = S1[:, c, 128:192], S1[:, c, 192:256]
                            items.append((SC[:, base:base + 64], NTm, Nm, True, True))
                            items.append((SC[:, base + 128:base + 192], Nm, NTm, True, True))
                        elif lvl < 5:
                            items.append((SC[:, base:base + 128], cur[:, c, 128:192], cur[:, c, 0:128], True, False))
                            items.append((SC[:, base + 64:base + 128], i64, cur[:, c, 64:128], False, True))
                            items.append((SC[:, base + 128:base + 192], cur[:, c, 0:64], cur[:, c, 128:192], True, True))
                        else:
                            items.append((SC[:, base + 64:base + 128], cur[:, c, 128:192], cur[:, c, 64:128], True, False))
                            items.append((SC[:, base + 64:base + 128], i64, cur[:, c, 64:128], False, True))
                    k.mm(items, [(S1, 0)] if lvl == 0 else [cur, ident], regres[e_])
                for e_ in range(2):
                    h = heads[e_]
                    S1 = hd[h]["S1"]
                    nxt = hd[h]["XT"][(lvl + 1) % 2]
                    src = SC[:, reg[e_]:reg[e_] + 768].rearrange("p (c x) -> p c x", x=192)
                    eng = "act" if e_ == 0 else "dve"
                    if lvl == 0:
                        k.cp(eng, nxt[:, :, 0:64], src[:, :, 0:64], regres[e_], [nxt])
                        k.cp(eng, nxt[:, :, 128:192], src[:, :, 128:192], regres[e_], [nxt])
                        k.tt("pool", nxt[:, :, 64:128], S1[:, :, 192:256],
                             ident[0:64, 0:64].unsqueeze(1).broadcast_to([64, NCH, 64]), ALU.add,
                             [(S1, 0), ident], [nxt])
                    elif lvl < 5:
                        k.cp(eng, nxt[:, :, :], src[:, :, :], regres[e_], [nxt])
                    else:
                        k.cp(eng, nxt[:, :, 64:128], src[:, :, 64:128], regres[e_], [nxt])
            for e_ in range(2):
                h = heads[e_]
                S1 = hd[h]["S1"]
                Tm = hd[h]["XT"][0]
                items = [(SC[:, reg[e_] + c * 128:reg[e_] + (c + 1) * 128], Tm[:, c, 64:128], S1[:, c, 0:128], True, True)
                         for c in range(NCH)]
                k.mm(items, [Tm, S1], regres[e_])
            for e_ in range(2):
                h = heads[e_]
                k.cp("act" if e_ == 0 else "dve", hd[h]["Q3"][:],
                     SC[:, reg[e_]:reg[e_] + 512].rearrange("p (c x) -> p c x", x=128), regres[e_], [hd[h]["Q3"]])
            for e_ in range(2):
                h = heads[e_]
                S1, Q3, f = hd[h]["S1"], hd[h]["Q3"], fmh[e_]
                items = []
                for c in range(NCH):
                    o = SC[:, reg[e_] + c * 128:reg[e_] + (c + 1) * 128]
                    tsl = slice(c * CH, (c + 1) * CH)
                    items.append((o, Q3[:, c, 0:64], S1[:, c, 256:384], True, False))
                    items.append((o, ident[0:64, 0:64], f[:, 3:5, tsl], False, True))
                k.mm(items, [Q3, S1, ident] + fmr[e_], regres[e_])
            for e_ in range(2):
                h = heads[e_]
                k.cp("act" if e_ == 0 else "dve", hd[h]["RM"][:],
                     SC[:, reg[e_]:reg[e_] + 512].rearrange("p (c x) -> p c x", x=128), regres[e_], [hd[h]["RM"]])
            for e_ in range(2):
                h = heads[e_]
                S1, Q3 = hd[h]["S1"], hd[h]["Q3"]
                items = []
                for c in range(NCH):
                    o = SC[:, reg[e_] + c * 128:reg[e_] + (c + 1) * 128]
                    items.append((o, Q3[:, c, 64:128], S1[:, c, 256:384], True, False))
                    items.append((o, ident[0:64, 0:64], S1[:, c, 448:576], False, True))
                k.mm(items, [Q3, S1, ident], regres[e_])
            for e_ in range(2):
                h = heads[e_]
                k.cp("act" if e_ == 0 else "dve", hd[h]["AG"][:],
                     SC[:, reg[e_]:reg[e_] + 512].rearrange("p (c x) -> p c x", x=128), regres[e_], [hd[h]["AG"]])
            for c in range(NCH):
                for e_ in range(2):
                    h = heads[e_]
                    RM, AG, VT, ST = hd[h]["RM"], hd[h]["AG"], hd[h]["VT"], hd[h]["ST"]
                    yo = SC[:, reg[e_] + 768 + c * CH:reg[e_] + 768 + (c + 1) * CH]
                    so = SC[:, reg[e_] + 512:reg[e_] + 576]
                    yres = (SC, 1) if e_ == 0 else (SC, 3)
                    k.mm([(yo, ST[:], RM[:, c, 0:64], True, False), (yo, VT[:, c, :], AG[:, c, 0:64], False, True)],
                         [ST, RM, VT, AG], [yres])
                    k.mm([(so, RM[:, c, 64:128], ST[:], True, False), (so, AG[:, c, 64:128], VT[:, c, :], False, True)],
                         [ST, RM, VT, AG], [yres])
                    k.cp("act" if e_ == 0 else "dve", ST[:], so, [yres], [ST])
            for e_ in range(2):
                h = heads[e_]
                yres = (SC, 1) if e_ == 0 else (SC, 3)
                k.cp("act" if e_ == 0 else "dve", hd[h]["YS"][:], SC[:, reg[e_] + 768:reg[e_] + 1024], [yres], [hd[h]["YS"]])
            if DBG and ti == 0 and hp == 0:
                h0 = hd[0]
                dump(S, h0["S1"], h0["S1"][:, 0, :], "b", "S1c0", rows=64, cols=576)
                dump(S, h0["XT"][0], h0["XT"][0][:, 0, :], "b", "XTc0", rows=64, cols=192)
                dump(S, h0["Q3"], h0["Q3"][:, 0, :], "b", "Q3c0", rows=64, cols=128)
                dump(S, h0["RM"], h0["RM"][:, 0, :], "b", "RMc0", rows=64, cols=128)
                dump(S, h0["AG"], h0["AG"][:, 0, :], "b", "AGc0", rows=64, cols=128)
                dump(S, h0["VT"], h0["VT"][:, 0, :], "b", "VTc0", rows=64, cols=64)
                dump(S, h0["YS"], h0["YS"][:], "b", "YS0", rows=64)
                dump(S, hd[1]["YS"], hd[1]["YS"][:], "b", "YS1", rows=64)
                dump(S, d["FMo"], d["FMo"][:, 0, :], "b", "FMo0", rows=64)
        def emitC(ti, hp, u):
            pmc[0] = 1
            d = tmp[u % 4]
            hs = slice(hp * 128, (hp + 1) * 128)
            pc = lambda c: pv[:, c + hp:c + hp + 1]
            PR, PK, PVv = PP[:, 0 + hp, :], PP[:, 2 + hp, :], PP[:, 4 + hp, :]
            FM, TM = d["FM"], d["TM"]
            heads = [hp * 2, hp * 2 + 1]
            t0 = ti * TT
            sy = pm()
            k.mm([(pap(sy), ilu[:, 0:128], hd[heads[0]]["YS"][:], True, False),
                  (pap(sy), ilu[:, 128:256], hd[heads[1]]["YS"][:], False, True)],
                 [ilu, hd[heads[0]]["YS"], hd[heads[1]]["YS"]], [sy])
            k.cp("act", d["YC"][:], pap(sy), [sy], [d["YC"]])
            k.cp("dve", d["YB"][:], d["YC"][:], [d["YC"]], [d["YB"]])
            k.tt("dve", d["YSQ"][:], d["YC"][:], d["YC"][:], ALU.mult, [d["YC"]], [d["YSQ"]])
            sm = pm()
            k.mm([(pap(sm), bd64[:], d["YB"][:], True, True)], [bd64, d["YB"]], [sm])
            k.cp("act", d["MS"][:], pap(sm), [sm], [d["MS"]])
            sq = pm()
            k.mm([(pap(sq), bd64[:], d["YSQ"][:], True, True)], [bd64, d["YSQ"]], [sq])
            k.tt("pool", d["NEG"][:], d["MS"][:], d["MS"][:], ALU.mult, [d["MS"]], [d["NEG"]])
            k.tt("dve", d["VAR"][:], pap(sq), d["NEG"][:], ALU.subtract, [sq, d["NEG"]], [d["VAR"]])
            k.rsqrt(d["RS"][:], d["VAR"][:], GN_EPS, [d["VAR"]], [d["RS"]])
            k.tt("dve", d["YC"][:], d["YC"][:], d["MS"][:], ALU.subtract, [d["YC"], d["MS"]], [d["YC"]])
            k.tt("pool", d["YN"][:], d["YC"][:], d["RS"][:], ALU.mult, [d["YC"], d["RS"]], [d["YN"]])
            k.ts("dve", d["YG"][:], d["YN"][:], pc(PV_GW), pc(PV_GB), ALU.mult, ALU.add, [d["YN"], pv], [d["YG"]])
            k.tt("pool", d["YG"][:], d["YG"][:], d["BON"][:], ALU.add, [d["YG"], d["BON"]], [d["YG"]])
            k.tt("pool", d["YO"][:], d["YG"][:], d["GT"][:], ALU.mult, [d["YG"], d["GT"]], [d["YO"]])
            if not fused:
                S.dma("sp", yr[hp * 128:(hp + 1) * 128, t0:t0 + TT], d["YO"][:], R=[d["YO"]], slot=f"yo{hp}")
            else:
                k.cp("pool", YOb[hp][:], d["YO"][:], [d["YO"]], [YOb[hp]])
            if hp == 1:
                emitP(ti)
        def emitP(ti):
            t0 = ti * TT
            if fused:
                ntok = ctx["ntok"]
                sh, col = t0 // ntok, t0 % ntok
                for oc in range(8):
                    so = pm()
                    k.mm([(pap(so), PRWb[:, hp_, oc * 128:(oc + 1) * 128], YOb[hp_][:], hp_ == 0, hp_ == 1) for hp_ in range(2)],
                         [PRWb, YOb[0], YOb[1]], [so])
                    k.cp("act", PRD[:, oc, :], pap(so), [so], [PRD])
                S.dma("sp", ctx["src_r"][sh * 1024:sh * 1024 + 1024, col:col + TT].rearrange("(oc p) t -> p oc t", p=128),
                      PRD[:], R=[PRD], slot="prd")
        units = [(ti, hp) for ti in range(nt) for hp in range(2)]
        NU = len(units)
        S.replay([S.record(emitTA, units[0][0], units[0][1], 0)])
        for n in range(NU + 1):
            streams = []
            if n < NU:
                streams.append(S.record(emitB, units[n][0], units[n][1], n))
            if n + 1 < NU:
                streams.append(S.record(emitTA, units[n + 1][0], units[n + 1][1], n + 1))
            if n >= 1:
                streams.append(S.record(emitC, units[n - 1][0], units[n - 1][1], n - 1))
            S.replay(streams)
        S.barrier()
    return nc


def p1a_inputs(x_b, j, norm_mix_g, w_in, shift_mu, w0, w2, a0, a2, g2, k_k, k_a, r_k, gn_w, gn_b):
    cs = slice(256 * j, 256 * j + 256)
    cols = np.concatenate([np.arange(256) + 256 * j, 1024 + np.arange(256) + 256 * j, 2048 + np.arange(256) + 256 * j,
                           np.arange(3072, 3360)])
    w1 = np.ascontiguousarray(w_in[:, cols])
    mu = shift_mu[cols]
    pvec = np.zeros((128, NPV), np.float32)
    for ci, (co, M) in enumerate(RW_CH):
        pvec[:M, PV_MU + ci] = mu[co:co + M]
    def two(v):
        return np.ascontiguousarray(v[cs].reshape(2, 128).T)
    pvec[:, PV_W0:PV_W0 + 2] = two(w0)
    pvec[:, PV_A0:PV_A0 + 2] = two(a0)
    pvec[:, PV_KK:PV_KK + 2] = two(k_k)
    pvec[:, PV_KA:PV_KA + 2] = two(k_a)
    pvec[:, PV_RK:PV_RK + 2] = two(r_k.reshape(-1))
    pvec[:, PV_GW:PV_GW + 2] = two(gn_w)
    pvec[:, PV_GB:PV_GB + 2] = two(gn_b)
    m = {"xT": np.ascontiguousarray(x_b.T), "w1": w1,
         "gmix": np.ascontiguousarray(norm_mix_g.reshape(8, 128).T), "pvec": pvec,
         "w2s": np.ascontiguousarray(w2[:, cs]), "a2s": np.ascontiguousarray(a2[:, cs]),
         "g2s": np.ascontiguousarray(g2[:, cs])}
    for kk, v in consts_p1().items():
        m["c_" + kk] = v
    return m


DILS = (1, 4, 16)


def attn_bias(j):
    out = np.zeros((128, 3, 2, 128), np.float32)
    kk = np.arange(128)[:, None].astype(np.float32)
    q = np.arange(128)[None, :].astype(np.float32)
    for g, d in enumerate(DILS):
        h = 4 * g + j
        slope = np.float32(2.0) ** np.float32(-8.0 * (h + 1.0) / 12.0)
        sp = q + 128 - kk
        out[:, g, 0, :] = np.where(sp <= 128, -slope * (sp * d), -30000.0)
        sc = q - kk
        out[:, g, 1, :] = np.where(sc >= 0, -slope * (sc * d), -30000.0)
    return out


def build_p1b(seq, ctx=None):
    nt = seq // TT
    fused = ctx is not None
    nc = ctx["nc"] if fused else bass.Bass("TRN2", target_bir_lowering=False)
    dr = _mk_dr(nc, ctx, "b_" if fused else "")
    xT = dr("xT", (D, seq))
    w1a = dr("w1a", (D, 576))
    gmix = dr("gmix", (128, 8))
    identd = dr("c_ident", (128, 128))
    biasd = dr("abias", (128, 3 * 2 * 128))
    seld = dr("sel", (65, 64))
    ya = None if fused else dr("ya", (64, seq), kind="ExternalOutput")
    pat = dr("pat", (64, D)) if fused else None
    with ExitStack() as es:
        if fused:
            k = ctx["k"]
            k.es = es
            k.epsb = {}
        else:
            k = K(nc, es)
        S = k.S
        if fused:
            PATb = k.sb((64, D), BF16)
            YAb = k.sb((64, 512), BF16)
            PRA = k.sb((128, 4, 512), BF16)
        idf = k.sb((128, 128), F32)
        S.dma("sp", idf[:], identd, W=[idf], slot="id")
        ident = k.sb((128, 128), BF16)
        k.cp("pool", ident[:], idf[:], [idf], [ident])
        bias = k.sb((128, 3, 2, 128), F32)
        S.dma("sp", bias[:].rearrange("p a b c -> p (a b c)"), biasd, W=[bias], slot="bias")
        sel = k.sb((65, 64), F32)
        S.dma("sp", sel[:], seld, W=[sel], slot="sel")
        gm = k.sb((128, 8), F32)
        S.dma("sp", gm[:], gmix, W=[gm], slot="gm")
        onesm = k.sb((128, 128), BF16)
        k.memset("pool", onesm[:], 1.0 / 1024, [onesm])
        WB = k.sb((128, 8, 576), BF16)
        wst = k.sb((128, 576), F32)
        for kc in range(8):
            S.dma("sp", wst[:], w1a[kc * 128:(kc + 1) * 128, :], W=[wst], slot="w")
            k.ts("dve", WB[:, kc, :], wst[:], gm[:, kc:kc + 1], None, ALU.mult, None, [wst, gm], [WB])
        if fused:
            for h_ in range(2):
                S.dma("sp", wst[0:64, 0:512], pat[:, h_ * 512:(h_ + 1) * 512], W=[wst], slot="w")
                k.cp("pool", PATb[:, h_ * 512:(h_ + 1) * 512], wst[0:64, 0:512], [wst], [PATb])
        XT32 = [k.sb((128, 8, TT), F32) for _ in range(2)]
        XBs = [k.sb((128, 8, TT), BF16) for _ in range(2)]
        XSQ1 = k.sb((128, 8, TT), BF16)
        XSQs = [XSQ1, XSQ1]
        RSTD = k.sb((128, TT), F32)
        QKVs = [k.sb((64, 3, seq), BF16) for _ in range(2)]
        OF = k.sb((65, seq), F32)
        VA = k.sb((128, seq // 128, 65), BF16)
        k.memset("pool", VA[:], 1.0, [VA])
        TMPs = [k.sb((128, 2, 128), F32) for _ in range(2)]
        PTbs = [k.sb((128, 2, 128), BF16) for _ in range(2)]
        RD = k.sb((64, 512), F32)
        YA = k.sb((64, 512), F32)
        PJ = k.ps((128, 512), F32)
        PM = k.ps((128, 512), F32)
        PSSs = [k.ps((128, 512), F32) for _ in range(2)]
        POs = [k.ps((128, 512), F32) for _ in range(2)]
        PSS = PSSs[0]
        PTr = k.ps((128, 4, 128), BF16)

        def emit_proj(g):
            QKV = QKVs[g % 2]

            def x_dma(tj):
                pj_ = tj % 2
                S.dma("sp", XT32[pj_][:], xT.rearrange("(kc p) t -> p kc t", p=128)[:, :, tj * TT:(tj + 1) * TT], W=[XT32[pj_]], slot=f"x{pj_}")

            def x_prep(tj):
                pj_ = tj % 2
                k.cp("dve", XBs[pj_][:], XT32[pj_][:], [XT32[pj_]], [XBs[pj_]])
                k.act(XSQs[pj_][:], XT32[pj_][:], AF.Square, [XT32[pj_]], [XSQs[pj_]])
            x_dma(0)
            x_prep(0)
            for ti in range(nt):
                t0 = ti * TT
                XB_, XSQ_ = XBs[ti % 2], XSQs[ti % 2]
                if ti + 1 < nt:
                    x_dma(ti + 1)
                k.mm([(PM[:, 0:TT], onesm[:], XSQ_[:, kc, :], kc == 0, kc == 7) for kc in range(8)], [onesm, XSQ_], [PM])
                k.rsqrt(RSTD[:], PM[:, 0:TT], NORM_EPS, [PM], [RSTD])
                for qi in range(3):
                    co = qi * 192 + g * 64
                    k.mm([(PJ[0:64, 0:TT], WB[:, kc, co:co + 64], XB_[:, kc, :], kc == 0, kc == 7) for kc in range(8)],
                         [WB, XB_], [PJ])
                    k.tt("dve", QKV[:, qi, t0:t0 + TT], PJ[0:64, 0:TT], RSTD[0:64, :], ALU.mult, [PJ, RSTD], [QKV])
                if ti + 1 < nt:
                    x_prep(ti + 1)

        def emit_blocks(g):
            d = DILS[g]
            QKV = QKVs[g % 2]
            span = 128 * d

            def blk_ap(qi, n, r):
                v = QKV[:, qi, n * span:(n + 1) * span]
                if d == 1:
                    return v
                return v.rearrange("p (q r) -> p r q", r=d)[:, r, :]

            def of_ap(n, r, rows):
                v = OF[0:rows, n * span:(n + 1) * span]
                if d == 1:
                    return v
                return v.rearrange("p (q r) -> p r q", r=d)[:, r, :]
            nb = seq // span
            for n in range(nb):
                for r0 in range(0, d, 4):
                    rr = list(range(r0, min(d, r0 + 4)))
                    fns = [lambda e, i=i, r=r, n=n: e.transpose(out=PTr[:, i, 0:64], in_=blk_ap(2, n, r), identity=ident[0:64, 0:64])
                           for i, r in enumerate(rr)]
                    S.op("pe", fns, _res([QKV, ident]), _res([PTr]))
                    b0 = n * d + r0
                    k.cp("dve", VA[:, b0:b0 + len(rr), 0:64], PTr[:, 0:len(rr), 0:64], [PTr], [VA])
            for n in range(nb):
                for r in range(d):
                    b = n * d + r
                    PSSb, PO, TMP, PTb = PSSs[b % 2], POs[b % 2], TMPs[b % 2], PTbs[b % 2]
                    items = []
                    if n > 0:
                        items.append((PSSb[:, 0:128], blk_ap(1, n - 1, r), blk_ap(0, n, r), True, True))
                    items.append((PSSb[:, 128:256], blk_ap(1, n, r), blk_ap(0, n, r), True, True))
                    k.mm(items, [QKV], [PSSb])
                    lo = 0 if n > 0 else 1
                    k.stt(TMP[:, lo:2, :], PSSb[:, lo * 128:256].rearrange("p (a b) -> p a b", b=128), 0.125,
                          bias[:, g, lo:2, :], ALU.mult, ALU.add, [PSSb, bias], [TMP])
                    k.act(PTb[:, lo:2, :], TMP[:, lo:2, :], AF.Exp, [TMP], [PTb])
                    items = []
                    if n > 0:
                        items.append((PO[0:65, 0:128], VA[:, b - d, :], PTb[:, 0, :], True, False))
                    items.append((PO[0:65, 0:128], VA[:, b, :], PTb[:, 1, :], n == 0, True))
                    k.mm(items, [VA, PTb], [PO])
                    if g == 0:
                        k.cp("dve", of_ap(n, r, 65), PO[0:65, 0:128], [PO], [OF])
                    else:
                        k.tt("dve", of_ap(n, r, 65), PO[0:65, 0:128], of_ap(n, r, 65), ALU.add, [PO, OF], [OF])

        S.replay([S.record(emit_proj, 0)])
        for g in range(3):
            streams = [S.record(emit_blocks, g)]
            if g + 1 < 3:
                streams.append(S.record(emit_proj, g + 1))
            S.replay(streams)
        RD2 = [RD, k.sb((64, 512), F32)]
        YA2 = [YA, k.sb((64, 512), F32)]
        if fused:
            YAb2 = [YAb, k.sb((64, 512), BF16)]
            PRA2 = [PRA, k.sb((128, 4, 512), BF16)]

        def emit_norm(par):
            RDp, YAp = RD2[par], YA2[par]
            pden = PM if par == 0 else PJ
            pps = [PSSs[par], POs[par]]
            for c0 in range(par * 512, seq, 1024):
                k.mm([(pden[0:64, 0:512], sel[:], OF[:, c0:c0 + 512], True, True)], [sel, OF], [pden])
                S.op("dve", lambda e: e.reciprocal(out=RDp[:], in_=pden[0:64, 0:512]), _res([pden]), _res([RDp]))
                k.tt("dve", YAp[:], OF[0:64, c0:c0 + 512], RDp[:], ALU.mult, [OF, RDp], [YAp])
                if not fused:
                    S.dma("sp", ya[:, c0:c0 + 512], YAp[:], R=[YAp], slot=f"ya{par}")
                else:
                    ntok = ctx["ntok"]
                    sh, col = c0 // ntok, c0 % ntok
                    k.cp("pool", YAb2[par][:], YAp[:], [YAp], [YAb2[par]])
                    for och in range(2):
                        for o4 in range(4):
                            oc = och * 4 + o4
                            pp_ = pps[oc % 2]
                            k.mm([(pp_[:, 0:512], PATb[:, oc * 128:(oc + 1) * 128], YAb2[par][:], True, True)], [PATb, YAb2[par]], [pp_])
                            k.cp("act" if par == 0 else "dve", PRA2[par][:, o4, :], pp_[:, 0:512], [pp_], [PRA2[par]])
                        r0 = sh * 1024 + och * 512
                        S.dma("sp", ctx["src_a"][r0:r0 + 512, col:col + 512].rearrange("(oc p) t -> p oc t", p=128),
                              PRA2[par][:], R=[PRA2[par]], slot=f"pra{par}")
        S.replay([S.record(emit_norm, 0), S.record(emit_norm, 1)])
        S.barrier()
    return nc


def p1b_inputs(x_b, j, norm_mix_g, w_in):
    cols = []
    for base in (3360, 3360 + 768, 3360 + 1536):
        for g in range(3):
            h = 4 * g + j
            cols.append(base + 64 * h + np.arange(64))
    cols = np.concatenate(cols)
    sel = np.zeros((65, 64), np.float32)
    sel[64, :] = 1
    return {"xT": np.ascontiguousarray(x_b.T), "w1a": np.ascontiguousarray(w_in[:, cols]),
            "gmix": np.ascontiguousarray(norm_mix_g.reshape(8, 128).T), "c_ident": np.eye(128, dtype=np.float32),
            "abias": attn_bias(j).reshape(128, -1), "sel": sel}


def build_p2(ntok, ctx):
    NPASS = min(ntok, 1024)
    NB = NPASS // 512
    nc = ctx["nc"]
    dr = lambda n, s, kind="ExternalInput": nc.dram_tensor("c_" + n, list(s), F32, kind=kind).ap()
    xT = dr("xT", (D, ntok))
    memT = dr("memT", (D, 256))
    wg = dr("wg", (D, 2048))
    w_out = dr("w_out", (D, D))
    wq = dr("wq", (D, D))
    wkv = dr("wkv", (D, 2048))
    wo = dr("wo", (D, D))
    w1 = dr("w1", (D, 4096))
    w2 = dr("w2", (4096, D))
    gains = dr("gains", (128, 40))
    outT = dr("outT", (D, ntok), kind="ExternalOutput")
    dst_r, dst_a, dstres_r, dstres_a = ctx["dst_r"], ctx["dst_a"], ctx["dstres_r"], ctx["dstres_a"]
    with ExitStack() as es:
        k = ctx["k"]
        k.es = es
        k.epsb = {}
        S = k.S
        gn = k.sb((128, 40), F32)
        S.dma("sp", gn[:], gains, W=[gn], slot="gn")
        onesm = k.sb((128, 128), BF16)
        k.memset("pool", onesm[:], 1.0 / 1024, [onesm])
        ones1 = k.sb((128, 128), BF16)
        k.memset("pool", ones1[:], 1.0, [ones1])
        NST = 4
        stF = [k.sb((128, 8, 256), F32) for _ in range(NST)]
        stB = [k.sb((128, 8, 256), BF16) for _ in range(NST)]
        PA = [k.ps((128, 512), F32) for _ in range(4)]
        PR = k.ps((128, 512), F32)
        PSc = [k.ps((128, 512), F32) for _ in range(2)]
        PD = k.ps((128, 512), F32)
        lc = [0]
        pac = [0]

        staged = {}

        def stage(W, kp, cb, key=None):
            key = key if key is not None else (id(W), kp, cb)
            if key in staged:
                return staged.pop(key)
            i = lc[0] % NST
            lc[0] += 1
            S.dma("sp" if i % 2 == 0 else "pool", stF[i][:],
                  W[kp * 1024:(kp + 1) * 1024, cb * 256:(cb + 1) * 256].rearrange("(kc p) o -> p kc o", p=128),
                  W=[stF[i]], slot=f"st{i}")
            k.cp("act", stB[i][:], stF[i][:], [stF[i]], [stB[i]])
            return stB[i]

        def prefetch(W, kp, cb):
            key = (id(W), kp, cb)
            if key not in staged:
                staged[key] = stage(W, kp, cb, key=("pf",) + key)

        pool8 = [False]

        def nextpa():
            lst = PA if not pool8[0] else PA + PSc + [PD, PR]
            p = lst[pac[0] % len(lst)]
            pac[0] += 1
            return p

        def linear(W, ocs, Xb, blocks, epi, xres=None, nxt=None):
            ocs = list(ocs)
            for i0 in range(0, len(ocs), 2):
                wb = stage(W, 0, ocs[i0] // 2)
                if i0 + 2 < len(ocs):
                    prefetch(W, 0, ocs[i0 + 2] // 2)
                elif nxt is not None:
                    prefetch(*nxt)
                for o2 in range(2):
                    oc = ocs[i0 + o2]
                    for blk in blocks:
                        pa = nextpa()
                        bs = slice(blk * 512, (blk + 1) * 512)
                        k.mm([(pa[:, :], wb[:, kc, o2 * 128:(o2 + 1) * 128], Xb[:, kc, bs], kc == 0, kc == 7) for kc in range(8)],
                             [wb, ((xres if xres is not None else Xb), blk)], [pa])
                        epi(oc, blk, pa[:, :], pa)

        XSQ = k.sb((128, 8, 512), BF16)
        RSTD = k.sb((128, NPASS), F32)

        def normed(X, gcol, out_bf, nblk, rstd=None, bw=512, off=0):
            rstd = RSTD if rstd is None else rstd
            for blk in range(nblk):
                bs = slice(blk * bw, (blk + 1) * bw)
                xs = slice(off + blk * bw, off + (blk + 1) * bw)
                xr = (X, min(off // 512 + blk, len(X.rs) - 1))
                k.act(XSQ[:, :, 0:bw], X[:, :, xs], AF.Square, [xr], [XSQ])
                k.mm([(PR[:, 0:bw], onesm[:], XSQ[:, kc, 0:bw], kc == 0, kc == 7) for kc in range(8)], [onesm, XSQ], [PR])
                k.rsqrt(rstd[:, bs], PR[:, 0:bw], NORM_EPS, [PR], [rstd])
                if out_bf is not None:
                    for kc in range(8):
                        k.stt(out_bf[:, kc, xs], X[:, kc, xs], gn[:, gcol + kc:gcol + kc + 1], rstd[:, bs], ALU.mult, ALU.mult,
                              [xr, gn, rstd], [(out_bf, min(off // 512 + blk, len(out_bf.rs) - 1))])

        M32 = k.sb((128, 8, 256), F32)
        S.dma("act", M32[:], memT.rearrange("(kc p) t -> p kc t", p=128), W=[M32], slot="mem")
        MN = k.sb((128, 8, 256), BF16)
        RSM = k.sb((128, 256), F32)
        normed(M32, 16, MN, 1, rstd=RSM, bw=256)
        KT = k.sb((128, 8, 256), BF16)
        for cb in range(4):
            wb = stage(wkv, 0, cb)
            for o2 in range(2):
                oc = 2 * cb + o2
                pa = nextpa()
                k.mm([(pa[:, 0:256], wb[:, kc, o2 * 128:(o2 + 1) * 128], MN[:, kc, :], kc == 0, kc == 7) for kc in range(8)], [wb, MN], [pa])
                k.cp("act", KT[:, oc, :], pa[:, 0:256], [pa], [KT])
        VM = k.sb((128, 2, 1024), BF16)
        for cb in range(4):
            wb = stage(wkv, 0, 4 + cb)
            for o2 in range(2):
                oc = 2 * cb + o2
                pa = nextpa()
                for mc in range(2):
                    k.mm([(pa[:, mc * 128:(mc + 1) * 128], MN[:, kc, mc * 128:(mc + 1) * 128], wb[:, kc, o2 * 128:(o2 + 1) * 128], kc == 0, kc == 7)
                          for kc in range(8)], [wb, MN], [pa])
                k.cp("act", VM[:, :, oc * 128:(oc + 1) * 128], pa[:, 0:256].rearrange("p (a b) -> p a b", b=128), [pa], [VM])
        X = k.sb((128, 8, NPASS), F32, n=NB)
        XN = k.sb((128, 8, NPASS), BF16, n=NB)
        AB = k.sb((128, 32, NPASS), BF16, n=NB)
        ABv = AB[:, 0:8, :]
        MRt = [k.sb((128, 512), BF16) for _ in range(2)]
        MAt = [k.sb((128, 512), BF16) for _ in range(2)]
        TG = [k.sb((128, 512), F32) for _ in range(2)]
        TR = TG
        trc = [0]
        PTm = k.sb((128, 2, 512), BF16)
        RDEN = k.sb((128, 512), F32)
        OAT = XN
        for ps_ in range(ntok // NPASS):
          toff = ps_ * NPASS
          S.dma("sp", X[:], xT.rearrange("(kc p) t -> p kc t", p=128)[:, :, toff:toff + NPASS], W=[X], slot="x")
          blocks = list(range(NB))
          prefetch(wg, 0, 0)
          prefetch(wg, 0, 4)
          normed(X, 0, XN, NB)
          for cb in range(4):
            wbr_ = stage(wg, 0, cb)
            wba_ = stage(wg, 0, 4 + cb)
            if cb < 3:
                prefetch(wg, 0, cb + 1)
                prefetch(wg, 0, 4 + cb + 1)
            else:
                prefetch(w_out, 0, 0)
            for o2 in range(2):
              oc = 2 * cb + o2
              wbr = wbr_[:, :, o2 * 128:(o2 + 1) * 128]
              wba = wba_[:, :, o2 * 128:(o2 + 1) * 128]
              for blk in blocks:
                  bs = slice(blk * 512, (blk + 1) * 512)
                  gs = slice(toff + blk * 512, toff + (blk + 1) * 512)
                  mr, ma = MRt[blk % 2], MAt[blk % 2]
                  S.dma("sp", mr[:], dst_r[oc * 128:(oc + 1) * 128, gs], R=[dstres_r], W=[mr], slot=f"mr{blk % 2}")
                  S.dma("pool", ma[:], dst_a[oc * 128:(oc + 1) * 128, gs], R=[dstres_a], W=[ma], slot=f"ma{blk % 2}")
                  pa = nextpa()
                  k.mm([(pa[:, :], wbr[:, kc, :], XN[:, kc, bs], kc == 0, kc == 7) for kc in range(8)], [wbr_, (XN, blk)], [pa])
                  k.act(TG[0][:], pa[:, :], AF.Sigmoid, [pa], [TG[0]])
                  k.tt("dve", TG[0][:], TG[0][:], mr[:], ALU.mult, [mr, TG[0]], [TG[0]])
                  pa = nextpa()
                  k.mm([(pa[:, :], wba[:, kc, :], XN[:, kc, bs], kc == 0, kc == 7) for kc in range(8)], [wba_, (XN, blk)], [pa])
                  k.act(TG[1][:], pa[:, :], AF.Sigmoid, [pa], [TG[1]])
                  k.tt("dve", TG[1][:], TG[1][:], ma[:], ALU.mult, [ma, TG[1]], [TG[1]])
                  k.tt("dve", ABv[:, oc, bs], TG[0][:], TG[1][:], ALU.add, [TG[0], TG[1]], [(AB, blk)])
          def epi_res(oc, blk, ap, pt):
              bs = slice(blk * 512, (blk + 1) * 512)
              k.tt("dve", X[:, oc, bs], ap, X[:, oc, bs], ALU.add, [pt, (X, blk)], [(X, blk)])
          linear(w_out, range(8), ABv, blocks, epi_res, xres=AB, nxt=(wq, 0, 0))
          normed(X, 8, XN, NB)
          linear(wq, range(8), XN, blocks,
                 lambda oc, blk, ap, pt: k.cp("act", ABv[:, oc, blk * 512:(blk + 1) * 512], ap, [pt], [(AB, blk)]), nxt=(wo, 0, 0))
          for blk in blocks:
              bs = slice(blk * 512, (blk + 1) * 512)
              for h in range(4):
                  for mc in range(2):
                      ps = PSc[mc]
                      k.mm([(ps[:, :], KT[:, 2 * h + hf, mc * 128:(mc + 1) * 128], ABv[:, 2 * h + hf, bs], hf == 0, hf == 1)
                            for hf in range(2)], [KT, (AB, blk)], [ps])
                      k.act(PTm[:, mc, :], ps[:, :], AF.Exp, [ps], [PTm], scale=0.0625)
                  k.mm([(PD[:, :], ones1[:], PTm[:, mc, :], mc == 0, mc == 1) for mc in range(2)], [ones1, PTm], [PD])
                  S.op("dve", lambda e: e.reciprocal(out=RDEN[:], in_=PD[:, :]), _res([PD]), _res([RDEN]))
                  for ch in range(2):
                      oc = 2 * h + ch
                      k.mm([(PR[:, :], VM[:, mc, oc * 128:(oc + 1) * 128], PTm[:, mc, :], mc == 0, mc == 1) for mc in range(2)],
                           [VM, PTm], [PR])
                      k.tt("dve", OAT[:, oc, bs], PR[:, :], RDEN[:], ALU.mult, [PR, RDEN], [(OAT, blk)])
          linear(wo, range(8), OAT, blocks, epi_res, nxt=(w1, 0, 0))
          normed(X, 24, XN, NB)

          def epi_u(oc, blk, ap, pt):
              t = TR[trc[0] % 2]
              trc[0] += 1
              k.act(t[:], ap, AF.Relu, [pt], [t])
              k.tt("dve", AB[:, oc, blk * 512:(blk + 1) * 512], t[:], t[:], ALU.mult, [t], [(AB, blk)])
          linear(w1, range(32), XN, blocks, epi_u, nxt=(w2, 0, 0))
          pool8[0] = True
          for cb in range(4):
              pas = {(o2, blk): nextpa() for o2 in range(2) for blk in blocks}
              for kp in range(4):
                  wb = stage(w2, kp, cb)
                  if kp < 3:
                      prefetch(w2, kp + 1, cb)
                  elif cb < 3:
                      prefetch(w2, 0, cb + 1)
                  for o2 in range(2):
                      for blk in blocks:
                          k.mm([(pas[(o2, blk)][:, :], wb[:, kc, o2 * 128:(o2 + 1) * 128], AB[:, kp * 8 + kc, blk * 512:(blk + 1) * 512],
                                 kp == 0 and kc == 0, kp == 3 and kc == 7) for kc in range(8)], [wb, (AB, blk)], [pas[(o2, blk)]])
              for o2 in range(2):
                  oc = 2 * cb + o2
                  for blk in blocks:
                      bs = slice(blk * 512, (blk + 1) * 512)
                      k.tt("dve", X[:, oc, bs], pas[(o2, blk)][:, :], X[:, oc, bs], ALU.add, [pas[(o2, blk)], (X, blk)], [(X, blk)])
          pool8[0] = False
          for blk in blocks:
              bs = slice(blk * 512, (blk + 1) * 512)
              normed(X, 32, None, 1, off=blk * 512)
              for kc in range(8):
                  k.stt(X[:, kc, bs], X[:, kc, bs], gn[:, 32 + kc:33 + kc], RSTD[:, 0:512], ALU.mult, ALU.mult, [(X, blk), gn, RSTD], [(X, blk)])
          S.dma("sp", outT.rearrange("(kc p) t -> p kc t", p=128)[:, :, toff:toff + NPASS], X[:], R=[X], slot="out")

        S.barrier()
    return nc


def g8(v):
    return np.ascontiguousarray(np.asarray(v, np.float32).reshape(8, 128).T)


def build_fused(seq):
    ntok = seq // 4
    nc = bass.Bass("TRN2", target_bir_lowering=False)
    RG = [[0, 1, 2, 3], [4, 5, 6, 7]]
    with ExitStack() as es:
        k = K(nc, es)
        S = k.S
        mk = lambda n, r: nc.dram_tensor(n, [r, ntok], BF16).ap()
        src_r, dst_r, src_a, dst_a = mk("rs_src_r", 4096), mk("rs_dst_r", 1024), mk("rs_src_a", 4096), mk("rs_dst_a", 1024)
        ctx = {"nc": nc, "k": k, "src_r": src_r, "dst_r": dst_r, "src_a": src_a, "dst_a": dst_a, "ntok": ntok, "shared": {}}

        def reduce_scatter(name, src, dst):
            cc = es.enter_context(nc.semaphore(name))
            nc.gpsimd.collective_compute("ReduceScatter", ALU.add, replica_groups=RG, ins=[src.opt()], outs=[dst.opt()]).then_inc(cc, 1)
            S.sem[name] = cc
            S.cnt[name] = 1
            r = Res()
            r.w = (name, 1)
            return r
        S.pre = "a_"
        build_p1a(seq, ctx)
        ctx["dstres_r"] = reduce_scatter("cc_r", src_r, dst_r)
        S.pre = "b_"
        build_p1b(seq, ctx)
        ctx["dstres_a"] = reduce_scatter("cc_a", src_a, dst_a)
        S.pre = "c_"
        build_p2(ntok, ctx)
    return nc


def kernel(x, mem, norm_mix_g, w_in, shift_mu, w0, w2, a0, a2, g2, k_k, k_a, r_k, gn_w, gn_b,
           p_rwkv, p_attn, w_out, norm_x_g, norm_mem_g, xa_wq, xa_wkv, xa_wo,
           norm_ffn_g, ffn_w1, ffn_w2, norm_final_g):
    A = lambda v: np.asarray(v, np.float32)
    x, mem = A(x), A(mem)
    L0 = lambda v: A(v)[0]
    B, Sq, _ = x.shape
    cores = list(range(8))
    ntok = Sq // 4
    nc = build_fused(Sq)
    gains = np.concatenate([g8(L0(norm_mix_g)), g8(L0(norm_x_g)), g8(L0(norm_mem_g)), g8(L0(norm_ffn_g)), g8(A(norm_final_g))], axis=1)
    wgc = np.ascontiguousarray(L0(w_in)[:, 3360 + 2304:])
    maps = []
    for c in cores:
        b, j = c // 4, c % 4
        m = {}
        ma = p1a_inputs(x[b], j, L0(norm_mix_g), L0(w_in), L0(shift_mu), L0(w0), L0(w2), L0(a0), L0(a2), L0(g2),
                        L0(k_k), L0(k_a), L0(r_k), L0(gn_w), L0(gn_b))
        mb = p1b_inputs(x[b], j, L0(norm_mix_g), L0(w_in))
        shared = ("xT", "gmix", "c_ident")
        for kk_, v in ma.items():
            m[kk_ if kk_ in shared else "a_" + kk_] = v
        for kk_, v in mb.items():
            if kk_ not in shared:
                m["b_" + kk_] = v
        m["a_prw"] = np.ascontiguousarray(L0(p_rwkv)[256 * j:256 * j + 256])
        m["b_pat"] = np.ascontiguousarray(L0(p_attn)[64 * j:64 * j + 64])
        ts_ = slice(j * ntok, (j + 1) * ntok)
        mc = {"xT": np.ascontiguousarray(x[b, ts_].T), "memT": np.ascontiguousarray(mem[b].T),
              "wg": wgc, "w_out": L0(w_out), "wq": L0(xa_wq),
              "wkv": L0(xa_wkv), "wo": L0(xa_wo), "w1": L0(ffn_w1), "w2": L0(ffn_w2), "gains": gains}
        for kk_, v in mc.items():
            m["c_" + kk_] = v
        maps.append(m)
    r3 = run_bass_kernel_spmd(nc, maps, core_ids=cores).results
    out = np.zeros((B, Sq, D), np.float32)
    for c in cores:
        b, j = c // 4, c % 4
        out[b, j * ntok:(j + 1) * ntok] = A(r3[c]["c_outT"]).T
    return out
```

```python
import numpy as np
from contextlib import ExitStack
import concourse.bass as bass
import concourse.mybir as mybir
from concourse.bass_utils import run_bass_kernel_spmd

F32 = mybir.dt.float32
BF16 = mybir.dt.bfloat16
AF = mybir.ActivationFunctionType
ALU = mybir.AluOpType

D = 1024
SEQ = 8192
TT = 256
CH = 64
NCH = TT // CH
CDEC = 0.6065306597126334
GN_EPS = 64 * 1e-5
NORM_EPS = 1e-6


import os
STOP = 99
DBG = False


class _Stop(Exception):
    pass


def stop(n):
    if STOP == n:
        raise _Stop()


class Res:
    __slots__ = ("w", "r")

    def __init__(self):
        self.w = None
        self.r = {}


class Tl:
    def __init__(self, h, n=1):
        self.h = h
        self.rs = [Res() for _ in range(n)]

    def __getitem__(self, k):
        return self.h[k]


def _res(xs):
    out = []
    for x in xs:
        if isinstance(x, Tl):
            out.extend(x.rs)
        elif isinstance(x, tuple):
            out.append(x[0].rs[x[1]])
        else:
            out.append(x)
    return out


class Sched:
    def __init__(self, nc, es):
        self.nc = nc
        self.es = es
        self.engs = {"pe": nc.tensor, "dve": nc.vector, "act": nc.scalar, "pool": nc.gpsimd, "sp": nc.sync}
        self.sem = {}
        self.cnt = {}
        self.seen = {k: {} for k in self.engs}
        self.pre = ""
        self.rec = None
        for k in self.engs:
            self.sem[k] = es.enter_context(nc.semaphore("s_" + k))
            self.cnt[k] = 0

    def need(self, eng, src, val):
        if src == eng and src == "pe":
            return
        if self.seen[eng].get(src, 0) >= val:
            return
        self.engs[eng].wait_ge(self.sem[src], val)
        self.seen[eng][src] = val

    def _deps(self, eng, R, W):
        deps = {}
        for res in R:
            if res.w is not None:
                s, v = res.w
                deps[s] = max(deps.get(s, 0), v)
        for res in W:
            if res.w is not None:
                s, v = res.w
                deps[s] = max(deps.get(s, 0), v)
            for s, v in res.r.items():
                deps[s] = max(deps.get(s, 0), v)
        for s, v in deps.items():
            self.need(eng, s, v)

    def op(self, eng, fns, R=(), W=()):
        R = _res(R)
        W = _res(W)
        if not isinstance(fns, (list, tuple)):
            fns = [fns]
        if self.rec is not None:
            self.rec.append(("op", eng, fns, R, W))
            return
        self._deps(eng, R, W)
        e = self.engs[eng]
        ins = None
        for f in fns:
            ins = f(e)
        self.cnt[eng] += 1
        idx = self.cnt[eng]
        ins.then_inc(self.sem[eng], 1)
        for res in R:
            res.r[eng] = idx
        for res in W:
            res.w = (eng, idx)
            res.r = {}

    def dma(self, q, out, in_, R=(), W=(), slot=None):
        R = _res(R)
        W = _res(W)
        if self.rec is not None:
            self.rec.append(("dma", q, out, in_, R, W, slot))
            return
        slot = self.pre + slot
        if slot not in self.sem:
            self.sem[slot] = self.es.enter_context(self.nc.semaphore("d_" + slot))
            self.cnt[slot] = 0
        self._deps(q, R, W)
        self.engs[q].dma_start(out=out, in_=in_).then_inc(self.sem[slot], 16)
        self.cnt[slot] += 16
        v = self.cnt[slot]
        for res in R:
            res.r[slot] = v
        for res in W:
            res.w = (slot, v)
            res.r = {}

    def record(self, fn, *args):
        assert self.rec is None
        self.rec = []
        fn(*args)
        r, self.rec = self.rec, None
        return r

    def _emit(self, it):
        if it[0] == "op":
            self.op(it[1], it[2], it[3], it[4])
        else:
            self.dma(it[1], it[2], it[3], it[4], it[5], it[6])

    def replay(self, streams):
        streams = [st for st in streams if st]
        if not streams:
            return
        main, others = streams[0], streams[1:]
        quanta, cur, seen_nonpe = [], [], False
        for it in main:
            is_pe = it[0] == "op" and it[1] == "pe"
            if is_pe and seen_nonpe:
                quanta.append(cur)
                cur, seen_nonpe = [], False
            cur.append(it)
            if not is_pe:
                seen_nonpe = True
        if cur:
            quanta.append(cur)
        nq = len(quanta)
        pos = [0] * len(others)
        for qi, q in enumerate(quanta):
            for it in q:
                self._emit(it)
            for oi, st in enumerate(others):
                tgt = (len(st) * (qi + 1) + nq - 1) // nq
                while pos[oi] < min(tgt, len(st)):
                    self._emit(st[pos[oi]])
                    pos[oi] += 1
        for oi, st in enumerate(others):
            while pos[oi] < len(st):
                self._emit(st[pos[oi]])
                pos[oi] += 1

    def barrier(self):
        for eng in self.engs:
            for s, v in self.cnt.items():
                if s != eng and v > 0:
                    self.need(eng, s, v)


def _mk_dr(nc, ctx, pre):
    shared = ctx["shared"] if ctx else None

    def dr(n, s, kind="ExternalInput"):
        if shared is not None and kind == "ExternalInput" and n in ("xT", "gmix", "c_ident"):
            if n not in shared:
                shared[n] = nc.dram_tensor(n, list(s), F32, kind=kind).ap()
            return shared[n]
        return nc.dram_tensor(pre + n, list(s), F32, kind=kind).ap()
    return dr


class K:
    def __init__(self, nc, es):
        self.nc = nc
        self.es = es
        self.S = Sched(nc, es)
        self.n = 0
        self.epsb = {}

    def sb(self, shape, dt, n=1, name=None):
        self.n += 1
        return Tl(self.es.enter_context(self.nc.sbuf_tensor(name or f"t{self.n}", list(shape), dt)), n)

    def ps(self, shape, dt, n=1, name=None):
        self.n += 1
        return Tl(self.es.enter_context(self.nc.psum_tensor(name or f"p{self.n}", list(shape), dt)), n)

    def tt(self, eng, out, a, b, op, R, W):
        self.S.op(eng, lambda e: e.tensor_tensor(out=out, in0=a, in1=b, op=op), R, W)

    def ts(self, eng, out, a, s1, s2, op0, op1, R, W):
        if op1 is None:
            self.S.op(eng, lambda e: e.tensor_scalar(out=out, in0=a, scalar1=s1, scalar2=None, op0=op0), R, W)
        else:
            self.S.op(eng, lambda e: e.tensor_scalar(out=out, in0=a, scalar1=s1, scalar2=s2, op0=op0, op1=op1), R, W)

    def stt(self, out, a, s, b, op0, op1, R, W):
        self.S.op("dve", lambda e: e.scalar_tensor_tensor(out=out, in0=a, scalar=s, in1=b, op0=op0, op1=op1), R, W)

    def act(self, out, in_, func, R, W, bias=0.0, scale=1.0):
        self.S.op("act", lambda e: e.activation(out=out, in_=in_, func=func, bias=bias, scale=scale), R, W)

    def cp(self, eng, out, in_, R, W):
        if eng == "act":
            self.S.op("act", lambda e: e.activation(out=out, in_=in_, func=AF.Copy), R, W)
        else:
            self.S.op(eng, lambda e: e.tensor_copy(out=out, in_=in_), R, W)

    def mm(self, items, R, W):
        fns = []
        for (o, l, r, st, sp) in items:
            fns.append(lambda e, o=o, l=l, r=r, st=st, sp=sp: e.matmul(o, lhsT=l, rhs=r, start=st, stop=sp))
        self.S.op("pe", fns, R, W)

    def rsqrt(self, out, in_, eps, R, W):
        if eps not in self.epsb:
            t = self.sb((128, 1), F32)
            self.memset("pool", t[:], float(eps), [t])
            self.epsb[eps] = t
        eb = self.epsb[eps]
        np_ = out.shape[0]
        self.S.op("act", lambda e: e.activation(out=out, in_=in_, func=AF.Ln, bias=eb[0:np_, :], scale=1.0), list(R) + [eb], W)
        self.S.op("act", lambda e: e.activation(out=out, in_=out, func=AF.Exp, scale=-0.5), W, W)

    def memset(self, eng, ap, val, W):
        self.S.op(eng, lambda e: e.memset(ap, val), (), W)


def consts_p1():
    c = {}
    c["ident"] = np.eye(128, dtype=np.float32)
    s = np.arange(64)[:, None]
    t = np.arange(64)[None, :]
    U = (t > s).astype(np.float32)
    Ui = (t >= s).astype(np.float32)
    L = (s > t).astype(np.float32)
    c["mask6"] = np.concatenate([L, L, U, Ui, U, Ui], axis=1).astype(np.float32)
    bd = np.zeros((128, 128), np.float32)
    bd[:64, :64] = 1
    bd[64:, 64:] = 1
    c["bd"] = bd
    sh = np.zeros((128, 64), np.float32)
    sh[64 + np.arange(64), np.arange(64)] = 1
    c["shdn"] = sh
    ilow = np.zeros((64, 128), np.float32)
    ilow[np.arange(64), np.arange(64)] = 1
    iup = np.zeros((64, 128), np.float32)
    iup[np.arange(64), 64 + np.arange(64)] = 1
    c["ilu"] = np.concatenate([ilow, iup], axis=1)
    id2 = np.zeros((128, 64), np.float32)
    id2[np.arange(128), np.arange(128) % 64] = 1
    c["id2"] = id2
    rm = np.ones((128, TT), np.float32)
    rm[:, ::CH] = 0
    c["rmask"] = rm
    return c


CONST_SHAPES_P1 = {"ident": (128, 128), "mask6": (64, 384), "bd": (128, 128), "shdn": (128, 64),
                   "ilu": (64, 256), "id2": (128, 64), "rmask": (128, TT)}

PV_MU = 0
PV_W0 = 10
PV_A0 = 12
PV_KK = 14
PV_KA = 16
PV_RK = 18
PV_GW = 20
PV_GB = 22
NPV = 24

RW_CH = [(0, 128), (128, 128), (256, 128), (384, 128), (512, 128), (640, 128),
         (768, 64), (832, 64), (896, 128), (1024, 32)]
NRW = 1056


def build_p1a(seq, ctx=None):
    nt = seq // TT
    fused = ctx is not None
    nc = ctx["nc"] if fused else bass.Bass("TRN2", target_bir_lowering=False)
    dr = _mk_dr(nc, ctx, "a_" if fused else "")
    xT = dr("xT", (D, seq))
    w1 = dr("w1", (D, NRW))
    gmix = dr("gmix", (128, 8))
    pvec = dr("pvec", (128, NPV))
    w2s = dr("w2s", (64, 256))
    a2s = dr("a2s", (64, 256))
    g2s = dr("g2s", (160, 256))
    cd = {k: dr("c_" + k, v) for k, v in CONST_SHAPES_P1.items()}
    yr = None if fused else dr("yr", (256, seq), kind="ExternalOutput")
    prw = dr("prw", (256, D)) if fused else None
    dbgf = dr("dbgf", (128, 40 * TT), kind="ExternalOutput") if DBG else None
    dbgb = nc.dram_tensor("dbgb", [128, 40 * TT], BF16, kind="ExternalOutput").ap() if DBG else None
    dbc = {"f": 0, "b": 0}

    def dump(S, tile, ap, kind, name, rows=128, cols=TT):
        i = dbc[kind]
        dbc[kind] += (cols + TT - 1) // TT
        dst = (dbgf if kind == "f" else dbgb)[0:rows, i * TT:i * TT + cols]
        S.dma("sp", dst, ap, R=[tile], slot=f"dbg{kind}{i}")

    with ExitStack() as es:
        if fused:
            k = ctx["k"]
            k.es = es
            k.epsb = {}
        else:
            k = K(nc, es)
        S = k.S
        if fused:
            PRWb = k.sb((128, 2, D), BF16)
            YOb = [k.sb((128, TT), BF16) for _ in range(2)]
            PRD = k.sb((128, 8, TT), BF16)
        def load_const(name, shape, to_bf16=True, q="sp"):
            f = k.sb(shape, F32)
            S.dma(q, f[:], cd[name], W=[f], slot="c_" + name)
            if not to_bf16:
                return f
            b = k.sb(shape, BF16)
            k.cp("pool", b[:], f[:], [f], [b])
            return b
        ident = load_const("ident", (128, 128))
        mask6 = load_const("mask6", (64, 384), to_bf16=False)
        bd = load_const("bd", (128, 128))
        shdn = load_const("shdn", (128, 64))
        ilu = load_const("ilu", (64, 256))
        id2f = load_const("id2", (128, 64), to_bf16=False)
        id2 = k.sb((128, 64), BF16)
        k.cp("pool", id2[:], id2f[:], [id2f], [id2])
        rmask = load_const("rmask", (128, TT), to_bf16=False)
        bd64 = k.sb((128, 128), BF16)
        k.ts("pool", bd64[:], bd[:], 1.0 / 64, None, ALU.mult, None, [bd], [bd64])
        onesm = k.sb((128, 128), BF16)
        k.memset("pool", onesm[:], 1.0 / 1024, [onesm])
        pv = k.sb((128, NPV), F32)
        S.dma("sp", pv[:], pvec, W=[pv], slot="pv")
        gm = k.sb((128, 8), F32)
        S.dma("sp", gm[:], gmix, W=[gm], slot="gm")
        def load_bf(ap, shape, q="sp", name="l"):
            f = k.sb(shape, F32)
            S.dma(q, f[:], ap, W=[f], slot=name)
            b = k.sb(shape, BF16)
            k.cp("pool", b[:], f[:], [f], [b])
            return b
        w2b = load_bf(w2s, (64, 256), name="w2")
        a2b = load_bf(a2s, (64, 256), name="a2")
        g2b0 = load_bf(g2s[0:128, :], (128, 256), name="g20")
        g2b1 = load_bf(g2s[128:160, :], (32, 256), name="g21")
        WB = k.sb((128, 8, NRW), BF16, n=8)
        wst = [k.sb((128, NRW), F32)]
        for kc in range(8):
            st = wst[0]
            S.dma("sp" if kc % 2 == 0 else "act", st[:], w1[kc * 128:(kc + 1) * 128, :], W=[st], slot="w0")
            k.ts("dve", WB[:, kc, :], st[:], gm[:, kc:kc + 1], None, ALU.mult, None, [st, gm], [(WB, kc)])
        if fused:
            for h_ in range(2):
                S.dma("sp", wst[0][:, 0:D], prw[h_ * 128:(h_ + 1) * 128, :], W=[wst[0]], slot="w0")
                k.cp("pool", PRWb[:, h_, :], wst[0][:, 0:D], [wst[0]], [PRWb])

        XT32 = [k.sb((128, 8, TT), F32) for _ in range(2)]
        XB = [k.sb((128, 8, TT), BF16) for _ in range(2)]
        XSQs = [k.sb((128, 8, TT), BF16) for _ in range(2)]
        RSTD = k.sb((128, TT), F32)
        PROJ = [k.sb((128, 10, TT + 1), F32, n=11) for _ in range(2)]
        for p in PROJ:
            k.memset("pool", p[:], 0.0, [p])
        PP = k.sb((128, 10, TT), F32, n=2)
        X6 = k.sb((64, 2, TT), BF16)
        SG = k.sb((128, TT), BF16)
        SG8 = k.sb((32, TT), BF16)
        f32t = lambda: k.sb((128, TT), F32)
        bft = lambda: k.sb((128, TT), BF16)
        base = dict()
        for nm in ["LD", "AS", "KK", "RN", "KN", "T1", "KM", "B", "CS", "WI", "WV", "E1", "WE", "E2", "WH"]:
            base[nm] = f32t()
        for nm in ["KK2", "RKB", "YB", "YSQ"]:
            base[nm] = bft()
        for nm in ["MS", "NEG", "VAR", "RS", "YC", "YN", "YG", "YO"]:
            base[nm] = f32t()
        tmp = []
        for i in range(4):
            d = dict(base)
            if i < 2:
                d["FM"] = k.sb((128, 5, TT), BF16, n=5)
                d["TM"] = k.sb((128, 3, TT), BF16, n=3)
                d["FMo"] = k.sb((64, 5, TT), BF16)
            else:
                d["FM"], d["TM"], d["FMo"] = tmp[i - 2]["FM"], tmp[i - 2]["TM"], tmp[i - 2]["FMo"]
            d["GT"] = f32t()
            d["BON"] = f32t()
            tmp.append(d)
        hd = [dict() for _ in range(4)]
        for h in range(4):
            d = hd[h]
            d["S1"] = k.sb((64, NCH, 576), BF16, n=3)
            d["VT"] = k.sb((64, NCH, 64), BF16)
            d["XT"] = [k.sb((64, NCH, 192), BF16) for _ in range(2)]
            d["Q3"] = k.sb((64, NCH, 128), BF16)
            d["RM"] = k.sb((64, NCH, 128), BF16)
            d["AG"] = k.sb((64, NCH, 128), BF16)
            d["ST"] = k.sb((64, 64), BF16)
            k.memset("pool", d["ST"][:], 0.0, [d["ST"]])
            d["YS"] = k.sb((64, TT), BF16)
        PB0 = k.ps((128, 512), F32)
        PB1 = k.ps((128, 512), F32)
        PB3 = k.ps((128, 512), F32)
        PT = k.ps((64, 2, 4, 128), BF16)
        SC = k.ps((64, 2048), F32, n=4)
        PJ = [(PB0, 0)]
        PM = [(PB1, 0), (PB3, 0)]
        pmc = [0]

        def pm():
            return PM[pmc[0]]

        def pap(slot, rows=128, cols=TT):
            t, i = slot
            return t[0:rows, 0:cols]

        pjc = [0]

        def emitT(ti):
            t0 = ti * TT
            par = ti % 2
            xt = XT32[par]
            xb = XB[par]
            pj = PROJ[par]
            pjn = PROJ[1 - par]
            def x_dma(tj):
                pj_ = tj % 2
                S.dma("sp", XT32[pj_][:], xT.rearrange("(kc p) t -> p kc t", p=128)[:, :, tj * TT:(tj + 1) * TT], W=[XT32[pj_]], slot=f"x{pj_}")

            def x_prep(tj):
                pj_ = tj % 2
                k.cp("dve", XB[pj_][:], XT32[pj_][:], [XT32[pj_]], [XB[pj_]])
                k.act(XSQs[pj_][:], XT32[pj_][:], AF.Square, [XT32[pj_]], [XSQs[pj_]])
            if ti == 0:
                x_dma(0)
                x_prep(0)
            if ti + 1 < nt:
                x_dma(ti + 1)
            XSQ = XSQs[par]
            slot = pm()
            k.mm([(pap(slot), onesm[:], XSQ[:, kc, :], kc == 0, kc == 7) for kc in range(8)], [onesm, XSQ], [slot])
            k.rsqrt(RSTD[:], pap(slot), NORM_EPS, [slot], [RSTD])
            def shift(c0, c1, pres):
                grp = [(pj, i) for i in range(c0, c1)] + [(pj, 10)]
                k.tt("dve", PP[:, c0:c1, :], pj[:, c0:c1, 0:TT], pj[:, c0:c1, 1:TT + 1], ALU.subtract, grp, [pres])
                k.tt("dve", PP[:, c0:c1, :], PP[:, c0:c1, :],
                     pv[:, PV_MU + c0:PV_MU + c1].unsqueeze(2).broadcast_to([128, c1 - c0, TT]), ALU.mult, [pres, pv], [pres])
                k.tt("dve", PP[:, c0:c1, :], PP[:, c0:c1, :], pj[:, c0:c1, 1:TT + 1], ALU.add, [pres] + grp, [pres])
            for n_, ci in enumerate([6, 7, 8, 9, 0, 1, 2, 3, 4, 5]):
                co, M = RW_CH[ci]
                slot = PJ[0]
                k.mm([(pap(slot, M), WB[:, kc, co:co + M], xb[:, kc, :], kc == 0, kc == 7) for kc in range(8)],
                     [WB, xb], [slot])
                k.tt("dve", pj[0:M, ci, 1:TT + 1], pap(slot, M), RSTD[0:M, :], ALU.mult, [slot, RSTD], [(pj, ci)])
                if n_ == 3:
                    shift(6, 10, (PP, 1))
                    k.act(X6[:, 0, :], PP[0:64, 6, :], AF.Tanh, [(PP, 1)], [X6])
                    k.cp("pool", X6[:, 1, :], PP[0:64, 7, :], [(PP, 1)], [X6])
                    k.act(SG[:], PP[:, 8, :], AF.Sigmoid, [(PP, 1)], [SG])
                    k.act(SG8[:], PP[0:32, 9, :], AF.Sigmoid, [(PP, 1)], [SG8])
            shift(0, 6, (PP, 0))
            allpj = [(pj, i) for i in range(11)]
            k.cp("pool", pjn[:, :, 0:1], pj[:, :, TT:TT + 1], allpj, [(pjn, 10)])
            if ti + 1 < nt:
                x_prep(ti + 1)
        def emitA(ti, hp, u):
            d = tmp[u % 4]
            hs = slice(hp * 128, (hp + 1) * 128)
            pc = lambda c: pv[:, c + hp:c + hp + 1]
            PR, PK, PVv = PP[:, 0 + hp, :], PP[:, 2 + hp, :], PP[:, 4 + hp, :]
            FM, TM = d["FM"], d["TM"]
            heads = [hp * 2, hp * 2 + 1]
            t0 = ti * TT
            hs = slice(hp * 128, (hp + 1) * 128)
            pc = lambda c: pv[:, c + hp:c + hp + 1]
            PR, PK, PVv = PP[:, 0 + hp, :], PP[:, 2 + hp, :], PP[:, 4 + hp, :]
            FM, TM = d["FM"], d["TM"]
            s1 = pm()
            k.mm([(pap(s1), w2b[:, hs], X6[:, 0, :], True, True)], [w2b, X6], [s1])
            k.act(d["LD"][:], pap(s1), AF.Sigmoid, [s1, pv], [d["LD"]], bias=pc(PV_W0))
            s2 = pm()
            k.mm([(pap(s2), a2b[:, hs], X6[:, 1, :], True, True)], [a2b, X6], [s2])
            k.act(d["AS"][:], pap(s2), AF.Sigmoid, [s2, pv], [d["AS"]], bias=pc(PV_A0))
            s3 = pm()
            k.mm([(pap(s3), g2b0[:, hs], SG[:], True, False), (pap(s3), g2b1[:, hs], SG8[:], False, True)],
                 [g2b0, g2b1, SG, SG8], [s3])
            k.cp("act", d["GT"][:], pap(s3), [s3], [d["GT"]])
            S.op("dve", lambda e, d=d: e.tensor_tensor_scan(out=d["CS"][:], data0=rmask[:], data1=d["LD"][:],
                                                             initial=0.0, op0=ALU.mult, op1=ALU.add),
                 _res([rmask, d["LD"]]), _res([d["CS"]]))
            k.act(d["WI"][:], d["CS"][:], AF.Exp, [d["CS"]], [d["WI"]], scale=-CDEC)
            k.act(d["WV"][:], d["CS"][:], AF.Exp, [d["CS"]], [d["WV"]], scale=CDEC)
            k.tt("dve", d["E1"][:], d["CS"][:], d["LD"][:], ALU.subtract, [d["CS"], d["LD"]], [d["E1"]])
            k.act(d["WE"][:], d["E1"][:], AF.Exp, [d["E1"]], [d["WE"]], scale=-CDEC)
            cs3 = d["CS"][:].rearrange("p (c t) -> p c t", t=CH)
            k.tt("pool", d["E2"][:].rearrange("p (c t) -> p c t", t=CH),
                 cs3[:, :, CH - 1:CH].broadcast_to([128, NCH, CH]), cs3, ALU.subtract, [d["CS"]], [d["E2"]])
            k.act(d["WH"][:], d["E2"][:], AF.Exp, [d["E2"]], [d["WH"]], scale=-CDEC)
            wi3 = d["WI"][:].rearrange("p (c t) -> p c t", t=CH)
            k.tt("dve", FM[:, 4, :].rearrange("p (c t) -> p c t", t=CH),
                 id2f[:].unsqueeze(1).broadcast_to([128, NCH, CH]),
                 wi3[:, :, CH - 1:CH].broadcast_to([128, NCH, CH]), ALU.mult, [id2f, d["WI"]], [(FM, 4)])
            k.ts("dve", d["KK"][:], PK, pc(PV_KK), None, ALU.mult, None, [(PP, 0), pv], [d["KK"]])
            k.tt("dve", d["KK2"][:], d["KK"][:], d["KK"][:], ALU.mult, [d["KK"]], [d["KK2"]])
            k.ts("dve", d["T1"][:], d["AS"][:], -1.0, pc(PV_KA), ALU.add, ALU.mult, [d["AS"], pv], [d["T1"]])
            k.stt(d["KM"][:], d["T1"][:], 1.0, PK, ALU.add, ALU.mult, [d["T1"], (PP, 0)], [d["KM"]])
            s4 = pm()
            k.mm([(pap(s4), bd[:], d["KK2"][:], True, True)], [bd, d["KK2"]], [s4])
            k.ts("dve", d["RN"][:], pap(s4), 1e-24, None, ALU.max, None, [s4], [d["RN"]])
            k.stt(d["RKB"][:], PR, pc(PV_RK), d["KM"][:], ALU.mult, ALU.mult, [(PP, 0), pv, d["KM"]], [d["RKB"]])
            k.tt("pool", FM[:, 3, :], PR, d["WI"][:], ALU.mult, [(PP, 0), d["WI"]], [(FM, 3)])
            k.cp("pool", TM[:, 2, :], PVv, [(PP, 0)], [(TM, 2)])
            k.tt("dve", FM[:, 0, :], d["KM"][:], d["WV"][:], ALU.mult, [d["KM"], d["WV"]], [(FM, 0)])
            k.tt("dve", TM[:, 1, :], d["KM"][:], d["WH"][:], ALU.mult, [d["KM"], d["WH"]], [(TM, 1)])
            s5 = pm()
            k.mm([(pap(s5), bd[:], d["RKB"][:], True, True)], [bd, d["RKB"]], [s5])
            k.tt("dve", d["BON"][:], pap(s5), PVv, ALU.mult, [s5, (PP, 0)], [d["BON"]])
            k.act(d["RN"][:], d["RN"][:], AF.Ln, [d["RN"]], [d["RN"]])
            k.act(d["RN"][:], d["RN"][:], AF.Exp, [d["RN"]], [d["RN"]], scale=-0.5)
            k.tt("dve", d["KN"][:], d["KK"][:], d["RN"][:], ALU.mult, [d["KK"], d["RN"]], [d["KN"]])
            k.tt("pool", d["B"][:], d["KN"][:], d["AS"][:], ALU.mult, [d["KN"], d["AS"]], [d["B"]])
            k.stt(FM[:, 2, :], d["KN"][:], -1.0, d["WE"][:], ALU.mult, ALU.mult, [d["KN"], d["WE"]], [(FM, 2)])
            k.tt("pool", FM[:, 1, :], d["B"][:], d["WV"][:], ALU.mult, [d["B"], d["WV"]], [(FM, 1)])
            k.tt("dve", TM[:, 0, :], d["B"][:], d["WH"][:], ALU.mult, [d["B"], d["WH"]], [(TM, 0)])
            if DBG and ti == 0 and hp == 0:
                for ci in range(10):
                    dump(S, PP, PP[:, ci, :], "f", f"PP{ci}")
                for nm in ["LD", "AS", "GT", "KN", "KM", "B", "CS", "WI", "WE", "WH", "BON"]:
                    dump(S, d[nm], d[nm][:], "f", nm)
                for q in range(5):
                    dump(S, FM, FM[:, q, :], "b", f"FM{q}")
                for q in range(3):
                    dump(S, TM, TM[:, q, :], "b", f"TM{q}")
            S.dma("act", d["FMo"][:], FM[64:128, :, :], R=[FM], W=[d["FMo"]], slot=f"fmo{u % 2}")
            srcs = [FM[:, 2, :], TM[:, 0, :], TM[:, 1, :], TM[:, 2, :]]
            for half in range(2):
                fns = []
                for cc in range(2):
                    c = half * 2 + cc
                    for q in range(4):
                        fns.append(lambda e, cc=cc, q=q, c=c: e.transpose(
                            out=PT[:, cc, q, :], in_=srcs[q][:, c * CH:(c + 1) * CH], identity=ident[:]))
                S.op("pe", fns, _res([FM, TM, ident]), _res([PT]))
                for e_ in range(2):
                    h = hp * 2 + e_
                    S1, VT = hd[h]["S1"], hd[h]["VT"]
                    cs_ = slice(half * 2, half * 2 + 2)
                    es_ = slice(e_ * 64, e_ * 64 + 64)
                    eng = "dve"
                    k.cp(eng, S1[:, cs_, 0:64], PT[:, :, 0, es_], [PT], [(S1, 1)])
                    k.cp(eng, S1[:, cs_, 320:384], PT[:, :, 1, es_], [PT], [(S1, 1)])
                    k.cp(eng, S1[:, cs_, 512:576], PT[:, :, 2, es_], [PT], [(S1, 1)])
                    k.cp(eng, VT[:, cs_, :], PT[:, :, 3, es_], [PT], [VT])
        def emitTA(ti, hp, u):
            pmc[0] = 0
            if hp == 0:
                emitT(ti)
            emitA(ti, hp, u)
        def emitB(ti, hp, u):
            d = tmp[u % 4]
            hs = slice(hp * 128, (hp + 1) * 128)
            pc = lambda c: pv[:, c + hp:c + hp + 1]
            PR, PK, PVv = PP[:, 0 + hp, :], PP[:, 2 + hp, :], PP[:, 4 + hp, :]
            FM, TM = d["FM"], d["TM"]
            heads = [hp * 2, hp * 2 + 1]
            t0 = ti * TT
            heads = [hp * 2, hp * 2 + 1]
            fmh = [FM[0:64], d["FMo"]]
            fmr = [[(FM, q) for q in range(5)], [d["FMo"]]]
            reg = [0, 1024]
            regres = [[(SC, 0), (SC, 1)], [(SC, 2), (SC, 3)]]
            for half in range(2):
                for e_ in range(2):
                    h = heads[e_]
                    f = fmh[e_]
                    items = []
                    for cc in range(2):
                        c = half * 2 + cc
                        tsl = slice(c * CH, (c + 1) * CH)
                        base = reg[e_] + cc * 384
                        items.append((SC[:, base:base + 128], f[:, 2, tsl], f[:, 0:2, tsl], True, True))
                        items.append((SC[:, base + 128:base + 256], f[:, 1, tsl], f[:, 2:4, tsl], True, True))
                        items.append((SC[:, base + 256:base + 384], f[:, 0, tsl], f[:, 2:4, tsl], True, True))
                    k.mm(items, fmr[e_], regres[e_])
                for e_ in range(2):
                    h = heads[e_]
                    S1 = hd[h]["S1"]
                    cs_ = slice(half * 2, half * 2 + 2)
                    src = SC[:, reg[e_]:reg[e_] + 768].rearrange("p (c x) -> p c x", x=384)
                    m6 = mask6[:].unsqueeze(1).broadcast_to([64, 2, 384])
                    k.tt("dve", S1[:, cs_, 64:320], src[:, :, 0:256], m6[:, :, 0:256], ALU.mult,
                         regres[e_] + [mask6], [(S1, 0)])
                    k.tt("dve", S1[:, cs_, 384:512], src[:, :, 256:384], m6[:, :, 256:384], ALU.mult,
                         regres[e_] + [mask6], [(S1, 0)])
            for lvl in range(6):
                for e_ in range(2):
                    h = heads[e_]
                    S1 = hd[h]["S1"]
                    cur = hd[h]["XT"][lvl % 2]
                    i64 = ident[0:64, 0:64]
                    items = []
                    for c in range(NCH):
                        base = reg[e_] + c * 192
                        if lvl == 0:
                            NTm, Nm = S1[:, c, 128:192], S1[:, c, 192:256]
                            items.append((SC[:, base:base + 64], NTm, Nm, True, True))
                            items.append((SC[:, base + 128:base + 192], Nm, NTm, True, True))
                        elif lvl < 5:
                            items.append((SC[:, base:base + 128], cur[:, c, 128:192], cur[:, c, 0:128], True, False))
                            items.append((SC[:, base + 64:base + 128], i64, cur[:, c, 64:128], False, True))
                            items.append((SC[:, base + 128:base + 192], cur[:, c, 0:64], cur[:, c, 128:192], True, True))
                        else:
                            items.append((SC[:, base + 64:base + 128], cur[:, c, 128:192], cur[:, c, 64:128], True, False))
                            items.append((SC[:, base + 64:base + 128], i64, cur[:, c, 64:128], False, True))
                    k.mm(items, [(S1, 0)] if lvl == 0 else [cur, ident], regres[e_])
                for e_ in range(2):
                    h = heads[e_]
                    S1 = hd[h]["S1"]
                    nxt = hd[h]["XT"][(lvl + 1) % 2]
                    src = SC[:, reg[e_]:reg[e_] + 768].rearrange("p (c x) -> p c x", x=192)
                    eng = "act" if e_ == 0 else "dve"
                    if lvl == 0:
                        k.cp(eng, nxt[:, :, 0:64], src[:, :, 0:64], regres[e_], [nxt])
                        k.cp(eng, nxt[:, :, 128:192], src[:, :, 128:192], regres[e_], [nxt])
                        k.tt("pool", nxt[:, :, 64:128], S1[:, :, 192:256],
                             ident[0:64, 0:64].unsqueeze(1).broadcast_to([64, NCH, 64]), ALU.add,
                             [(S1, 0), ident], [nxt])
                    elif lvl < 5:
                        k.cp(eng, nxt[:, :, :], src[:, :, :], regres[e_], [nxt])
                    else:
                        k.cp(eng, nxt[:, :, 64:128], src[:, :, 64:128], regres[e_], [nxt])
            for e_ in range(2):
                h = heads[e_]
                S1 = hd[h]["S1"]
                Tm = hd[h]["XT"][0]
                items = [(SC[:, reg[e_] + c * 128:reg[e_] + (c + 1) * 128], Tm[:, c, 64:128], S1[:, c, 0:128], True, True)
                         for c in range(NCH)]
                k.mm(items, [Tm, S1], regres[e_])
            for e_ in range(2):
                h = heads[e_]
                k.cp("act" if e_ == 0 else "dve", hd[h]["Q3"][:],
                     SC[:, reg[e_]:reg[e_] + 512].rearrange("p (c x) -> p c x", x=128), regres[e_], [hd[h]["Q3"]])
            for e_ in range(2):
                h = heads[e_]
                S1, Q3, f = hd[h]["S1"], hd[h]["Q3"], fmh[e_]
                items = []
                for c in range(NCH):
                    o = SC[:, reg[e_] + c * 128:reg[e_] + (c + 1) * 128]
                    tsl = slice(c * CH, (c + 1) * CH)
                    items.append((o, Q3[:, c, 0:64], S1[:, c, 256:384], True, False))
                    items.append((o, ident[0:64, 0:64], f[:, 3:5, tsl], False, True))
                k.mm(items, [Q3, S1, ident] + fmr[e_], regres[e_])
            for e_ in range(2):
                h = heads[e_]
                k.cp("act" if e_ == 0 else "dve", hd[h]["RM"][:],
                     SC[:, reg[e_]:reg[e_] + 512].rearrange("p (c x) -> p c x", x=128), regres[e_], [hd[h]["RM"]])
            for e_ in range(2):
                h = heads[e_]
                S1, Q3 = hd[h]["S1"], hd[h]["Q3"]
                items = []
                for c in range(NCH):
                    o = SC[:, reg[e_] + c * 128:reg[e_] + (c + 1) * 128]
                    items.append((o, Q3[:, c, 64:128], S1[:, c, 256:384], True, False))
                    items.append((o, ident[0:64, 0:64], S1[:, c, 448:576], False, True))
                k.mm(items, [Q3, S1, ident], regres[e_])
            for e_ in range(2):
                h = heads[e_]
                k.cp("act" if e_ == 0 else "dve", hd[h]["AG"][:],
                     SC[:, reg[e_]:reg[e_] + 512].rearrange("p (c x) -> p c x", x=128), regres[e_], [hd[h]["AG"]])
            for c in range(NCH):
                for e_ in range(2):
                    h = heads[e_]
                    RM, AG, VT, ST = hd[h]["RM"], hd[h]["AG"], hd[h]["VT"], hd[h]["ST"]
                    yo = SC[:, reg[e_] + 768 + c * CH:reg[e_] + 768 + (c + 1) * CH]
                    so = SC[:, reg[e_] + 512:reg[e_] + 576]
                    yres = (SC, 1) if e_ == 0 else (SC, 3)
                    k.mm([(yo, ST[:], RM[:, c, 0:64], True, False), (yo, VT[:, c, :], AG[:, c, 0:64], False, True)],
                         [ST, RM, VT, AG], [yres])
                    k.mm([(so, RM[:, c, 64:128], ST[:], True, False), (so, AG[:, c, 64:128], VT[:, c, :], False, True)],
                         [ST, RM, VT, AG], [yres])
                    k.cp("act" if e_ == 0 else "dve", ST[:], so, [yres], [ST])
            for e_ in range(2):
                h = heads[e_]
                yres = (SC, 1) if e_ == 0 else (SC, 3)
                k.cp("act" if e_ == 0 else "dve", hd[h]["YS"][:], SC[:, reg[e_] + 768:reg[e_] + 1024], [yres], [hd[h]["YS"]])
            if DBG and ti == 0 and hp == 0:
                h0 = hd[0]
                dump(S, h0["S1"], h0["S1"][:, 0, :], "b", "S1c0", rows=64, cols=576)
                dump(S, h0["XT"][0], h0["XT"][0][:, 0, :], "b", "XTc0", rows=64, cols=192)
                dump(S, h0["Q3"], h0["Q3"][:, 0, :], "b", "Q3c0", rows=64, cols=128)
                dump(S, h0["RM"], h0["RM"][:, 0, :], "b", "RMc0", rows=64, cols=128)
                dump(S, h0["AG"], h0["AG"][:, 0, :], "b", "AGc0", rows=64, cols=128)
                dump(S, h0["VT"], h0["VT"][:, 0, :], "b", "VTc0", rows=64, cols=64)
                dump(S, h0["YS"], h0["YS"][:], "b", "YS0", rows=64)
                dump(S, hd[1]["YS"], hd[1]["YS"][:], "b", "YS1", rows=64)
                dump(S, d["FMo"], d["FMo"][:, 0, :], "b", "FMo0", rows=64)
        def emitC(ti, hp, u):
            pmc[0] = 1
            d = tmp[u % 4]
            hs = slice(hp * 128, (hp + 1) * 128)
            pc = lambda c: pv[:, c + hp:c + hp + 1]
            PR, PK, PVv = PP[:, 0 + hp, :], PP[:, 2 + hp, :], PP[:, 4 + hp, :]
            FM, TM = d["FM"], d["TM"]
            heads = [hp * 2, hp * 2 + 1]
            t0 = ti * TT
            sy = pm()
            k.mm([(pap(sy), ilu[:, 0:128], hd[heads[0]]["YS"][:], True, False),
                  (pap(sy), ilu[:, 128:256], hd[heads[1]]["YS"][:], False, True)],
                 [ilu, hd[heads[0]]["YS"], hd[heads[1]]["YS"]], [sy])
            k.cp("act", d["YC"][:], pap(sy), [sy], [d["YC"]])
            k.cp("dve", d["YB"][:], d["YC"][:], [d["YC"]], [d["YB"]])
            k.tt("dve", d["YSQ"][:], d["YC"][:], d["YC"][:], ALU.mult, [d["YC"]], [d["YSQ"]])
            sm = pm()
            k.mm([(pap(sm), bd64[:], d["YB"][:], True, True)], [bd64, d["YB"]], [sm])
            k.cp("act", d["MS"][:], pap(sm), [sm], [d["MS"]])
            sq = pm()
            k.mm([(pap(sq), bd64[:], d["YSQ"][:], True, True)], [bd64, d["YSQ"]], [sq])
            k.tt("pool", d["NEG"][:], d["MS"][:], d["MS"][:], ALU.mult, [d["MS"]], [d["NEG"]])
            k.tt("dve", d["VAR"][:], pap(sq), d["NEG"][:], ALU.subtract, [sq, d["NEG"]], [d["VAR"]])
            k.rsqrt(d["RS"][:], d["VAR"][:], GN_EPS, [d["VAR"]], [d["RS"]])
            k.tt("dve", d["YC"][:], d["YC"][:], d["MS"][:], ALU.subtract, [d["YC"], d["MS"]], [d["YC"]])
            k.tt("pool", d["YN"][:], d["YC"][:], d["RS"][:], ALU.mult, [d["YC"], d["RS"]], [d["YN"]])
            k.ts("dve", d["YG"][:], d["YN"][:], pc(PV_GW), pc(PV_GB), ALU.mult, ALU.add, [d["YN"], pv], [d["YG"]])
            k.tt("pool", d["YG"][:], d["YG"][:], d["BON"][:], ALU.add, [d["YG"], d["BON"]], [d["YG"]])
            k.tt("pool", d["YO"][:], d["YG"][:], d["GT"][:], ALU.mult, [d["YG"], d["GT"]], [d["YO"]])
            if not fused:
                S.dma("sp", yr[hp * 128:(hp + 1) * 128, t0:t0 + TT], d["YO"][:], R=[d["YO"]], slot=f"yo{hp}")
            else:
                k.cp("pool", YOb[hp][:], d["YO"][:], [d["YO"]], [YOb[hp]])
            if hp == 1:
                emitP(ti)
        def emitP(ti):
            t0 = ti * TT
            if fused:
                ntok = ctx["ntok"]
                sh, col = t0 // ntok, t0 % ntok
                for oc in range(8):
                    so = pm()
                    k.mm([(pap(so), PRWb[:, hp_, oc * 128:(oc + 1) * 128], YOb[hp_][:], hp_ == 0, hp_ == 1) for hp_ in range(2)],
                         [PRWb, YOb[0], YOb[1]], [so])
                    k.cp("act", PRD[:, oc, :], pap(so), [so], [PRD])
                S.dma("sp", ctx["src_r"][sh * 1024:sh * 1024 + 1024, col:col + TT].rearrange("(oc p) t -> p oc t", p=128),
                      PRD[:], R=[PRD], slot="prd")
        units = [(ti, hp) for ti in range(nt) for hp in range(2)]
        NU = len(units)
        S.replay([S.record(emitTA, units[0][0], units[0][1], 0)])
        for n in range(NU + 1):
            streams = []
            if n < NU:
                streams.append(S.record(emitB, units[n][0], units[n][1], n))
            if n + 1 < NU:
                streams.append(S.record(emitTA, units[n + 1][0], units[n + 1][1], n + 1))
            if n >= 1:
                streams.append(S.record(emitC, units[n - 1][0], units[n - 1][1], n - 1))
            S.replay(streams)
        S.barrier()
    return nc


def p1a_inputs(x_b, j, norm_mix_g, w_in, shift_mu, w0, w2, a0, a2, g2, k_k, k_a, r_k, gn_w, gn_b):
    cs = slice(256 * j, 256 * j + 256)
    cols = np.concatenate([np.arange(256) + 256 * j, 1024 + np.arange(256) + 256 * j, 2048 + np.arange(256) + 256 * j,
                           np.arange(3072, 3360)])
    w1 = np.ascontiguousarray(w_in[:, cols])
    mu = shift_mu[cols]
    pvec = np.zeros((128, NPV), np.float32)
    for ci, (co, M) in enumerate(RW_CH):
        pvec[:M, PV_MU + ci] = mu[co:co + M]
    def two(v):
        return np.ascontiguousarray(v[cs].reshape(2, 128).T)
    pvec[:, PV_W0:PV_W0 + 2] = two(w0)
    pvec[:, PV_A0:PV_A0 + 2] = two(a0)
    pvec[:, PV_KK:PV_KK + 2] = two(k_k)
    pvec[:, PV_KA:PV_KA + 2] = two(k_a)
    pvec[:, PV_RK:PV_RK + 2] = two(r_k.reshape(-1))
    pvec[:, PV_GW:PV_GW + 2] = two(gn_w)
    pvec[:, PV_GB:PV_GB + 2] = two(gn_b)
    m = {"xT": np.ascontiguousarray(x_b.T), "w1": w1,
         "gmix": np.ascontiguousarray(norm_mix_g.reshape(8, 128).T), "pvec": pvec,
         "w2s": np.ascontiguousarray(w2[:, cs]), "a2s": np.ascontiguousarray(a2[:, cs]),
         "g2s": np.ascontiguousarray(g2[:, cs])}
    for kk, v in consts_p1().items():
        m["c_" + kk] = v
    return m


DILS = (1, 4, 16)


def attn_bias(j):
    out = np.zeros((128, 3, 2, 128), np.float32)
    kk = np.arange(128)[:, None].astype(np.float32)
    q = np.arange(128)[None, :].astype(np.float32)
    for g, d in enumerate(DILS):
        h = 4 * g + j
        slope = np.float32(2.0) ** np.float32(-8.0 * (h + 1.0) / 12.0)
        sp = q + 128 - kk
        out[:, g, 0, :] = np.where(sp <= 128, -slope * (sp * d), -30000.0)
        sc = q - kk
        out[:, g, 1, :] = np.where(sc >= 0, -slope * (sc * d), -30000.0)
    return out


def build_p1b(seq, ctx=None):
    nt = seq // TT
    fused = ctx is not None
    nc = ctx["nc"] if fused else bass.Bass("TRN2", target_bir_lowering=False)
    dr = _mk_dr(nc, ctx, "b_" if fused else "")
    xT = dr("xT", (D, seq))
    w1a = dr("w1a", (D, 576))
    gmix = dr("gmix", (128, 8))
    identd = dr("c_ident", (128, 128))
    biasd = dr("abias", (128, 3 * 2 * 128))
    seld = dr("sel", (65, 64))
    ya = None if fused else dr("ya", (64, seq), kind="ExternalOutput")
    pat = dr("pat", (64, D)) if fused else None
    with ExitStack() as es:
        if fused:
            k = ctx["k"]
            k.es = es
            k.epsb = {}
        else:
            k = K(nc, es)
        S = k.S
        if fused:
            PATb = k.sb((64, D), BF16)
            YAb = k.sb((64, 512), BF16)
            PRA = k.sb((128, 4, 512), BF16)
        idf = k.sb((128, 128), F32)
        S.dma("sp", idf[:], identd, W=[idf], slot="id")
        ident = k.sb((128, 128), BF16)
        k.cp("pool", ident[:], idf[:], [idf], [ident])
        bias = k.sb((128, 3, 2, 128), F32)
        S.dma("sp", bias[:].rearrange("p a b c -> p (a b c)"), biasd, W=[bias], slot="bias")
        sel = k.sb((65, 64), F32)
        S.dma("sp", sel[:], seld, W=[sel], slot="sel")
        gm = k.sb((128, 8), F32)
        S.dma("sp", gm[:], gmix, W=[gm], slot="gm")
        onesm = k.sb((128, 128), BF16)
        k.memset("pool", onesm[:], 1.0 / 1024, [onesm])
        WB = k.sb((128, 8, 576), BF16)
        wst = k.sb((128, 576), F32)
        for kc in range(8):
            S.dma("sp", wst[:], w1a[kc * 128:(kc + 1) * 128, :], W=[wst], slot="w")
            k.ts("dve", WB[:, kc, :], wst[:], gm[:, kc:kc + 1], None, ALU.mult, None, [wst, gm], [WB])
        if fused:
            for h_ in range(2):
                S.dma("sp", wst[0:64, 0:512], pat[:, h_ * 512:(h_ + 1) * 512], W=[wst], slot="w")
                k.cp("pool", PATb[:, h_ * 512:(h_ + 1) * 512], wst[0:64, 0:512], [wst], [PATb])
        XT32 = [k.sb((128, 8, TT), F32) for _ in range(2)]
        XBs = [k.sb((128, 8, TT), BF16) for _ in range(2)]
        XSQ1 = k.sb((128, 8, TT), BF16)
        XSQs = [XSQ1, XSQ1]
        RSTD = k.sb((128, TT), F32)
        QKVs = [k.sb((64, 3, seq), BF16) for _ in range(2)]
        OF = k.sb((65, seq), F32)
        VA = k.sb((128, seq // 128, 65), BF16)
        k.memset("pool", VA[:], 1.0, [VA])
        TMPs = [k.sb((128, 2, 128), F32) for _ in range(2)]
        PTbs = [k.sb((128, 2, 128), BF16) for _ in range(2)]
        RD = k.sb((64, 512), F32)
        YA = k.sb((64, 512), F32)
        PJ = k.ps((128, 512), F32)
        PM = k.ps((128, 512), F32)
        PSSs = [k.ps((128, 512), F32) for _ in range(2)]
        POs = [k.ps((128, 512), F32) for _ in range(2)]
        PSS = PSSs[0]
        PTr = k.ps((128, 4, 128), BF16)

        def emit_proj(g):
            QKV = QKVs[g % 2]

            def x_dma(tj):
                pj_ = tj % 2
                S.dma("sp", XT32[pj_][:], xT.rearrange("(kc p) t -> p kc t", p=128)[:, :, tj * TT:(tj + 1) * TT], W=[XT32[pj_]], slot=f"x{pj_}")

            def x_prep(tj):
                pj_ = tj % 2
                k.cp("dve", XBs[pj_][:], XT32[pj_][:], [XT32[pj_]], [XBs[pj_]])
                k.act(XSQs[pj_][:], XT32[pj_][:], AF.Square, [XT32[pj_]], [XSQs[pj_]])
            x_dma(0)
            x_prep(0)
            for ti in range(nt):
                t0 = ti * TT
                XB_, XSQ_ = XBs[ti % 2], XSQs[ti % 2]
                if ti + 1 < nt:
                    x_dma(ti + 1)
                k.mm([(PM[:, 0:TT], onesm[:], XSQ_[:, kc, :], kc == 0, kc == 7) for kc in range(8)], [onesm, XSQ_], [PM])
                k.rsqrt(RSTD[:], PM[:, 0:TT], NORM_EPS, [PM], [RSTD])
                for qi in range(3):
                    co = qi * 192 + g * 64
                    k.mm([(PJ[0:64, 0:TT], WB[:, kc, co:co + 64], XB_[:, kc, :], kc == 0, kc == 7) for kc in range(8)],
                         [WB, XB_], [PJ])
                    k.tt("dve", QKV[:, qi, t0:t0 + TT], PJ[0:64, 0:TT], RSTD[0:64, :], ALU.mult, [PJ, RSTD], [QKV])
                if ti + 1 < nt:
                    x_prep(ti + 1)

        def emit_blocks(g):
            d = DILS[g]
            QKV = QKVs[g % 2]
            span = 128 * d

            def blk_ap(qi, n, r):
                v = QKV[:, qi, n * span:(n + 1) * span]
                if d == 1:
                    return v
                return v.rearrange("p (q r) -> p r q", r=d)[:, r, :]

            def of_ap(n, r, rows):
                v = OF[0:rows, n * span:(n + 1) * span]
                if d == 1:
                    return v
                return v.rearrange("p (q r) -> p r q", r=d)[:, r, :]
            nb = seq // span
            for n in range(nb):
                for r0 in range(0, d, 4):
                    rr = list(range(r0, min(d, r0 + 4)))
                    fns = [lambda e, i=i, r=r, n=n: e.transpose(out=PTr[:, i, 0:64], in_=blk_ap(2, n, r), identity=ident[0:64, 0:64])
                           for i, r in enumerate(rr)]
                    S.op("pe", fns, _res([QKV, ident]), _res([PTr]))
                    b0 = n * d + r0
                    k.cp("dve", VA[:, b0:b0 + len(rr), 0:64], PTr[:, 0:len(rr), 0:64], [PTr], [VA])
            for n in range(nb):
                for r in range(d):
                    b = n * d + r
                    PSSb, PO, TMP, PTb = PSSs[b % 2], POs[b % 2], TMPs[b % 2], PTbs[b % 2]
                    items = []
                    if n > 0:
                        items.append((PSSb[:, 0:128], blk_ap(1, n - 1, r), blk_ap(0, n, r), True, True))
                    items.append((PSSb[:, 128:256], blk_ap(1, n, r), blk_ap(0, n, r), True, True))
                    k.mm(items, [QKV], [PSSb])
                    lo = 0 if n > 0 else 1
                    k.stt(TMP[:, lo:2, :], PSSb[:, lo * 128:256].rearrange("p (a b) -> p a b", b=128), 0.125,
                          bias[:, g, lo:2, :], ALU.mult, ALU.add, [PSSb, bias], [TMP])
                    k.act(PTb[:, lo:2, :], TMP[:, lo:2, :], AF.Exp, [TMP], [PTb])
                    items = []
                    if n > 0:
                        items.append((PO[0:65, 0:128], VA[:, b - d, :], PTb[:, 0, :], True, False))
                    items.append((PO[0:65, 0:128], VA[:, b, :], PTb[:, 1, :], n == 0, True))
                    k.mm(items, [VA, PTb], [PO])
                    if g == 0:
                        k.cp("dve", of_ap(n, r, 65), PO[0:65, 0:128], [PO], [OF])
                    else:
                        k.tt("dve", of_ap(n, r, 65), PO[0:65, 0:128], of_ap(n, r, 65), ALU.add, [PO, OF], [OF])

        S.replay([S.record(emit_proj, 0)])
        for g in range(3):
            streams = [S.record(emit_blocks, g)]
            if g + 1 < 3:
                streams.append(S.record(emit_proj, g + 1))
            S.replay(streams)
        RD2 = [RD, k.sb((64, 512), F32)]
        YA2 = [YA, k.sb((64, 512), F32)]
        if fused:
            YAb2 = [YAb, k.sb((64, 512), BF16)]
            PRA2 = [PRA, k.sb((128, 4, 512), BF16)]

        def emit_norm(par):
            RDp, YAp = RD2[par], YA2[par]
            pden = PM if par == 0 else PJ
            pps = [PSSs[par], POs[par]]
            for c0 in range(par * 512, seq, 1024):
                k.mm([(pden[0:64, 0:512], sel[:], OF[:, c0:c0 + 512], True, True)], [sel, OF], [pden])
                S.op("dve", lambda e: e.reciprocal(out=RDp[:], in_=pden[0:64, 0:512]), _res([pden]), _res([RDp]))
                k.tt("dve", YAp[:], OF[0:64, c0:c0 + 512], RDp[:], ALU.mult, [OF, RDp], [YAp])
                if not fused:
                    S.dma("sp", ya[:, c0:c0 + 512], YAp[:], R=[YAp], slot=f"ya{par}")
                else:
                    ntok = ctx["ntok"]
                    sh, col = c0 // ntok, c0 % ntok
                    k.cp("dve", YAb2[par][:], YAp[:], [YAp], [YAb2[par]])
                    for och in range(2):
                        for o4 in range(4):
                            oc = och * 4 + o4
                            pp_ = pps[oc % 2]
                            k.mm([(pp_[:, 0:512], PATb[:, oc * 128:(oc + 1) * 128], YAb2[par][:], True, True)], [PATb, YAb2[par]], [pp_])
                            k.cp("act" if par == 0 else "dve", PRA2[par][:, o4, :], pp_[:, 0:512], [pp_], [PRA2[par]])
                        r0 = sh * 1024 + och * 512
                        S.dma("sp", ctx["src_a"][r0:r0 + 512, col:col + 512].rearrange("(oc p) t -> p oc t", p=128),
                              PRA2[par][:], R=[PRA2[par]], slot=f"pra{par}")
        S.replay([S.record(emit_norm, 0), S.record(emit_norm, 1)])
        S.barrier()
    return nc


def p1b_inputs(x_b, j, norm_mix_g, w_in):
    cols = []
    for base in (3360, 3360 + 768, 3360 + 1536):
        for g in range(3):
            h = 4 * g + j
            cols.append(base + 64 * h + np.arange(64))
    cols = np.concatenate(cols)
    sel = np.zeros((65, 64), np.float32)
    sel[64, :] = 1
    return {"xT": np.ascontiguousarray(x_b.T), "w1a": np.ascontiguousarray(w_in[:, cols]),
            "gmix": np.ascontiguousarray(norm_mix_g.reshape(8, 128).T), "c_ident": np.eye(128, dtype=np.float32),
            "abias": attn_bias(j).reshape(128, -1), "sel": sel}


def build_p2(ntok, ctx):
    NPASS = min(ntok, 1024)
    NB = NPASS // 512
    nc = ctx["nc"]
    dr = lambda n, s, kind="ExternalInput": nc.dram_tensor("c_" + n, list(s), F32, kind=kind).ap()
    xT = dr("xT", (D, ntok))
    memT = dr("memT", (D, 256))
    wg = dr("wg", (D, 2048))
    w_out = dr("w_out", (D, D))
    wq = dr("wq", (D, D))
    wkv = dr("wkv", (D, 2048))
    wo = dr("wo", (D, D))
    w1 = dr("w1", (D, 4096))
    w2 = dr("w2", (4096, D))
    gains = dr("gains", (128, 40))
    outT = dr("outT", (D, ntok), kind="ExternalOutput")
    dst_r, dst_a, dstres_r, dstres_a = ctx["dst_r"], ctx["dst_a"], ctx["dstres_r"], ctx["dstres_a"]
    with ExitStack() as es:
        k = ctx["k"]
        k.es = es
        k.epsb = {}
        S = k.S
        gn = k.sb((128, 40), F32)
        S.dma("sp", gn[:], gains, W=[gn], slot="gn")
        onesm = k.sb((128, 128), BF16)
        k.memset("pool", onesm[:], 1.0 / 1024, [onesm])
        ones1 = k.sb((128, 128), BF16)
        k.memset("pool", ones1[:], 1.0, [ones1])
        NST = 4
        stF = [k.sb((128, 8, 256), F32) for _ in range(NST)]
        stB = [k.sb((128, 8, 256), BF16) for _ in range(NST)]
        PA = [k.ps((128, 512), F32) for _ in range(4)]
        PR = k.ps((128, 512), F32)
        PSc = [k.ps((128, 512), F32) for _ in range(2)]
        PD = k.ps((128, 512), F32)
        lc = [0]
        pac = [0]

        staged = {}

        def stage(W, kp, cb, key=None):
            key = key if key is not None else (id(W), kp, cb)
            if key in staged:
                return staged.pop(key)
            i = lc[0] % NST
            lc[0] += 1
            S.dma("sp" if i % 2 == 0 else "pool", stF[i][:],
                  W[kp * 1024:(kp + 1) * 1024, cb * 256:(cb + 1) * 256].rearrange("(kc p) o -> p kc o", p=128),
                  W=[stF[i]], slot=f"st{i}")
            k.cp("act", stB[i][:], stF[i][:], [stF[i]], [stB[i]])
            return stB[i]

        def prefetch(W, kp, cb):
            key = (id(W), kp, cb)
            if key not in staged:
                staged[key] = stage(W, kp, cb, key=("pf",) + key)

        pool8 = [False]

        def nextpa():
            lst = PA if not pool8[0] else PA + PSc + [PD, PR]
            p = lst[pac[0] % len(lst)]
            pac[0] += 1
            return p

        def linear(W, ocs, Xb, blocks, epi, xres=None, nxt=None):
            ocs = list(ocs)
            for i0 in range(0, len(ocs), 2):
                wb = stage(W, 0, ocs[i0] // 2)
                if i0 + 2 < len(ocs):
                    prefetch(W, 0, ocs[i0 + 2] // 2)
                elif nxt is not None:
                    prefetch(*nxt)
                for o2 in range(2):
                    oc = ocs[i0 + o2]
                    for blk in blocks:
                        pa = nextpa()
                        bs = slice(blk * 512, (blk + 1) * 512)
                        k.mm([(pa[:, :], wb[:, kc, o2 * 128:(o2 + 1) * 128], Xb[:, kc, bs], kc == 0, kc == 7) for kc in range(8)],
                             [wb, ((xres if xres is not None else Xb), blk)], [pa])
                        epi(oc, blk, pa[:, :], pa)

        XSQ = k.sb((128, 8, 512), BF16)
        RSTD = k.sb((128, NPASS), F32)

        def normed(X, gcol, out_bf, nblk, rstd=None, bw=512, off=0):
            rstd = RSTD if rstd is None else rstd
            for blk in range(nblk):
                bs = slice(blk * bw, (blk + 1) * bw)
                xs = slice(off + blk * bw, off + (blk + 1) * bw)
                xr = (X, min(off // 512 + blk, len(X.rs) - 1))
                k.act(XSQ[:, :, 0:bw], X[:, :, xs], AF.Square, [xr], [XSQ])
                k.mm([(PR[:, 0:bw], onesm[:], XSQ[:, kc, 0:bw], kc == 0, kc == 7) for kc in range(8)], [onesm, XSQ], [PR])
                k.rsqrt(rstd[:, bs], PR[:, 0:bw], NORM_EPS, [PR], [rstd])
                if out_bf is not None:
                    for kc in range(8):
                        k.stt(out_bf[:, kc, xs], X[:, kc, xs], gn[:, gcol + kc:gcol + kc + 1], rstd[:, bs], ALU.mult, ALU.mult,
                              [xr, gn, rstd], [(out_bf, min(off // 512 + blk, len(out_bf.rs) - 1))])

        M32 = k.sb((128, 8, 256), F32)
        S.dma("act", M32[:], memT.rearrange("(kc p) t -> p kc t", p=128), W=[M32], slot="mem")
        MN = k.sb((128, 8, 256), BF16)
        RSM = k.sb((128, 256), F32)
        normed(M32, 16, MN, 1, rstd=RSM, bw=256)
        KT = k.sb((128, 8, 256), BF16)
        for cb in range(4):
            wb = stage(wkv, 0, cb)
            for o2 in range(2):
                oc = 2 * cb + o2
                pa = nextpa()
                k.mm([(pa[:, 0:256], wb[:, kc, o2 * 128:(o2 + 1) * 128], MN[:, kc, :], kc == 0, kc == 7) for kc in range(8)], [wb, MN], [pa])
                k.cp("act", KT[:, oc, :], pa[:, 0:256], [pa], [KT])
        VM = k.sb((128, 2, 1024), BF16)
        for cb in range(4):
            wb = stage(wkv, 0, 4 + cb)
            for o2 in range(2):
                oc = 2 * cb + o2
                pa = nextpa()
                for mc in range(2):
                    k.mm([(pa[:, mc * 128:(mc + 1) * 128], MN[:, kc, mc * 128:(mc + 1) * 128], wb[:, kc, o2 * 128:(o2 + 1) * 128], kc == 0, kc == 7)
                          for kc in range(8)], [wb, MN], [pa])
                k.cp("act", VM[:, :, oc * 128:(oc + 1) * 128], pa[:, 0:256].rearrange("p (a b) -> p a b", b=128), [pa], [VM])
        X = k.sb((128, 8, NPASS), F32, n=NB)
        XN = k.sb((128, 8, NPASS), BF16, n=NB)
        AB = k.sb((128, 32, NPASS), BF16, n=NB)
        ABv = AB[:, 0:8, :]
        MRt = [k.sb((128, 512), BF16) for _ in range(2)]
        MAt = [k.sb((128, 512), BF16) for _ in range(2)]
        TG = [k.sb((128, 512), F32) for _ in range(2)]
        TR = TG
        trc = [0]
        PTm = k.sb((128, 2, 512), BF16)
        RDEN = k.sb((128, 512), F32)
        OAT = XN
        for ps_ in range(ntok // NPASS):
          toff = ps_ * NPASS
          S.dma("sp", X[:], xT.rearrange("(kc p) t -> p kc t", p=128)[:, :, toff:toff + NPASS], W=[X], slot="x")
          blocks = list(range(NB))
          prefetch(wg, 0, 0)
          prefetch(wg, 0, 4)
          normed(X, 0, XN, NB)
          for cb in range(4):
            wbr_ = stage(wg, 0, cb)
            wba_ = stage(wg, 0, 4 + cb)
            if cb < 3:
                prefetch(wg, 0, cb + 1)
                prefetch(wg, 0, 4 + cb + 1)
            else:
                prefetch(w_out, 0, 0)
            for o2 in range(2):
              oc = 2 * cb + o2
              wbr = wbr_[:, :, o2 * 128:(o2 + 1) * 128]
              wba = wba_[:, :, o2 * 128:(o2 + 1) * 128]
              for blk in blocks:
                  bs = slice(blk * 512, (blk + 1) * 512)
                  gs = slice(toff + blk * 512, toff + (blk + 1) * 512)
                  mr, ma = MRt[blk % 2], MAt[blk % 2]
                  S.dma("sp", mr[:], dst_r[oc * 128:(oc + 1) * 128, gs], R=[dstres_r], W=[mr], slot=f"mr{blk % 2}")
                  S.dma("pool", ma[:], dst_a[oc * 128:(oc + 1) * 128, gs], R=[dstres_a], W=[ma], slot=f"ma{blk % 2}")
                  pa = nextpa()
                  k.mm([(pa[:, :], wbr[:, kc, :], XN[:, kc, bs], kc == 0, kc == 7) for kc in range(8)], [wbr_, (XN, blk)], [pa])
                  k.act(TG[0][:], pa[:, :], AF.Sigmoid, [pa], [TG[0]])
                  k.tt("dve", TG[0][:], TG[0][:], mr[:], ALU.mult, [mr, TG[0]], [TG[0]])
                  pa = nextpa()
                  k.mm([(pa[:, :], wba[:, kc, :], XN[:, kc, bs], kc == 0, kc == 7) for kc in range(8)], [wba_, (XN, blk)], [pa])
                  k.act(TG[1][:], pa[:, :], AF.Sigmoid, [pa], [TG[1]])
                  k.tt("dve", TG[1][:], TG[1][:], ma[:], ALU.mult, [ma, TG[1]], [TG[1]])
                  k.tt("dve", ABv[:, oc, bs], TG[0][:], TG[1][:], ALU.add, [TG[0], TG[1]], [(AB, blk)])
          def epi_res(oc, blk, ap, pt):
              bs = slice(blk * 512, (blk + 1) * 512)
              k.tt("dve", X[:, oc, bs], ap, X[:, oc, bs], ALU.add, [pt, (X, blk)], [(X, blk)])
          linear(w_out, range(8), ABv, blocks, epi_res, xres=AB, nxt=(wq, 0, 0))
          normed(X, 8, XN, NB)
          linear(wq, range(8), XN, blocks,
                 lambda oc, blk, ap, pt: k.cp("act", ABv[:, oc, blk * 512:(blk + 1) * 512], ap, [pt], [(AB, blk)]), nxt=(wo, 0, 0))
          for blk in blocks:
              bs = slice(blk * 512, (blk + 1) * 512)
              for h in range(4):
                  for mc in range(2):
                      ps = PSc[mc]
                      k.mm([(ps[:, :], KT[:, 2 * h + hf, mc * 128:(mc + 1) * 128], ABv[:, 2 * h + hf, bs], hf == 0, hf == 1)
                            for hf in range(2)], [KT, (AB, blk)], [ps])
                      k.act(PTm[:, mc, :], ps[:, :], AF.Exp, [ps], [PTm], scale=0.0625)
                  k.mm([(PD[:, :], ones1[:], PTm[:, mc, :], mc == 0, mc == 1) for mc in range(2)], [ones1, PTm], [PD])
                  S.op("dve", lambda e: e.reciprocal(out=RDEN[:], in_=PD[:, :]), _res([PD]), _res([RDEN]))
                  for ch in range(2):
                      oc = 2 * h + ch
                      k.mm([(PR[:, :], VM[:, mc, oc * 128:(oc + 1) * 128], PTm[:, mc, :], mc == 0, mc == 1) for mc in range(2)],
                           [VM, PTm], [PR])
                      k.tt("dve", OAT[:, oc, bs], PR[:, :], RDEN[:], ALU.mult, [PR, RDEN], [(OAT, blk)])
          linear(wo, range(8), OAT, blocks, epi_res, nxt=(w1, 0, 0))
          normed(X, 24, XN, NB)

          def epi_u(oc, blk, ap, pt):
              t = TR[trc[0] % 2]
              trc[0] += 1
              k.act(t[:], ap, AF.Relu, [pt], [t])
              k.tt("dve", AB[:, oc, blk * 512:(blk + 1) * 512], t[:], t[:], ALU.mult, [t], [(AB, blk)])
          linear(w1, range(32), XN, blocks, epi_u, nxt=(w2, 0, 0))
          pool8[0] = True
          for cb in range(4):
              pas = {(o2, blk): nextpa() for o2 in range(2) for blk in blocks}
              for kp in range(4):
                  wb = stage(w2, kp, cb)
                  if kp < 3:
                      prefetch(w2, kp + 1, cb)
                  elif cb < 3:
                      prefetch(w2, 0, cb + 1)
                  for o2 in range(2):
                      for blk in blocks:
                          k.mm([(pas[(o2, blk)][:, :], wb[:, kc, o2 * 128:(o2 + 1) * 128], AB[:, kp * 8 + kc, blk * 512:(blk + 1) * 512],
                                 kp == 0 and kc == 0, kp == 3 and kc == 7) for kc in range(8)], [wb, (AB, blk)], [pas[(o2, blk)]])
              for o2 in range(2):
                  oc = 2 * cb + o2
                  for blk in blocks:
                      bs = slice(blk * 512, (blk + 1) * 512)
                      k.tt("dve", X[:, oc, bs], pas[(o2, blk)][:, :], X[:, oc, bs], ALU.add, [pas[(o2, blk)], (X, blk)], [(X, blk)])
          pool8[0] = False
          for blk in blocks:
              bs = slice(blk * 512, (blk + 1) * 512)
              normed(X, 32, None, 1, off=blk * 512)
              for kc in range(8):
                  k.stt(X[:, kc, bs], X[:, kc, bs], gn[:, 32 + kc:33 + kc], RSTD[:, 0:512], ALU.mult, ALU.mult, [(X, blk), gn, RSTD], [(X, blk)])
          S.dma("sp", outT.rearrange("(kc p) t -> p kc t", p=128)[:, :, toff:toff + NPASS], X[:], R=[X], slot="out")

        S.barrier()
    return nc


def g8(v):
    return np.ascontiguousarray(np.asarray(v, np.float32).reshape(8, 128).T)


def build_fused(seq):
    ntok = seq // 4
    nc = bass.Bass("TRN2", target_bir_lowering=False)
    RG = [[0, 1, 2, 3], [4, 5, 6, 7]]
    with ExitStack() as es:
        k = K(nc, es)
        S = k.S
        mk = lambda n, r: nc.dram_tensor(n, [r, ntok], BF16).ap()
        src_r, dst_r, src_a, dst_a = mk("rs_src_r", 4096), mk("rs_dst_r", 1024), mk("rs_src_a", 4096), mk("rs_dst_a", 1024)
        ctx = {"nc": nc, "k": k, "src_r": src_r, "dst_r": dst_r, "src_a": src_a, "dst_a": dst_a, "ntok": ntok, "shared": {}}

        def reduce_scatter(name, src, dst):
            cc = es.enter_context(nc.semaphore(name))
            nc.gpsimd.collective_compute("ReduceScatter", ALU.add, replica_groups=RG, ins=[src.opt()], outs=[dst.opt()]).then_inc(cc, 1)
            S.sem[name] = cc
            S.cnt[name] = 1
            r = Res()
            r.w = (name, 1)
            return r
        S.pre = "a_"
        build_p1a(seq, ctx)
        ctx["dstres_r"] = reduce_scatter("cc_r", src_r, dst_r)
        S.pre = "b_"
        build_p1b(seq, ctx)
        ctx["dstres_a"] = reduce_scatter("cc_a", src_a, dst_a)
        S.pre = "c_"
        build_p2(ntok, ctx)
    return nc


def kernel(x, mem, norm_mix_g, w_in, shift_mu, w0, w2, a0, a2, g2, k_k, k_a, r_k, gn_w, gn_b,
           p_rwkv, p_attn, w_out, norm_x_g, norm_mem_g, xa_wq, xa_wkv, xa_wo,
           norm_ffn_g, ffn_w1, ffn_w2, norm_final_g):
    A = lambda v: np.asarray(v, np.float32)
    x, mem = A(x), A(mem)
    L0 = lambda v: A(v)[0]
    B, Sq, _ = x.shape
    cores = list(range(8))
    ntok = Sq // 4
    nc = build_fused(Sq)
    gains = np.concatenate([g8(L0(norm_mix_g)), g8(L0(norm_x_g)), g8(L0(norm_mem_g)), g8(L0(norm_ffn_g)), g8(A(norm_final_g))], axis=1)
    wgc = np.ascontiguousarray(L0(w_in)[:, 3360 + 2304:])
    maps = []
    for c in cores:
        b, j = c // 4, c % 4
        m = {}
        ma = p1a_inputs(x[b], j, L0(norm_mix_g), L0(w_in), L0(shift_mu), L0(w0), L0(w2), L0(a0), L0(a2), L0(g2),
                        L0(k_k), L0(k_a), L0(r_k), L0(gn_w), L0(gn_b))
        mb = p1b_inputs(x[b], j, L0(norm_mix_g), L0(w_in))
        shared = ("xT", "gmix", "c_ident")
        for kk_, v in ma.items():
            m[kk_ if kk_ in shared else "a_" + kk_] = v
        for kk_, v in mb.items():
            if kk_ not in shared:
                m["b_" + kk_] = v
        m["a_prw"] = np.ascontiguousarray(L0(p_rwkv)[256 * j:256 * j + 256])
        m["b_pat"] = np.ascontiguousarray(L0(p_attn)[64 * j:64 * j + 64])
        ts_ = slice(j * ntok, (j + 1) * ntok)
        mc = {"xT": np.ascontiguousarray(x[b, ts_].T), "memT": np.ascontiguousarray(mem[b].T),
              "wg": wgc, "w_out": L0(w_out), "wq": L0(xa_wq),
              "wkv": L0(xa_wkv), "wo": L0(xa_wo), "w1": L0(ffn_w1), "w2": L0(ffn_w2), "gains": gains}
        for kk_, v in mc.items():
            m["c_" + kk_] = v
        maps.append(m)
    r3 = run_bass_kernel_spmd(nc, maps, core_ids=cores).results
    out = np.zeros((B, Sq, D), np.float32)
    for c in cores:
        b, j = c // 4, c % 4
        out[b, j * ntok:(j + 1) * ntok] = A(r3[c]["c_outT"]).T
    return out
```
